# Optimizing a Trainium2 kernel written in Bass

```python
import math
import jax, jax.numpy as jnp
from jax import lax
import numpy as np

D_MODEL = 1024
BATCH = 4
SEQ = 4096
DEPTH = 1

MEM_LEN = 256
EPS = 1e-6
CONV_WIDTH = CONV_HEADS = None
CONV_K = 3
CONV_GROUPS = 8
CONV_DIM = D_MODEL
GM_HEADS = 8
GM_HEAD_DIM = D_MODEL // GM_HEADS
GM_DIM = GM_HEADS * GM_HEAD_DIM
CHUNK = 128
MIX_DIM = CONV_DIM + GM_DIM
IN_DIM = 4 * CONV_DIM + 3 * GM_DIM
X_HEADS = 4
X_HEAD_DIM = D_MODEL // X_HEADS

kernel_name = "hybrid_shortconv_gmlp_memxattn_block"


def rms_norm(x, g):
    xf = x.astype(jnp.float32)
    y = xf * lax.rsqrt(jnp.mean(xf * xf, axis=-1, keepdims=True) + EPS)
    return (y * g.astype(jnp.float32)).astype(x.dtype)


def causal_depthwise_conv(h, w):
    c = h.shape[-1]
    return lax.conv_general_dilated(
        h, w[:, None, :].astype(h.dtype), window_strides=(1,),
        padding=[(CONV_K - 1, 0)], dimension_numbers=("NWC", "WIO", "NWC"),
        feature_group_count=c)


def chunked_spatial_gating(u, v, ln_g, ln_b, ws, bs):
    b, s, _ = v.shape
    n = s // CHUNK
    vh = v.reshape(b, n, CHUNK, GM_HEADS, GM_HEAD_DIM).astype(jnp.float32)
    mu = jnp.mean(vh, axis=-1, keepdims=True)
    var = jnp.mean(jnp.square(vh - mu), axis=-1, keepdims=True)
    vn = (vh - mu) * lax.rsqrt(var + EPS)
    vn = (vn * ln_g.reshape(GM_HEADS, GM_HEAD_DIM).astype(jnp.float32)
          + ln_b.reshape(GM_HEADS, GM_HEAD_DIM).astype(jnp.float32)).astype(v.dtype)
    mask = jnp.tril(jnp.ones((CHUNK, CHUNK), dtype=bool))
    w_c = jnp.where(mask[None], ws, jnp.zeros_like(ws))
    sp = jnp.einsum("hts,bnshc->bnthc", w_c, vn) + bs.T[:, :, None]
    return u * sp.reshape(b, s, GM_DIM)


def mixer_sublayer(h, w_in, conv_w, gm_ln_g, gm_ln_b, gm_ws, gm_bs, w_out):
    proj = h @ w_in
    gb, gc, xa, za, u, v, zb = jnp.split(
        proj, np.cumsum([CONV_DIM] * 4 + [GM_DIM] * 2).tolist(), axis=-1)
    a = gb * causal_depthwise_conv(gc * xa, conv_w)
    a = a * jax.nn.silu(za)
    bo = chunked_spatial_gating(jax.nn.gelu(u), jax.nn.gelu(v),
                                gm_ln_g, gm_ln_b, gm_ws, gm_bs)
    bo = bo * jax.nn.silu(zb)
    return jnp.concatenate([a, bo], axis=-1) @ w_out


def memory_cross_attention(h, m, w_q, w_kv, w_xo):
    b, s, _ = h.shape
    q = (h @ w_q).reshape(b, s, X_HEADS, X_HEAD_DIM)
    k, vv = jnp.split(m @ w_kv, 2, axis=-1)
    k = k.reshape(b, MEM_LEN, X_HEADS, X_HEAD_DIM)
    vv = vv.reshape(b, MEM_LEN, X_HEADS, X_HEAD_DIM)
    scores = jnp.einsum("bshd,bmhd->bhsm", q, k).astype(jnp.float32)
    p = jax.nn.softmax(scores * (1.0 / math.sqrt(X_HEAD_DIM)), axis=-1).astype(vv.dtype)
    o = jnp.einsum("bhsm,bmhd->bshd", p, vv).reshape(b, s, D_MODEL)
    return o @ w_xo


def setup_inputs(seed: int = 0) -> dict:
    key = jax.random.key(seed)
    ks = jax.random.split(key, 20)
    f32 = jnp.float32
    L = DEPTH
    nrm = lambda k, shape, scale: jax.random.normal(k, shape, f32) * scale
    return {
        "x": nrm(ks[0], (BATCH, SEQ, D_MODEL), 1.0),
        "mem": nrm(ks[1], (BATCH, MEM_LEN, D_MODEL), 1.0),
        "norm_mix_g": 1.0 + nrm(ks[2], (L, D_MODEL), 0.02),
        "w_in": nrm(ks[3], (L, D_MODEL, IN_DIM), D_MODEL ** -0.5),
        "conv_w": nrm(ks[4], (L, CONV_K, CONV_DIM), CONV_K ** -0.5),
        "gm_ln_g": 1.0 + nrm(ks[5], (L, GM_DIM), 0.02),
        "gm_ln_b": nrm(ks[6], (L, GM_DIM), 0.02),
        "gm_ws": nrm(ks[7], (L, GM_HEADS, CHUNK, CHUNK), 0.5 * CHUNK ** -0.5),
        "gm_bs": 1.0 + nrm(ks[8], (L, GM_HEADS, CHUNK), 0.02),
        "w_out": nrm(ks[9], (L, MIX_DIM, D_MODEL), MIX_DIM ** -0.5),
        "norm_x_g": 1.0 + nrm(ks[10], (L, D_MODEL), 0.02),
        "norm_mem_g": 1.0 + nrm(ks[11], (L, D_MODEL), 0.02),
        "w_q": nrm(ks[12], (L, D_MODEL, D_MODEL), D_MODEL ** -0.5),
        "w_kv": nrm(ks[13], (L, D_MODEL, 2 * D_MODEL), D_MODEL ** -0.5),
        "w_xo": nrm(ks[14], (L, D_MODEL, D_MODEL), D_MODEL ** -0.5),
        "norm_final_g": 1.0 + nrm(ks[15], (D_MODEL,), 0.02),
    }


def reference(x, mem, norm_mix_g, w_in, conv_w, gm_ln_g, gm_ln_b, gm_ws, gm_bs,
              w_out, norm_x_g, norm_mem_g, w_q, w_kv, w_xo, norm_final_g):
    for l in range(DEPTH):
        h = rms_norm(x, norm_mix_g[l])
        x = x + mixer_sublayer(h, w_in[l], conv_w[l], gm_ln_g[l], gm_ln_b[l],
                               gm_ws[l], gm_bs[l], w_out[l])
        h = rms_norm(x, norm_x_g[l])
        m = rms_norm(mem, norm_mem_g[l])
        x = x + memory_cross_attention(h, m, w_q[l], w_kv[l], w_xo[l])
    return rms_norm(x, norm_final_g)
```

```python
import numpy as np
from contextlib import ExitStack
import concourse.bass as bass
import concourse.mybir as mybir
from concourse.bass_utils import run_bass_kernel_spmd

F32 = mybir.dt.float32
BF16 = mybir.dt.bfloat16
AF = mybir.ActivationFunctionType
ALU = mybir.AluOpType

ENGS = ("pe", "act", "dve", "pool", "sp")
EPS = 1e-6
NCORES = 8
TOK = 2048
NT = 16
D = 1024


class Op:
    __slots__ = ("eng", "fn", "stream", "deps", "needs_inc", "sig", "idx")

    def __init__(self, eng, fn, stream):
        self.eng = eng
        self.fn = fn
        self.stream = stream
        self.deps = []
        self.needs_inc = False
        self.sig = None


class Prog:
    def __init__(self, nc):
        self.nc = nc
        self.ops = []
        self.last_writer = {}
        self.readers = {}
        self.last_on = {}
        self.pending = {}

    def op(self, eng, fn, reads=(), writes=(), stream=None, like=None):
        o = Op(eng, fn, stream)
        o.idx = len(self.ops)
        deps = {}
        if like is not None:
            for d in like.deps:
                deps[d.idx] = (d, "like")
        for r in reads:
            w = self.last_writer.get(r)
            if w is not None:
                deps[w.idx] = (w, "raw")
        for r in writes:
            w = self.last_writer.get(r)
            if w is not None and w.idx not in deps:
                deps[w.idx] = (w, "waw")
            for rd in self.readers.get(r, {}).values():
                if rd.idx not in deps:
                    deps[rd.idx] = (rd, "war")
        for d in self.pending.pop(eng, ()):
            if d.idx not in deps:
                deps[d.idx] = (d, "bar")
        for _, (d, kind) in sorted(deps.items()):
            same = (d.stream is None and o.stream is None and d.eng == o.eng)
            if same and o.eng == "pe":
                continue
            o.deps.append(d)
            d.needs_inc = True
        for r in reads:
            rk = eng if stream is None else ("dma", o.idx)
            self.readers.setdefault(r, {})[rk] = o
        for r in writes:
            self.last_writer[r] = o
            self.readers[r] = {}
        if stream is None:
            self.last_on[eng] = o
        self.ops.append(o)
        return o

    def barrier(self, exempt=()):
        lasts = [o for o in self.last_on.values()]
        for e in ENGS:
            if e not in exempt:
                self.pending[e] = list(lasts)

    def pe(self, fn, reads=(), writes=()):
        return self.op("pe", fn, reads, writes)

    def act(self, fn, reads=(), writes=()):
        return self.op("act", fn, reads, writes)

    def dve(self, fn, reads=(), writes=()):
        return self.op("dve", fn, reads, writes)

    def pool(self, fn, reads=(), writes=()):
        return self.op("pool", fn, reads, writes)

    def dma(self, q, stream, fn, reads=(), writes=(), like=None):
        return self.op(q, fn, reads, writes, stream=stream, like=like)

    def emit(self, final_wait_streams=()):
        nc = self.nc
        counters = {}
        semkeys = []
        for o in self.ops:
            key = ("dma", o.stream) if o.stream is not None else ("eng", o.eng)
            if o.needs_inc or o.stream is not None:
                step = 16 if o.stream is not None else 1
                counters[key] = counters.get(key, 0) + step
                o.sig = (key, counters[key])
                if key not in semkeys:
                    semkeys.append(key)
        with ExitStack() as es:
            sems = {}
            for key in semkeys:
                sems[key] = es.enter_context(nc.semaphore("s_%s_%s" % key))
            block = es.enter_context(nc.Block())
            per_eng = {e: [o for o in self.ops if o.eng == e] for e in ENGS}

            def run(engname, engobj):
                waited = {}
                for o in per_eng[engname]:
                    for d in o.deps:
                        key, val = d.sig
                        if waited.get(key, 0) >= val:
                            continue
                        engobj.wait_ge(sems[key], val)
                        waited[key] = val
                    ins = o.fn(engobj)
                    if o.sig is not None:
                        ins.then_inc(sems[o.sig[0]], 16 if o.stream is not None else 1)
                if engname == "sp":
                    for st in final_wait_streams:
                        key = ("dma", st)
                        if key in counters:
                            engobj.wait_ge(sems[key], counters[key])

            @block.tensor
            def _(e):
                run("pe", e)

            @block.scalar
            def _(e):
                run("act", e)

            @block.vector
            def _(e):
                run("dve", e)

            @block.gpsimd
            def _(e):
                run("pool", e)

            @block.sync
            def _(e):
                run("sp", e)
        self.nsems = len(semkeys)


class Arena:
    def __init__(self, base_ap, nelem):
        self.base = base_ap
        self.n = nelem
        self.off = 0

    def reset(self):
        self.off = 0

    def alloc(self, free_shape, dtype):
        n = int(np.prod(free_shape))
        nb = n * (2 if dtype == F32 else 1)
        nb = (nb + 15) // 16 * 16
        assert self.off + nb <= self.n, ("arena overflow", self.off, nb, self.n)
        ap = self.base[:, self.off:self.off + nb]
        self.off += nb
        if dtype == F32:
            ap = ap.bitcast(F32)
        ap = ap[:, 0:n]
        if len(free_shape) == 2:
            ap = ap.rearrange("p (a b) -> p a b", a=free_shape[0])
        elif len(free_shape) == 3:
            ap = ap.rearrange("p (a b c) -> p a b c", a=free_shape[0], b=free_shape[1])
        elif len(free_shape) == 4:
            ap = ap.rearrange("p (a b c d) -> p a b c d", a=free_shape[0], b=free_shape[1], c=free_shape[2])
        return ap


NSLOT = 7
PF = 3
NXS = 12
REGB = 35200
WQ_EL = 8192


def build_program():
    nc = bass.Bass("TRN2", target_bir_lowering=False)

    def din(name, shape):
        return nc.dram_tensor(name, list(shape), F32, kind="ExternalInput").ap()

    x_d = din("x", [TOK, D])
    xh_d = din("xh", [128, D])
    mem_d = din("mem", [256, D])
    w_in_d = din("w_in", [56 * 128, D])
    w_out_d = din("w_out", [2048, D])
    w_q_d = din("w_q", [D, D])
    w_kv_d = din("w_kv", [D, 2048])
    w_xo_d = din("w_xo", [D, D])
    gx_d = din("gx", [128, 8])
    gmem_d = din("gmem", [128, 8])
    gfin_d = din("gfin", [128, D])
    gmixb_d = din("gmixb", [128, D])
    convw_d = din("convw", [128, 24])
    lng_d = din("lng", [128, 8])
    lnb_d = din("lnb", [128, 8])
    ws_d = din("ws", [128, 1024])
    bsb_d = din("bsb", [128, 1024])
    tril_d = din("tril", [128, 128])
    ident_d = din("ident", [128, 128])
    out_d = nc.dram_tensor("out", [TOK, D], F32, kind="ExternalOutput").ap()

    w_out_v = w_out_d.rearrange("(k p) n -> p k n", p=128)
    w_q_v = w_q_d.rearrange("(k p) n -> p k n", p=128)
    w_kv_v = w_kv_d.rearrange("(k p) n -> p k n", p=128)
    w_xo_v = w_xo_d.rearrange("(k p) n -> p k n", p=128)

    with ExitStack() as es:
        def sb(name, shape, dt):
            return es.enter_context(nc.sbuf_tensor("sb_" + name, list(shape), dt))

        regA = sb("regA", [128, 16400], BF16)
        mixT = sb("mixT", [128, 16, 2048], BF16)
        wout = sb("wout", [128, 16, 1024], BF16)
        kT = sb("kT", [128, 8, 256], BF16)
        Vt = sb("Vt", [128, 2, 1024], BF16)
        ident = sb("ident", [128, 128], BF16)
        onesb = sb("onesb", [128, 128], BF16)
        gx = sb("gx", [128, 8], F32)
        gmem = sb("gmem", [128, 8], F32)
        convw = sb("convw", [128, 24], F32)
        lng = sb("lng", [128, 8], F32)
        lnb = sb("lnb", [128, 8], F32)
        mhalf = sb("mhalf", [128, 4], F32)
        regB = sb("regB", [128, REGB], BF16)

        ps = [es.enter_context(nc.psum_tensor("ps%d" % b, [128, 512], F32)) for b in (0, 1, 2, 3)]
        psT4 = es.enter_context(nc.psum_tensor("psT4", [128, 1024], BF16))
        psT5 = es.enter_context(nc.psum_tensor("psT5", [128, 1024], BF16))
        ps6 = es.enter_context(nc.psum_tensor("ps6", [128, 512], F32))
        ps7 = es.enter_context(nc.psum_tensor("ps7", [128, 512], F32))
        psf = {0: ps[0], 1: ps[1], 2: ps[2], 3: ps[3], 6: ps6, 7: ps7}
        psT = {4: psT4, 5: psT5}

        hT = regA[:, 0:16400].rearrange("p (k n) -> p k n", k=8)
        wxo = regA[:, 0:8192].rearrange("p (k n) -> p k n", k=8)
        HT_KEYS = [("hT", i) for i in range(4)] + [("hT", "h")]
        wkv = wout[:, :, :].rearrange("p j n -> p (j n)").rearrange("p (k n) -> p k n", k=8)
        wq = regB[:, 0:WQ_EL].rearrange("p (k n) -> p k n", k=8)

        def xst(s):
            return mixT[:, 4 + s, :].bitcast(F32)

        def xst_keys(s):
            return [("mix", 4 + s, tg) for tg in range(4)]

        P = Prog(nc)
        ar = Arena(regB, REGB)

        ybuf = ar.alloc([2, 2050], BF16)
        gcs = ar.alloc([2, 512], BF16)
        gch = ar.alloc([16], BF16)
        ccv = ar.alloc([2, 512], BF16)
        gbc = ar.alloc([2, 512], BF16)
        gbs = ar.alloc([2, 512], BF16)
        mT = ar.alloc([8, 256], BF16)
        assert ar.off >= WQ_EL, ar.off
        EARLY_KEYS = ([("y", jy, t) for jy in range(2) for t in (0, 1, 2, 3, "h")] + [("gcs", p) for p in range(2)]
                      + ["gch"] + [("ccv", p) for p in range(2)] + [("gbc", p) for p in range(2)] + [("gbs", p) for p in range(2)]
                      + [("mT", 0), ("mT", 1)])
        X1_OFF = ar.off
        hn = ar.alloc([4, 1024], BF16)
        sqj = ar.alloc([2, 1024], BF16)
        gmixb = ar.alloc([1024], F32)
        wsb = ar.alloc([8, 128], BF16)
        trilb = ar.alloc([128], BF16)
        LATE_END = ar.off
        LATE_KEYS = [("hn", i) for i in range(4)] + [("sqj", q) for q in range(2)] + ["gmixb", "wsb", "trilb"]
        wblk = ar.alloc([NSLOT, 8, 128], BF16)
        gvT = ar.alloc([2, 512], BF16)
        nmr = ar.alloc([2, 4], F32)
        Abf = ar.alloc([2, 4, 128], BF16)
        spt = ar.alloc([2, 512], BF16)
        gub = ar.alloc([2, 512], BF16)
        stt = ar.alloc([2, 4, 6], F32)
        mvv = ar.alloc([2, 4, 2], F32)
        rs4 = ar.alloc([2, 4], F32)
        tm4 = ar.alloc([2, 4], F32)
        ss = ar.alloc([NXS], F32)
        rs = ar.alloc([NXS], F32)
        tms = ar.alloc([NXS], F32)
        Rb = ar.alloc([8, 128], F32)
        WcT = ar.alloc([8, 128], BF16)
        NX1 = 4
        ar.off = X1_OFF
        x1 = ar.alloc([NX1, 1024], F32)
        assert ar.off <= LATE_END, (ar.off, LATE_END)

        cnt = {"c": 0}

        def cdma(q, dst, src, key):
            cnt["c"] += 1
            P.dma(q, "c%d" % cnt["c"], lambda e: e.dma_start(out=dst, in_=src), writes=[key])

        cdma("pool", ident[:], ident_d, "ident")
        P.pool(lambda e: e.memset(mhalf[:], -0.5), writes=["mhalf"])
        P.pool(lambda e: e.memset(onesb[:], 1.0), writes=["onesb"])

        def late_consts():
            cdma("sp", gmem[:], gmem_d, "gmem")
            cdma("sp", gx[:], gx_d, "gx")
            cdma("sp", convw[:], convw_d, "convw")
            cdma("sp", lng[:], lng_d, "lng")
            cdma("sp", lnb[:], lnb_d, "lnb")

        tcount = {"n": 0}

        def norm_T_tile(src_ap, gvec, gkey, dstT, dst_lo, src_lo, n, dst_keys, gfull=None, rows=None):
            i = tcount["n"]
            tcount["n"] += 1
            s = i % NXS
            h3 = i % 4
            q2 = i % 2
            tb = 4 + (i % 4)
            pT_ = psT[tb][:, :] if tb in psT else psf[tb][:, :].bitcast(BF16)
            xs = xst(s)
            if rows is None:
                P.dma("sp", "xs%d" % s, lambda e: e.dma_start(out=xs, in_=src_ap), writes=xst_keys(s))
            else:
                r_lo, r_hi = rows
                P.pool(lambda e: e.memset(xs, 0.0), writes=xst_keys(s))
                P.dma("sp", "xs%d" % s, lambda e: e.dma_start(out=xs[r_lo:r_hi, :], in_=src_ap[r_lo:r_hi, :]),
                      writes=xst_keys(s))
            P.act(lambda e: e.activation(out=sqj[:, q2, :], in_=xs, func=AF.Square, accum_out=ss[:, s:s + 1]),
                  reads=xst_keys(s), writes=[("sqj", q2), ("ss", s)])
            P.pool(lambda e: e.tensor_scalar(out=tms[:, s:s + 1], in0=ss[:, s:s + 1], scalar1=1.0 / D, scalar2=EPS,
                                             op0=ALU.mult, op1=ALU.add),
                   reads=[("ss", s)], writes=[("tms", s)])
            P.pool(lambda e: e.tensor_tensor(out=rs[:, s:s + 1], in0=tms[:, s:s + 1], in1=mhalf[:, 0:1], op=ALU.pow),
                   reads=[("tms", s), "mhalf"], writes=[("rs", s)])
            pv = pT_.rearrange("p (k t) -> p k t", k=8)

            def front2():
                if gfull is None:
                    P.dve(lambda e: e.tensor_scalar(out=hn[:, h3, :], in0=xs, scalar1=rs[:, s:s + 1], scalar2=None,
                                                    op0=ALU.mult),
                          reads=xst_keys(s) + [("rs", s)], writes=[("hn", h3)])
                else:
                    P.dve(lambda e: e.scalar_tensor_tensor(out=hn[:, h3, :], in0=xs, scalar=rs[:, s:s + 1], in1=gfull,
                                                           op0=ALU.mult, op1=ALU.mult),
                          reads=xst_keys(s) + [("rs", s), gkey], writes=[("hn", h3)])

            def mid():
                for kc in range(8):
                    P.pe(lambda e, kc=kc: e.transpose(out=pT_[:, kc * 128:(kc + 1) * 128],
                                                      in_=hn[:, h3, kc * 128:(kc + 1) * 128], identity=ident[:]),
                         reads=[("hn", h3), "ident"], writes=[("ps", tb)])

            def back():
                if gfull is None:
                    P.dve(lambda e: e.tensor_tensor(out=dstT[:, :, dst_lo:dst_lo + n], in0=pv[:, :, src_lo:src_lo + n],
                                                    in1=gvec[:, :].unsqueeze(2).broadcast_to([128, 8, n]), op=ALU.mult),
                          reads=[("ps", tb), gkey], writes=dst_keys)
                elif i % 4 in (0, 3):
                    P.act(lambda e: e.activation(out=dstT[:, :, dst_lo:dst_lo + n], in_=pv[:, :, src_lo:src_lo + n],
                                                 func=AF.Copy),
                          reads=[("ps", tb)], writes=dst_keys)
                else:
                    P.dve(lambda e: e.tensor_copy(out=dstT[:, :, dst_lo:dst_lo + n], in_=pv[:, :, src_lo:src_lo + n]),
                          reads=[("ps", tb)], writes=dst_keys)
            return front2, mid, back

        order = []
        order.append(("conv0", 0, [1024, 2048, 3072, 0]))
        for j in range(1, 8):
            order.append(("gcxa", j, [1024 + j * 128, 2048 + j * 128]))
            order.append(("zagb", j, [3072 + j * 128, j * 128]))
        for j in range(8):
            order.append(("zb", j, [6144 + j * 128]))
        for j in range(8):
            order.append(("vu", j, [5120 + j * 128, 4096 + j * 128]))
        blocks = [c for (_, _, cols) in order for c in cols]
        loaded = {"n": 0}

        def load_block_upto(n):
            while loaded["n"] < min(n, len(blocks)):
                bi = loaded["n"]
                loaded["n"] += 1
                slot = bi % NSLOT
                col = blocks[bi]
                P.dma("pool", "wb%d" % slot,
                      lambda e, slot=slot, col=col: e.dma_start(out=wblk[:, slot, :, :],
                                                                in_=w_in_d[col:col + 128, :].rearrange("p (k n) -> p k n", k=8)),
                      writes=[("wblk", slot)])

        mainb = {"n": 0}

        def main_mm(bi, tg):
            slot = bi % NSLOT
            b = mainb["n"] % 3
            mainb["n"] += 1
            for k in range(8):
                P.pe(lambda e, k=k, b=b: e.matmul(out=psf[b][:, :], lhsT=wblk[:, slot, k, :],
                                                  rhs=hT[:, k, 2 + tg * 512:2 + (tg + 1) * 512],
                                                  start=(k == 0), stop=(k == 7)),
                     reads=[("wblk", slot), ("hT", tg)], writes=[("ps", b)])
            return b

        def halo_mm(bi, off):
            slot = bi % NSLOT
            for k in range(8):
                P.pe(lambda e, k=k: e.matmul(out=psf[3][:, off:off + 2], lhsT=wblk[:, slot, k, :],
                                             rhs=hT[:, k, 0:2], start=(k == 0), stop=(k == 7)),
                     reads=[("wblk", slot), ("hT", "h")], writes=[("ps", 3)])

        vcount = {"n": 0}

        def v_step(bi, j, tg):
            par = vcount["n"] % 2
            vcount["n"] += 1
            tb = 4 + par
            mb = 6 + par
            b = main_mm(bi, tg)
            P.act(lambda e: e.activation(out=gvT[:, par, :], in_=psf[b][:, :], func=AF.Gelu_apprx_tanh),
                  reads=[("ps", b)], writes=[("gvT", par)])
            yield
            for ck in range(4):
                P.pe(lambda e, ck=ck: e.transpose(out=psT[tb][:, ck * 128:(ck + 1) * 128],
                                                  in_=gvT[:, par, ck * 128:(ck + 1) * 128], identity=ident[:]),
                     reads=[("gvT", par), "ident"], writes=[("ps", tb)])
            for ck in range(4):
                P.dve(lambda e, ck=ck: e.bn_stats(out=stt[:, par, ck, :], in_=psT[tb][:, ck * 128:(ck + 1) * 128]),
                      reads=[("ps", tb)], writes=[("stt", par, ck)])
            for ck in range(4):
                P.dve(lambda e, ck=ck: e.bn_aggr(out=mvv[:, par, ck, :], in_=stt[:, par, ck, :]),
                      reads=[("stt", par, ck)], writes=[("mvv", par, ck)])
            mvk = [("mvv", par, ck) for ck in range(4)]
            P.pool(lambda e: e.tensor_scalar(out=tm4[:, par, :], in0=mvv[:, par, :, 1], scalar1=EPS, scalar2=None,
                                             op0=ALU.add),
                   reads=mvk, writes=[("tm4", par)])
            P.pool(lambda e: e.tensor_tensor(out=rs4[:, par, :], in0=tm4[:, par, :], in1=mhalf[:, :], op=ALU.pow),
                   reads=[("tm4", par), "mhalf"], writes=[("rs4", par)])
            P.pool(lambda e: e.tensor_tensor(out=tm4[:, par, :], in0=mvv[:, par, :, 0], in1=rs4[:, par, :], op=ALU.mult),
                   reads=mvk + [("rs4", par), ("tm4", par)], writes=[("tm4", par)])
            P.pool(lambda e: e.tensor_scalar(out=nmr[:, par, :], in0=tm4[:, par, :], scalar1=-1.0, scalar2=None,
                                             op0=ALU.mult),
                   reads=[("tm4", par)], writes=[("nmr", par)])
            yield
            for ck in range(4):
                P.act(lambda e, ck=ck: e.activation(out=Abf[:, par, ck, :], in_=psT[tb][:, ck * 128:(ck + 1) * 128],
                                                    func=AF.Identity, scale=rs4[:, par, ck:ck + 1],
                                                    bias=nmr[:, par, ck:ck + 1]),
                      reads=[("ps", tb), ("rs4", par), ("nmr", par)], writes=[("Abf", par, ck)])
            yield
            for ck in range(4):
                P.pe(lambda e, ck=ck: e.matmul(out=psf[mb][:, ck * 128:(ck + 1) * 128], lhsT=Abf[:, par, ck, :],
                                               rhs=WcT[:, j, :], start=True, stop=True),
                     reads=[("Abf", par, ck), "WcT"], writes=[("ps", mb)])
            P.dve(lambda e: e.scalar_tensor_tensor(
                out=spt[:, par, :].rearrange("p (c t) -> p c t", c=4),
                in0=psf[mb][:, :].rearrange("p (c t) -> p c t", c=4), scalar=lng[:, j:j + 1],
                in1=Rb[:, j, :].unsqueeze(1).broadcast_to([128, 4, 128]), op0=ALU.mult, op1=ALU.add),
                reads=[("ps", mb), "lng", "Rb"], writes=[("spt", par)])
            P.dve(lambda e: e.tensor_tensor(out=mixT[:, 8 + j, tg * 512:(tg + 1) * 512],
                                            in0=mixT[:, 8 + j, tg * 512:(tg + 1) * 512], in1=spt[:, par, :],
                                            op=ALU.mult),
                  reads=[("spt", par), ("mix", 8 + j, tg)], writes=[("mix", 8 + j, tg)])

        def step_gcxa(bgc, bxa, j, tg, defer_halo=False):
            b = main_mm(bgc, tg)
            p2 = tg % 2
            jy = j % 2
            P.act(lambda e: e.activation(out=gcs[:, p2, :], in_=psf[b][:, :], func=AF.Copy),
                  reads=[("ps", b)], writes=[("gcs", p2)])

            def halo_part():
                halo_mm(bgc, 0)
                halo_mm(bxa, 2)
                P.act(lambda e: e.activation(out=gch[:, 0:4], in_=psf[3][:, 0:4], func=AF.Copy),
                      reads=[("ps", 3)], writes=["gch"])
                P.dve(lambda e: e.tensor_tensor(out=ybuf[:, jy, 0:2], in0=gch[:, 0:2], in1=gch[:, 2:4], op=ALU.mult),
                      reads=["gch"], writes=[("y", jy, "h")])

            b2 = main_mm(bxa, tg)
            P.dve(lambda e: e.tensor_tensor(out=ybuf[:, jy, 2 + tg * 512:2 + (tg + 1) * 512], in0=psf[b2][:, :],
                                            in1=gcs[:, p2, :], op=ALU.mult),
                  reads=[("ps", b2), ("gcs", p2)], writes=[("y", jy, tg)])
            if tg == 0:
                if defer_halo:
                    return halo_part
                halo_part()
            return None

        def step_za(bi, j, tg):
            b = main_mm(bi, tg)
            P.act(lambda e: e.activation(out=mixT[:, j, tg * 512:(tg + 1) * 512], in_=psf[b][:, :], func=AF.Silu),
                  reads=[("ps", b)], writes=[("mix", j, tg)])

        def step_gb(bi, j, tg):
            b = main_mm(bi, tg)
            jy = j % 2
            p2 = tg % 2
            P.act(lambda e: e.activation(out=gbs[:, p2, :], in_=psf[b][:, :], func=AF.Copy),
                  reads=[("ps", b)], writes=[("gbs", p2)])
            yk = [("y", jy, tg), ("y", jy, tg - 1 if tg > 0 else "h")]
            P.dve(lambda e: e.tensor_scalar(out=ccv[:, p2, :], in0=ybuf[:, jy, tg * 512:tg * 512 + 512],
                                            scalar1=convw[:, j * 3:j * 3 + 1], scalar2=None, op0=ALU.mult),
                  reads=yk + ["convw"], writes=[("ccv", p2)])
            for kk in (1, 2):
                P.dve(lambda e, kk=kk: e.scalar_tensor_tensor(
                    out=ccv[:, p2, :], in0=ybuf[:, jy, tg * 512 + kk:tg * 512 + kk + 512],
                    scalar=convw[:, j * 3 + kk:j * 3 + kk + 1], in1=ccv[:, p2, :], op0=ALU.mult, op1=ALU.add),
                    reads=yk + ["convw", ("ccv", p2)], writes=[("ccv", p2)])
            P.dve(lambda e: e.tensor_tensor(out=gbc[:, p2, :], in0=gbs[:, p2, :], in1=ccv[:, p2, :], op=ALU.mult),
                  reads=[("gbs", p2), ("ccv", p2)], writes=[("gbc", p2)])
            P.pool(lambda e: e.tensor_tensor(out=mixT[:, j, tg * 512:(tg + 1) * 512],
                                             in0=mixT[:, j, tg * 512:(tg + 1) * 512], in1=gbc[:, p2, :], op=ALU.mult),
                   reads=[("gbc", p2), ("mix", j, tg)], writes=[("mix", j, tg)])

        def step_zb(bi, j, tg):
            b = main_mm(bi, tg)
            P.act(lambda e: e.activation(out=mixT[:, 8 + j, tg * 512:(tg + 1) * 512], in_=psf[b][:, :], func=AF.Silu),
                  reads=[("ps", b)], writes=[("mix", 8 + j, tg)])

        def step_u(bi, j, tg):
            b = main_mm(bi, tg)
            p2 = tg % 2
            P.act(lambda e: e.activation(out=gub[:, p2, :], in_=psf[b][:, :], func=AF.Gelu_apprx_tanh),
                  reads=[("ps", b)], writes=[("gub", p2)])
            P.dve(lambda e: e.tensor_tensor(out=mixT[:, 8 + j, tg * 512:(tg + 1) * 512],
                                            in0=mixT[:, 8 + j, tg * 512:(tg + 1) * 512], in1=gub[:, p2, :], op=ALU.mult),
                  reads=[("gub", p2), ("mix", 8 + j, tg)], writes=[("mix", 8 + j, tg)])

        active = []

        def advance():
            for g_ in list(active):
                try:
                    next(g_)
                except StopIteration:
                    active.remove(g_)

        def kv_phase():
            for c in range(8):
                b = c % 3
                for k in range(8):
                    P.pe(lambda e, c=c, k=k, b=b: e.matmul(out=psf[b][:, 0:256], lhsT=wkv[:, k, c * 128:(c + 1) * 128],
                                                           rhs=mT[:, k, :], start=(k == 0), stop=(k == 7)),
                         reads=[("wkv", c // 4), ("mT", 0), ("mT", 1)], writes=[("ps", b)])
                P.act(lambda e, c=c, b=b: e.activation(out=kT[:, c, :], in_=psf[b][:, 0:256], func=AF.Copy),
                      reads=[("ps", b)], writes=[("kT", c)])
            for mt in range(2):
                for cg in range(2):
                    b = (mt * 2 + cg) % 3
                    for k in range(8):
                        P.pe(lambda e, mt=mt, cg=cg, k=k, b=b: e.matmul(
                            out=psf[b][:, :], lhsT=mT[:, k, mt * 128:(mt + 1) * 128],
                            rhs=wkv[:, k, 1024 + cg * 512:1024 + (cg + 1) * 512], start=(k == 0), stop=(k == 7)),
                            reads=[("wkv", 2 + cg), ("mT", mt)], writes=[("ps", b)])
                    P.act(lambda e, mt=mt, cg=cg, b=b: e.activation(out=Vt[:, mt, cg * 512:(cg + 1) * 512],
                                                                    in_=psf[b][:, :], func=AF.Copy),
                          reads=[("ps", b)], writes=[("V", mt)])

        def ws_setup(part):
            if part == 0:
                P.dve(lambda e: e.tensor_tensor(out=wsb, in0=wsb, in1=trilb.unsqueeze(1).broadcast_to([128, 8, 128]),
                                                op=ALU.mult),
                      reads=["wsb", "trilb"], writes=["wsb"])
            elif part == 1:
                for h in range(8):
                    P.pe(lambda e, h=h: e.transpose(out=psT[4][:, h * 128:(h + 1) * 128], in_=wsb[:, h, :],
                                                    identity=ident[:]),
                         reads=["wsb", "ident"], writes=[("ps", 4)])
                P.act(lambda e: e.activation(out=WcT.rearrange("p h t -> p (h t)"), in_=psT[4][:, :], func=AF.Copy),
                      reads=[("ps", 4)], writes=["WcT"])
            else:
                for h in range(8):
                    b = 6 + h // 4
                    P.pe(lambda e, h=h, b=b: e.matmul(out=psf[b][:, (h % 4) * 128:(h % 4 + 1) * 128], lhsT=onesb[:],
                                                     rhs=WcT[:, h, :], start=True, stop=True),
                         reads=["onesb", "WcT"], writes=[("ps", b)])
                for h in range(8):
                    b = 6 + h // 4
                    P.dve(lambda e, h=h, b=b: e.scalar_tensor_tensor(
                        out=Rb[:, h, :], in0=psf[b][:, (h % 4) * 128:(h % 4 + 1) * 128], scalar=lnb[:, h:h + 1],
                        in1=Rb[:, h, :], op0=ALU.mult, op1=ALU.add),
                        reads=[("ps", b), "lnb", "Rb"], writes=["Rb"])

        bi = 0
        gi = 0
        kind, j, cols = order[gi]

        def wkv_dmas(which=(0, 1, 2, 3)):
            for cgi in which:
                P.dma("pool", "wkv%d" % cgi,
                      lambda e, cgi=cgi: e.dma_start(out=wkv[:, :, cgi * 512:(cgi + 1) * 512],
                                                     in_=w_kv_v[:, :, cgi * 512:(cgi + 1) * 512]),
                      writes=[("wkv", cgi)])

        load_block_upto(5)
        P.act(lambda e: e.activation(out=sqj[:, 0, 0:1], in_=mhalf[:, 0:1], func=AF.Square), reads=["mhalf"], writes=[("sqj", 0)])

        def xfront(tt):
            return norm_T_tile(x_d[tt * 128:(tt + 1) * 128, :], None, "gmixb", hT, 2 + tt * 128, 0, 128,
                               [("hT", tt // 4)], gfull=gmixb)

        def xfront(tt):
            tcount["n"] = tt
            return norm_T_tile(x_d[tt * 128:(tt + 1) * 128, :], None, "gmixb", hT, 2 + tt * 128, 0, 128,
                               [("hT", tt // 4)], gfull=gmixb)

        def mb_group(g):
            g[0][1]()
            g[1][1]()
            g[0][2]()
            g[2][1]()
            g[1][2]()
            g[3][1]()
            g[2][2]()
            g[3][2]()

        grp = {0: [xfront(0)]}
        cdma("sp", gmixb, gmixb_d, "gmixb")
        grp[0] += [xfront(tt) for tt in range(1, 4)]
        for t_ in grp[0]:
            t_[0]()
        tcount["n"] = 16
        fh, mh, bh = norm_T_tile(xh_d, None, "gmixb", hT, 0, 126, 2, [("hT", "h")], gfull=gmixb, rows=(126, 128))
        mb_group(grp[0])
        fh()
        mh()
        bh()
        grp[1] = [xfront(tt) for tt in range(4, 8)]
        late_consts()
        for t_ in grp[1]:
            t_[0]()
        for tg in range(4):
            g_ = grp.get(tg + 1) if tg + 1 < 4 else None
            early_f = (tg >= 1 and tg + 2 < 4)
            if early_f:
                grp[tg + 2] = [xfront(tt) for tt in range(4 * (tg + 2), 4 * (tg + 2) + 4)]
            hp = step_gcxa(bi, bi + 1, 0, tg, defer_halo=True)
            if g_ is not None:
                g_[0][1]()
                g_[1][1]()
                g_[0][2]()
            step_za(bi + 2, 0, tg)
            if hp is not None:
                hp()
            if g_ is not None:
                g_[2][1]()
                g_[1][2]()
                g_[3][1]()
                g_[2][2]()
                g_[3][2]()
            if early_f:
                for t_ in grp[tg + 2]:
                    t_[0]()
            step_gb(bi + 3, 0, tg)
            if tg + 2 < 4 and not early_f:
                grp[tg + 2] = [xfront(tt) for tt in range(4 * (tg + 2), 4 * (tg + 2) + 4)]
                for t_ in grp[tg + 2]:
                    t_[0]()
            if tg == 1:
                load_block_upto(7)
            if tg == 2:
                tcount["n"] = 17
                mem_t = [norm_T_tile(mem_d[mt * 128:(mt + 1) * 128, :], gmem, "gmem", mT, mt * 128, 0, 128,
                                     [("mT", mt)]) for mt in range(2)]
                for t_ in mem_t:
                    t_[0]()
        bi += 4
        gi += 1
        wout_at = {9: 0, 11: 1, 12: 2, 13: 3}
        wq_at = {24: 0, 26: 1}
        while gi < len(order):
            kind, j, cols = order[gi]
            load_block_upto(bi + len(cols) + PF)
            if 1 <= gi <= 4:
                wkv_dmas((gi - 1,))
            if gi == 1:
                mem_t[0][1]()
                mem_t[1][1]()
                mem_t[0][2]()
                mem_t[1][2]()
            if gi == 2:
                cdma("sp", Rb.rearrange("p h s -> p (h s)"), bsb_d, "Rb")
                cdma("pool", wsb.rearrange("p h s -> p (h s)"), ws_d, "wsb")
                cdma("pool", trilb, tril_d, "trilb")
            if gi == 5:
                kv_phase()
            if gi in (6, 8, 10):
                ws_setup((gi - 6) // 2)
            if gi in wout_at:
                g = wout_at[gi]
                P.dma("pool", "wout%d" % g,
                      lambda e, g=g: e.dma_start(out=wout[:, 4 * g:4 * g + 4, :], in_=w_out_v[:, 4 * g:4 * g + 4, :]),
                      writes=[("wout", g)] + [("wkv", c) for c in range(4)])
            if gi == 17:
                o0 = P.dma("pool", "wq0", lambda e: e.dma_start(out=wq[:, 0:4, :], in_=w_q_v[:, 0:4, :]),
                           writes=EARLY_KEYS + [("wq", 0)])
                P.dma("pool", "wq1", lambda e: e.dma_start(out=wq[:, 4:8, :], in_=w_q_v[:, 4:8, :]),
                      writes=[("wq", 1)], like=o0)
            for tg in range(4):
                if kind == "gcxa":
                    step_gcxa(bi, bi + 1, j, tg)
                elif kind == "zagb":
                    step_za(bi, j, tg)
                    step_gb(bi + 1, j, tg)
                elif kind == "zb":
                    step_zb(bi, j, tg)
                elif kind == "vu":
                    gen = v_step(bi, j, tg)
                    next(gen)
                    step_u(bi + 1, j, tg)
                    advance()
                    active.append(gen)
            bi += len(cols)
            gi += 1
        NS2 = 3
        BIG = {0: (0, 1), 1: (2, 3), 2: (6, 7)}

        def s1_pe(tt, halves=(0, 1)):
            B = BIG[tt % NS2]
            r0 = tt * 128
            for cg in halves:
                for jc in range(16):
                    P.pe(lambda e, cg=cg, jc=jc: e.matmul(out=psf[B[cg]][:, :], lhsT=mixT[:, jc, r0:r0 + 128],
                                                          rhs=wout[:, jc, cg * 512:(cg + 1) * 512],
                                                          start=(jc == 0), stop=(jc == 15)),
                         reads=[("mix", jc, tt // 4), ("wout", jc // 4)], writes=[("ps", B[cg])])

        def x1_load(tt):
            xl = tt % NX1
            r0 = tt * 128
            extra = LATE_KEYS if tt < NX1 else []
            P.dma("sp", "x1s%d" % xl, lambda e: e.dma_start(out=x1[:, xl, :], in_=x_d[r0:r0 + 128, :]),
                  writes=[("x1", xl, 0), ("x1", xl, 1)] + extra)

        EARLY_S1 = 3
        for t_ in range(NX1):
            x1_load(t_)
        for t_ in range(2):
            s1_pe(t_)
            advance()
        while active:
            advance()

        P.barrier(exempt=("pe",))
        s1_pe(2)
        ar.off = LATE_END
        gfin = ar.alloc([1024], F32)
        tokb = ar.alloc([NS2, 1024], BF16)
        Tb = ar.alloc([NS2, 8, 128], BF16)
        qT = ar.alloc([NS2, 8, 128], BF16)
        Eb = ar.alloc([NS2, 2, 4, 128], BF16)
        sq2 = ar.alloc([1, 1024], BF16)
        ssa = ar.alloc([NS2], F32)
        ssb = ar.alloc([NS2], F32)
        tma = ar.alloc([NS2], F32)
        tmb = ar.alloc([NS2], F32)
        rsa = ar.alloc([NS2], F32)
        rsb = ar.alloc([NS2], F32)
        rden = ar.alloc([NS2, 4], F32)
        denv = psT5[:, 0:32].bitcast(F32)

        o0 = P.dma("pool", "wxo0", lambda e: e.dma_start(out=wxo[:, 0:4, :], in_=w_xo_v[:, 0:4, :]),
                   writes=HT_KEYS + [("wxo", 0)])
        P.dma("pool", "wxo1", lambda e: e.dma_start(out=wxo[:, 4:8, :], in_=w_xo_v[:, 4:8, :]),
              writes=[("wxo", 1)], like=o0)
        cdma("sp", gfin, gfin_d, "gfin")

        def pool_rstd(ssx, tmx, rsx, sl, tag):
            P.pool(lambda e: e.tensor_scalar(out=tmx[:, sl:sl + 1], in0=ssx[:, sl:sl + 1], scalar1=1.0 / D,
                                             scalar2=EPS, op0=ALU.mult, op1=ALU.add),
                   reads=[(tag + "ss", sl)], writes=[(tag + "tm", sl)])
            P.pool(lambda e: e.tensor_tensor(out=rsx[:, sl:sl + 1], in0=tmx[:, sl:sl + 1], in1=mhalf[:, 0:1],
                                             op=ALU.pow),
                   reads=[(tag + "tm", sl), "mhalf"], writes=[(tag + "rs", sl)])

        sqc = {"n": 0}
        def tile_gen(tt):
            sl = tt % NS2
            xl = tt % NX1
            B = BIG[sl]
            tbk = ("ps", B[0])
            pT = psf[B[0]][:, :].bitcast(BF16)
            dnk = ("ps", 5)
            r0 = tt * 128
            X1K = [("x1", xl, 0), ("x1", xl, 1)]
            for cg in range(2):
                P.dve(lambda e, cg=cg: e.tensor_tensor(out=x1[:, xl, cg * 512:(cg + 1) * 512], in0=psf[B[cg]][:, :],
                                                       in1=x1[:, xl, cg * 512:(cg + 1) * 512], op=ALU.add),
                      reads=[("ps", B[cg]), ("x1", xl, cg)], writes=[("x1", xl, cg)])
            q2 = 0
            sqc["n"] += 1
            P.act(lambda e: e.activation(out=sq2[:, q2, :], in_=x1[:, xl, :], func=AF.Square, accum_out=ssa[:, sl:sl + 1]),
                  reads=X1K, writes=[("sq2", q2), ("ass", sl)])
            pool_rstd(ssa, tma, rsa, sl, "a")
            P.dve(lambda e: e.tensor_scalar(out=tokb[:, sl, :], in0=x1[:, xl, :], scalar1=rsa[:, sl:sl + 1],
                                            scalar2=None, op0=ALU.mult),
                  reads=X1K + [("ars", sl)], writes=[("tokb", sl, 0), ("tokb", sl, 1)])
            yield
            for kc in range(8):
                P.pe(lambda e, kc=kc: e.transpose(out=pT[:, kc * 128:(kc + 1) * 128],
                                                  in_=tokb[:, sl, kc * 128:(kc + 1) * 128], identity=ident[:]),
                     reads=[("tokb", sl, kc // 4), "ident"], writes=[tbk])
            P.dve(lambda e: e.tensor_tensor(out=Tb[:, sl, :, :], in0=pT.rearrange("p (k t) -> p k t", k=8),
                                            in1=gx[:, :].unsqueeze(2).broadcast_to([128, 8, 128]), op=ALU.mult),
                  reads=[tbk, "gx"], writes=[("Tb", sl)])
            yield
            for c in range(8):
                bq = B[c // 4]
                for k in range(8):
                    P.pe(lambda e, c=c, k=k, bq=bq: e.matmul(out=psf[bq][:, (c % 4) * 128:(c % 4 + 1) * 128],
                                                             lhsT=wq[:, k, c * 128:(c + 1) * 128], rhs=Tb[:, sl, k, :],
                                                             start=(k == 0), stop=(k == 7)),
                         reads=[("wq", k // 4), ("Tb", sl)], writes=[("ps", bq)])
            for hb in range(2):
                P.act(lambda e, hb=hb: e.activation(out=qT[:, sl, 4 * hb:4 * hb + 4, :].rearrange("p c t -> p (c t)"),
                                                    in_=psf[B[hb]][:, :], func=AF.Copy),
                      reads=[("ps", B[hb])], writes=[("qT", sl, hb)])
            yield
            for mt in range(2):
                for h in range(4):
                    for kk in range(2):
                        P.pe(lambda e, mt=mt, h=h, kk=kk: e.matmul(
                            out=psf[B[mt]][:, h * 128:(h + 1) * 128], lhsT=kT[:, 2 * h + kk, mt * 128:(mt + 1) * 128],
                            rhs=qT[:, sl, 2 * h + kk, :], start=(kk == 0), stop=(kk == 1)),
                            reads=[("kT", 2 * h + kk), ("qT", sl, h // 2)], writes=[("ps", B[mt])])
                P.act(lambda e, mt=mt: e.activation(out=Eb[:, sl, mt, :, :].rearrange("p h t -> p (h t)"),
                                                    in_=psf[B[mt]][:, :], func=AF.Exp, scale=0.0625),
                      reads=[("ps", B[mt])], writes=[("E", sl, mt)])
            yield
            for h in range(4):
                for mt in range(2):
                    P.pe(lambda e, h=h, mt=mt: e.matmul(out=psf[B[h // 2]][:, (h % 2) * 256:(h % 2 + 1) * 256],
                                                        lhsT=Eb[:, sl, mt, h, :], rhs=Vt[:, mt, h * 256:(h + 1) * 256],
                                                        start=(mt == 0), stop=(mt == 1)),
                         reads=[("E", sl, mt), ("V", mt)], writes=[("ps", B[h // 2])])
                for mt in range(2):
                    P.pe(lambda e, h=h, mt=mt: e.matmul(out=denv[:, 4 * sl + h:4 * sl + h + 1], lhsT=Eb[:, sl, mt, h, :],
                                                        rhs=onesb[:, 0:1], start=(mt == 0), stop=(mt == 1)),
                         reads=[("E", sl, mt), "onesb"], writes=[dnk])
            P.dve(lambda e: e.reciprocal(out=rden[:, sl, :], in_=denv[:, 4 * sl:4 * sl + 4]),
                  reads=[dnk], writes=[("rden", sl)])
            for hb in range(2):
                P.dve(lambda e, hb=hb: e.tensor_tensor(
                    out=tokb[:, sl, hb * 512:(hb + 1) * 512].rearrange("p (h d) -> p h d", h=2),
                    in0=psf[B[hb]][:, :].rearrange("p (h d) -> p h d", h=2),
                    in1=rden[:, sl, 2 * hb:2 * hb + 2].unsqueeze(2).broadcast_to([128, 2, 256]), op=ALU.mult),
                    reads=[("ps", B[hb]), ("rden", sl)], writes=[("tokb", sl, hb)])
            yield
            for kc in range(8):
                P.pe(lambda e, kc=kc: e.transpose(out=pT[:, kc * 128:(kc + 1) * 128],
                                                  in_=tokb[:, sl, kc * 128:(kc + 1) * 128], identity=ident[:]),
                     reads=[("tokb", sl, kc // 4), "ident"], writes=[tbk])
            P.act(lambda e: e.activation(out=Tb[:, sl, :, :].rearrange("p k t -> p (k t)"), in_=pT,
                                         func=AF.Copy),
                  reads=[tbk], writes=[("Tb", sl)])
            yield
            for cg in range(2):
                for k in range(8):
                    P.pe(lambda e, cg=cg, k=k: e.matmul(out=psf[B[cg]][:, :], lhsT=Tb[:, sl, k, :],
                                                        rhs=wxo[:, k, cg * 512:(cg + 1) * 512],
                                                        start=(k == 0), stop=(k == 7)),
                         reads=[("Tb", sl), ("wxo", k // 4)], writes=[("ps", B[cg])])
                P.dve(lambda e, cg=cg: e.tensor_tensor(out=x1[:, xl, cg * 512:(cg + 1) * 512], in0=psf[B[cg]][:, :],
                                                       in1=x1[:, xl, cg * 512:(cg + 1) * 512], op=ALU.add),
                      reads=[("ps", B[cg]), ("x1", xl, cg)], writes=[("x1", xl, cg)])
            yield
            q2b = 0
            sqc["n"] += 1
            P.act(lambda e: e.activation(out=sq2[:, q2b, :], in_=x1[:, xl, :], func=AF.Square,
                                         accum_out=ssb[:, sl:sl + 1]),
                  reads=X1K, writes=[("sq2", q2b), ("bss", sl)])
            pool_rstd(ssb, tmb, rsb, sl, "b")
            P.dve(lambda e: e.scalar_tensor_tensor(out=x1[:, xl, :], in0=x1[:, xl, :], scalar=rsb[:, sl:sl + 1],
                                                   in1=gfin, op0=ALU.mult, op1=ALU.mult),
                  reads=X1K + [("brs", sl), "gfin"], writes=X1K)
            P.dma("sp", "st%d" % xl, lambda e: e.dma_start(out=out_d[r0:r0 + 128, :], in_=x1[:, xl, :]), reads=X1K)

        gens = {}

        def run_stage(t, k):
            if not (0 <= t < NT):
                return
            if k in (10, 11):
                if t >= EARLY_S1:
                    s1_pe(t, halves=(k - 10,))
                return
            if k == 1:
                gens[t] = tile_gen(t)
            try:
                next(gens[t])
            except StopIteration:
                assert k == 8, (t, k)

        for n in range(NT):
            if NX1 <= n + 1 < NT:
                x1_load(n + 1)
            for (t, k) in ((n - 2, 4), (n - 1, 2), (n, 10), (n - 2, 5), (n, 11), (n, 1), (n - 2, 6), (n - 1, 3),
                           (n - 2, 7), (n - 2, 8)):
                if n == NT - 1 and k == 8:
                    continue
                run_stage(t, k)
        A_, B_ = NT - 2, NT - 1
        for (t, k) in ((B_, 2), (A_ - 1, 8), (A_, 4), (B_, 3), (A_, 5), (B_, 4), (A_, 6), (B_, 5), (A_, 7), (B_, 6), (A_, 8),
                       (B_, 7), (B_, 8)):
            run_stage(t, k)

        P.emit(final_wait_streams=["st0", "st1", "st2", "st3"])
    return nc


_CACHE = {}


def _prep_inputs(inp):
    f = lambda a: np.ascontiguousarray(np.asarray(a, dtype=np.float32))
    x = f(inp["x"])
    mem = f(inp["mem"])
    vec8 = lambda v: f(np.asarray(v, dtype=np.float32).reshape(8, 128).T)
    common = {
        "w_in": f(np.asarray(inp["w_in"][0], dtype=np.float32).reshape(8, 128, 56, 128).transpose(2, 1, 0, 3)
                  .reshape(56 * 128, D)),
        "w_out": f(inp["w_out"][0]),
        "w_q": f(inp["w_q"][0]),
        "w_kv": f(inp["w_kv"][0]),
        "w_xo": f(inp["w_xo"][0]),
        "gx": vec8(inp["norm_x_g"][0]),
        "gmem": vec8(inp["norm_mem_g"][0]),
        "gfin": f(np.broadcast_to(np.asarray(inp["norm_final_g"], dtype=np.float32)[None, :], (128, D))),
        "gmixb": f(np.broadcast_to(np.asarray(inp["norm_mix_g"][0], dtype=np.float32)[None, :], (128, D))),
        "convw": f(np.asarray(inp["conv_w"][0], dtype=np.float32).reshape(3, 8, 128).transpose(2, 1, 0).reshape(128, 24)),
        "lng": vec8(inp["gm_ln_g"][0]),
        "lnb": vec8(inp["gm_ln_b"][0]),
        "ws": f(np.asarray(inp["gm_ws"][0], dtype=np.float32).transpose(1, 0, 2).reshape(128, 1024)),
        "bsb": f(np.broadcast_to(np.asarray(inp["gm_bs"][0], dtype=np.float32).reshape(1, 1024), (128, 1024))),
        "tril": f(np.tril(np.ones((128, 128), dtype=np.float32))),
        "ident": f(np.eye(128, dtype=np.float32)),
    }
    in_maps = []
    for c in range(NCORES):
        b, half = divmod(c, 2)
        xs = x[b, half * TOK:(half + 1) * TOK]
        xh = np.zeros((128, D), dtype=np.float32)
        if half == 1:
            xh[126:128] = x[b, TOK - 2:TOK]
        m = dict(common)
        m["x"] = f(xs)
        m["xh"] = xh
        m["mem"] = f(mem[b])
        in_maps.append(m)
    return in_maps


def kernel(**inputs):
    in_maps = _prep_inputs(inputs)
    if "nc" not in _CACHE:
        _CACHE["nc"] = build_program()
    nc = _CACHE["nc"]
    res = run_bass_kernel_spmd(nc, in_maps, core_ids=list(range(NCORES)))
    out = np.empty((4, 4096, D), dtype=np.float32)
    for c in range(NCORES):
        b, half = divmod(c, 2)
        out[b, half * TOK:(half + 1) * TOK] = np.asarray(res.results[c]["out"], dtype=np.float32)
    return out
```

```python
import numpy as np
from contextlib import ExitStack
import concourse.bass as bass
import concourse.mybir as mybir
from concourse.bass_utils import run_bass_kernel_spmd

F32 = mybir.dt.float32
BF16 = mybir.dt.bfloat16
AF = mybir.ActivationFunctionType
ALU = mybir.AluOpType

ENGS = ("pe", "act", "dve", "pool", "sp")
EPS = 1e-6
NCORES = 8
TOK = 2048
NT = 16
D = 1024


class Op:
    __slots__ = ("eng", "fn", "stream", "deps", "needs_inc", "sig", "idx")

    def __init__(self, eng, fn, stream):
        self.eng = eng
        self.fn = fn
        self.stream = stream
        self.deps = []
        self.needs_inc = False
        self.sig = None


class Prog:
    def __init__(self, nc):
        self.nc = nc
        self.ops = []
        self.last_writer = {}
        self.readers = {}
        self.last_on = {}
        self.pending = {}

    def op(self, eng, fn, reads=(), writes=(), stream=None, like=None):
        o = Op(eng, fn, stream)
        o.idx = len(self.ops)
        deps = {}
        if like is not None:
            for d in like.deps:
                deps[d.idx] = (d, "like")
        for r in reads:
            w = self.last_writer.get(r)
            if w is not None:
                deps[w.idx] = (w, "raw")
        for r in writes:
            w = self.last_writer.get(r)
            if w is not None and w.idx not in deps:
                deps[w.idx] = (w, "waw")
            for rd in self.readers.get(r, {}).values():
                if rd.idx not in deps:
                    deps[rd.idx] = (rd, "war")
        for d in self.pending.pop(eng, ()):
            if d.idx not in deps:
                deps[d.idx] = (d, "bar")
        for _, (d, kind) in sorted(deps.items()):
            same = (d.stream is None and o.stream is None and d.eng == o.eng)
            if same and o.eng == "pe":
                continue
            o.deps.append(d)
            d.needs_inc = True
        for r in reads:
            rk = eng if stream is None else ("dma", o.idx)
            self.readers.setdefault(r, {})[rk] = o
        for r in writes:
            self.last_writer[r] = o
            self.readers[r] = {}
        if stream is None:
            self.last_on[eng] = o
        self.ops.append(o)
        return o

    def barrier(self, exempt=()):
        lasts = [o for o in self.last_on.values()]
        for e in ENGS:
            if e not in exempt:
                self.pending[e] = list(lasts)

    def pe(self, fn, reads=(), writes=()):
        return self.op("pe", fn, reads, writes)

    def act(self, fn, reads=(), writes=()):
        return self.op("act", fn, reads, writes)

    def dve(self, fn, reads=(), writes=()):
        return self.op("dve", fn, reads, writes)

    def pool(self, fn, reads=(), writes=()):
        return self.op("pool", fn, reads, writes)

    def dma(self, q, stream, fn, reads=(), writes=(), like=None):
        return self.op(q, fn, reads, writes, stream=stream, like=like)

    def emit(self, final_wait_streams=()):
        nc = self.nc
        counters = {}
        semkeys = []
        for o in self.ops:
            key = ("dma", o.stream) if o.stream is not None else ("eng", o.eng)
            if o.needs_inc or o.stream is not None:
                step = 16 if o.stream is not None else 1
                counters[key] = counters.get(key, 0) + step
                o.sig = (key, counters[key])
                if key not in semkeys:
                    semkeys.append(key)
        with ExitStack() as es:
            sems = {}
            for key in semkeys:
                sems[key] = es.enter_context(nc.semaphore("s_%s_%s" % key))
            block = es.enter_context(nc.Block())
            per_eng = {e: [o for o in self.ops if o.eng == e] for e in ENGS}

            def run(engname, engobj):
                waited = {}
                for o in per_eng[engname]:
                    for d in o.deps:
                        key, val = d.sig
                        if waited.get(key, 0) >= val:
                            continue
                        engobj.wait_ge(sems[key], val)
                        waited[key] = val
                    ins = o.fn(engobj)
                    if o.sig is not None:
                        ins.then_inc(sems[o.sig[0]], 16 if o.stream is not None else 1)
                if engname == "sp":
                    for st in final_wait_streams:
                        key = ("dma", st)
                        if key in counters:
                            engobj.wait_ge(sems[key], counters[key])

            @block.tensor
            def _(e):
                run("pe", e)

            @block.scalar
            def _(e):
                run("act", e)

            @block.vector
            def _(e):
                run("dve", e)

            @block.gpsimd
            def _(e):
                run("pool", e)

            @block.sync
            def _(e):
                run("sp", e)
        self.nsems = len(semkeys)


class Arena:
    def __init__(self, base_ap, nelem):
        self.base = base_ap
        self.n = nelem
        self.off = 0

    def reset(self):
        self.off = 0

    def alloc(self, free_shape, dtype):
        n = int(np.prod(free_shape))
        nb = n * (2 if dtype == F32 else 1)
        nb = (nb + 15) // 16 * 16
        assert self.off + nb <= self.n, ("arena overflow", self.off, nb, self.n)
        ap = self.base[:, self.off:self.off + nb]
        self.off += nb
        if dtype == F32:
            ap = ap.bitcast(F32)
        ap = ap[:, 0:n]
        if len(free_shape) == 2:
            ap = ap.rearrange("p (a b) -> p a b", a=free_shape[0])
        elif len(free_shape) == 3:
            ap = ap.rearrange("p (a b c) -> p a b c", a=free_shape[0], b=free_shape[1])
        elif len(free_shape) == 4:
            ap = ap.rearrange("p (a b c d) -> p a b c d", a=free_shape[0], b=free_shape[1], c=free_shape[2])
        return ap


NSLOT = 7
PF = 3
NXS = 12
REGB = 35200
WQ_EL = 8192


def build_program():
    nc = bass.Bass("TRN2", target_bir_lowering=False)

    def din(name, shape):
        return nc.dram_tensor(name, list(shape), F32, kind="ExternalInput").ap()

    x_d = din("x", [TOK, D])
    xh_d = din("xh", [128, D])
    mem_d = din("mem", [256, D])
    w_in_d = din("w_in", [56 * 128, D])
    w_out_d = din("w_out", [2048, D])
    w_q_d = din("w_q", [D, D])
    w_kv_d = din("w_kv", [D, 2048])
    w_xo_d = din("w_xo", [D, D])
    gx_d = din("gx", [128, 8])
    gmem_d = din("gmem", [128, 8])
    gfin_d = din("gfin", [128, D])
    gmixb_d = din("gmixb", [128, D])
    convw_d = din("convw", [128, 24])
    lng_d = din("lng", [128, 8])
    lnb_d = din("lnb", [128, 8])
    ws_d = din("ws", [128, 1024])
    bsb_d = din("bsb", [128, 1024])
    tril_d = din("tril", [128, 128])
    ident_d = din("ident", [128, 128])
    out_d = nc.dram_tensor("out", [TOK, D], F32, kind="ExternalOutput").ap()

    w_out_v = w_out_d.rearrange("(k p) n -> p k n", p=128)
    w_q_v = w_q_d.rearrange("(k p) n -> p k n", p=128)
    w_kv_v = w_kv_d.rearrange("(k p) n -> p k n", p=128)
    w_xo_v = w_xo_d.rearrange("(k p) n -> p k n", p=128)

    with ExitStack() as es:
        def sb(name, shape, dt):
            return es.enter_context(nc.sbuf_tensor("sb_" + name, list(shape), dt))

        regA = sb("regA", [128, 16400], BF16)
        mixT = sb("mixT", [128, 16, 2048], BF16)
        wout = sb("wout", [128, 16, 1024], BF16)
        kT = sb("kT", [128, 8, 256], BF16)
        Vt = sb("Vt", [128, 2, 1024], BF16)
        ident = sb("ident", [128, 128], BF16)
        onesb = sb("onesb", [128, 128], BF16)
        gx = sb("gx", [128, 8], F32)
        gmem = sb("gmem", [128, 8], F32)
        convw = sb("convw", [128, 24], F32)
        lng = sb("lng", [128, 8], F32)
        lnb = sb("lnb", [128, 8], F32)
        mhalf = sb("mhalf", [128, 4], F32)
        regB = sb("regB", [128, REGB], BF16)

        ps = [es.enter_context(nc.psum_tensor("ps%d" % b, [128, 512], F32)) for b in (0, 1, 2, 3)]
        psT4 = es.enter_context(nc.psum_tensor("psT4", [128, 1024], BF16))
        psT5 = es.enter_context(nc.psum_tensor("psT5", [128, 1024], BF16))
        ps6 = es.enter_context(nc.psum_tensor("ps6", [128, 512], F32))
        ps7 = es.enter_context(nc.psum_tensor("ps7", [128, 512], F32))
        psf = {0: ps[0], 1: ps[1], 2: ps[2], 3: ps[3], 6: ps6, 7: ps7}
        psT = {4: psT4, 5: psT5}

        hT = regA[:, 0:16400].rearrange("p (k n) -> p k n", k=8)
        wxo = regA[:, 0:8192].rearrange("p (k n) -> p k n", k=8)
        HT_KEYS = [("hT", i) for i in range(4)] + [("hT", "h")]
        wkv = wout[:, :, :].rearrange("p j n -> p (j n)").rearrange("p (k n) -> p k n", k=8)
        wq = regB[:, 0:WQ_EL].rearrange("p (k n) -> p k n", k=8)

        def xst(s):
            return mixT[:, 4 + s, :].bitcast(F32)

        def xst_keys(s):
            return [("mix", 4 + s, tg) for tg in range(4)]

        P = Prog(nc)
        ar = Arena(regB, REGB)

        ybuf = ar.alloc([2, 2050], BF16)
        gcs = ar.alloc([2, 512], BF16)
        gch = ar.alloc([16], BF16)
        ccv = ar.alloc([2, 512], BF16)
        gbc = ar.alloc([2, 512], BF16)
        gbs = ar.alloc([2, 512], BF16)
        mT = ar.alloc([8, 256], BF16)
        assert ar.off >= WQ_EL, ar.off
        EARLY_KEYS = ([("y", jy, t) for jy in range(2) for t in (0, 1, 2, 3, "h")] + [("gcs", p) for p in range(2)]
                      + ["gch"] + [("ccv", p) for p in range(2)] + [("gbc", p) for p in range(2)] + [("gbs", p) for p in range(2)]
                      + [("mT", 0), ("mT", 1)])
        X1_OFF = ar.off
        hn = ar.alloc([4, 1024], BF16)
        sqj = ar.alloc([2, 1024], BF16)
        gmixb = ar.alloc([1024], F32)
        wsb = ar.alloc([8, 128], BF16)
        trilb = ar.alloc([128], BF16)
        LATE_END = ar.off
        LATE_KEYS = [("hn", i) for i in range(4)] + [("sqj", q) for q in range(2)] + ["gmixb", "wsb", "trilb"]
        wblk = ar.alloc([NSLOT, 8, 128], BF16)
        gvT = ar.alloc([2, 512], BF16)
        nmr = ar.alloc([2, 4], F32)
        Abf = ar.alloc([2, 4, 128], BF16)
        spt = ar.alloc([2, 512], BF16)
        gub = ar.alloc([2, 512], BF16)
        stt = ar.alloc([2, 4, 6], F32)
        mvv = ar.alloc([2, 4, 2], F32)
        rs4 = ar.alloc([2, 4], F32)
        tm4 = ar.alloc([2, 4], F32)
        ss = ar.alloc([NXS], F32)
        rs = ar.alloc([NXS], F32)
        tms = ar.alloc([NXS], F32)
        Rb = ar.alloc([8, 128], F32)
        WcT = ar.alloc([8, 128], BF16)
        NX1 = 4
        ar.off = X1_OFF
        x1 = ar.alloc([NX1, 1024], F32)
        assert ar.off <= LATE_END, (ar.off, LATE_END)

        cnt = {"c": 0}

        def cdma(q, dst, src, key):
            cnt["c"] += 1
            P.dma(q, "c%d" % cnt["c"], lambda e: e.dma_start(out=dst, in_=src), writes=[key])

        cdma("pool", ident[:], ident_d, "ident")
        P.pool(lambda e: e.memset(mhalf[:], -0.5), writes=["mhalf"])
        P.pool(lambda e: e.memset(onesb[:], 1.0), writes=["onesb"])

        def late_consts():
            cdma("sp", gmem[:], gmem_d, "gmem")
            cdma("sp", gx[:], gx_d, "gx")
            cdma("sp", convw[:], convw_d, "convw")
            cdma("sp", lng[:], lng_d, "lng")
            cdma("sp", lnb[:], lnb_d, "lnb")
            cdma("sp", Rb.rearrange("p h s -> p (h s)"), bsb_d, "Rb")

        tcount = {"n": 0}

        def norm_T_tile(src_ap, gvec, gkey, dstT, dst_lo, src_lo, n, dst_keys, gfull=None, pair_src=None, no_load=False):
            i = tcount["n"]
            tcount["n"] += 1
            s = i % NXS
            h3 = i % 4
            q2 = i % 2
            tb = 4 + (i % 4)
            pT_ = psT[tb][:, :] if tb in psT else psf[tb][:, :].bitcast(BF16)
            xs = xst(s)
            if pair_src is None and not no_load:
                P.dma("sp", "xs%d" % s, lambda e: e.dma_start(out=xs, in_=src_ap), writes=xst_keys(s))
            elif pair_src is not None:
                xs2 = (mixT[:, 4 + s:4 + s + 2, :].rearrange("p a n -> p (a n)").bitcast(F32)
                       .rearrange("p (a n) -> p a n", a=2))
                P.dma("sp", "xs%d" % s, lambda e: e.dma_start(out=xs2, in_=pair_src),
                      writes=xst_keys(s) + xst_keys(s + 1))
            P.act(lambda e: e.activation(out=sqj[:, q2, :], in_=xs, func=AF.Square, accum_out=ss[:, s:s + 1]),
                  reads=xst_keys(s), writes=[("sqj", q2), ("ss", s)])
            P.pool(lambda e: e.tensor_scalar(out=tms[:, s:s + 1], in0=ss[:, s:s + 1], scalar1=1.0 / D, scalar2=EPS,
                                             op0=ALU.mult, op1=ALU.add),
                   reads=[("ss", s)], writes=[("tms", s)])
            P.pool(lambda e: e.tensor_tensor(out=rs[:, s:s + 1], in0=tms[:, s:s + 1], in1=mhalf[:, 0:1], op=ALU.pow),
                   reads=[("tms", s), "mhalf"], writes=[("rs", s)])
            pv = pT_.rearrange("p (k t) -> p k t", k=8)

            def front2():
                if gfull is None:
                    P.dve(lambda e: e.tensor_scalar(out=hn[:, h3, :], in0=xs, scalar1=rs[:, s:s + 1], scalar2=None,
                                                    op0=ALU.mult),
                          reads=xst_keys(s) + [("rs", s)], writes=[("hn", h3)])
                else:
                    P.dve(lambda e: e.scalar_tensor_tensor(out=hn[:, h3, :], in0=xs, scalar=rs[:, s:s + 1], in1=gfull,
                                                           op0=ALU.mult, op1=ALU.mult),
                          reads=xst_keys(s) + [("rs", s), gkey], writes=[("hn", h3)])

            def mid():
                for kc in range(8):
                    P.pe(lambda e, kc=kc: e.transpose(out=pT_[:, kc * 128:(kc + 1) * 128],
                                                      in_=hn[:, h3, kc * 128:(kc + 1) * 128], identity=ident[:]),
                         reads=[("hn", h3), "ident"], writes=[("ps", tb)])

            def back():
                if gfull is None:
                    P.dve(lambda e: e.tensor_tensor(out=dstT[:, :, dst_lo:dst_lo + n], in0=pv[:, :, src_lo:src_lo + n],
                                                    in1=gvec[:, :].unsqueeze(2).broadcast_to([128, 8, n]), op=ALU.mult),
                          reads=[("ps", tb), gkey], writes=dst_keys)
                elif i % 4 in (0, 3):
                    P.act(lambda e: e.activation(out=dstT[:, :, dst_lo:dst_lo + n], in_=pv[:, :, src_lo:src_lo + n],
                                                 func=AF.Copy),
                          reads=[("ps", tb)], writes=dst_keys)
                else:
                    P.dve(lambda e: e.tensor_copy(out=dstT[:, :, dst_lo:dst_lo + n], in_=pv[:, :, src_lo:src_lo + n]),
                          reads=[("ps", tb)], writes=dst_keys)
            return front2, mid, back

        order = []
        order.append(("conv0", 0, [1024, 2048, 3072, 0]))
        for j in range(1, 8):
            order.append(("gcxa", j, [1024 + j * 128, 2048 + j * 128]))
            order.append(("zagb", j, [3072 + j * 128, j * 128]))
        for j in range(8):
            order.append(("zb", j, [6144 + j * 128]))
        for j in range(8):
            order.append(("vu", j, [5120 + j * 128, 4096 + j * 128]))
        blocks = [c for (_, _, cols) in order for c in cols]
        loaded = {"n": 0}

        def load_block_upto(n):
            while loaded["n"] < min(n, len(blocks)):
                bi = loaded["n"]
                loaded["n"] += 1
                slot = bi % NSLOT
                col = blocks[bi]
                P.dma("pool", "wb%d" % slot,
                      lambda e, slot=slot, col=col: e.dma_start(out=wblk[:, slot, :, :],
                                                                in_=w_in_d[col:col + 128, :].rearrange("p (k n) -> p k n", k=8)),
                      writes=[("wblk", slot)])

        mainb = {"n": 0}

        def main_mm(bi, tg):
            slot = bi % NSLOT
            b = mainb["n"] % 3
            mainb["n"] += 1
            for k in range(8):
                P.pe(lambda e, k=k, b=b: e.matmul(out=psf[b][:, :], lhsT=wblk[:, slot, k, :],
                                                  rhs=hT[:, k, 2 + tg * 512:2 + (tg + 1) * 512],
                                                  start=(k == 0), stop=(k == 7)),
                     reads=[("wblk", slot), ("hT", tg)], writes=[("ps", b)])
            return b

        def halo_mm(bi, off):
            slot = bi % NSLOT
            for k in range(8):
                P.pe(lambda e, k=k: e.matmul(out=psf[3][:, off:off + 2], lhsT=wblk[:, slot, k, :],
                                             rhs=hT[:, k, 0:2], start=(k == 0), stop=(k == 7)),
                     reads=[("wblk", slot), ("hT", "h")], writes=[("ps", 3)])

        vcount = {"n": 0}

        def v_step(bi, j, tg):
            par = vcount["n"] % 2
            vcount["n"] += 1
            tb = 4 + par
            mb = 6 + par
            b = main_mm(bi, tg)
            P.act(lambda e: e.activation(out=gvT[:, par, :], in_=psf[b][:, :], func=AF.Gelu_apprx_tanh),
                  reads=[("ps", b)], writes=[("gvT", par)])
            yield
            for ck in range(4):
                P.pe(lambda e, ck=ck: e.transpose(out=psT[tb][:, ck * 128:(ck + 1) * 128],
                                                  in_=gvT[:, par, ck * 128:(ck + 1) * 128], identity=ident[:]),
                     reads=[("gvT", par), "ident"], writes=[("ps", tb)])
            for ck in range(4):
                P.dve(lambda e, ck=ck: e.bn_stats(out=stt[:, par, ck, :], in_=psT[tb][:, ck * 128:(ck + 1) * 128]),
                      reads=[("ps", tb)], writes=[("stt", par, ck)])
            for ck in range(4):
                P.dve(lambda e, ck=ck: e.bn_aggr(out=mvv[:, par, ck, :], in_=stt[:, par, ck, :]),
                      reads=[("stt", par, ck)], writes=[("mvv", par, ck)])
            mvk = [("mvv", par, ck) for ck in range(4)]
            P.pool(lambda e: e.tensor_scalar(out=tm4[:, par, :], in0=mvv[:, par, :, 1], scalar1=EPS, scalar2=None,
                                             op0=ALU.add),
                   reads=mvk, writes=[("tm4", par)])
            P.pool(lambda e: e.tensor_tensor(out=rs4[:, par, :], in0=tm4[:, par, :], in1=mhalf[:, :], op=ALU.pow),
                   reads=[("tm4", par), "mhalf"], writes=[("rs4", par)])
            P.pool(lambda e: e.tensor_tensor(out=tm4[:, par, :], in0=mvv[:, par, :, 0], in1=rs4[:, par, :], op=ALU.mult),
                   reads=mvk + [("rs4", par), ("tm4", par)], writes=[("tm4", par)])
            P.pool(lambda e: e.tensor_scalar(out=nmr[:, par, :], in0=tm4[:, par, :], scalar1=-1.0, scalar2=None,
                                             op0=ALU.mult),
                   reads=[("tm4", par)], writes=[("nmr", par)])
            yield
            for ck in range(4):
                P.act(lambda e, ck=ck: e.activation(out=Abf[:, par, ck, :], in_=psT[tb][:, ck * 128:(ck + 1) * 128],
                                                    func=AF.Identity, scale=rs4[:, par, ck:ck + 1],
                                                    bias=nmr[:, par, ck:ck + 1]),
                      reads=[("ps", tb), ("rs4", par), ("nmr", par)], writes=[("Abf", par, ck)])
            yield
            for ck in range(4):
                P.pe(lambda e, ck=ck: e.matmul(out=psf[mb][:, ck * 128:(ck + 1) * 128], lhsT=Abf[:, par, ck, :],
                                               rhs=WcT[:, j, :], start=True, stop=True),
                     reads=[("Abf", par, ck), "WcT"], writes=[("ps", mb)])
            P.dve(lambda e: e.scalar_tensor_tensor(
                out=spt[:, par, :].rearrange("p (c t) -> p c t", c=4),
                in0=psf[mb][:, :].rearrange("p (c t) -> p c t", c=4), scalar=lng[:, j:j + 1],
                in1=Rb[:, j, :].unsqueeze(1).broadcast_to([128, 4, 128]), op0=ALU.mult, op1=ALU.add),
                reads=[("ps", mb), "lng", "Rb"], writes=[("spt", par)])
            P.dve(lambda e: e.tensor_tensor(out=mixT[:, 8 + j, tg * 512:(tg + 1) * 512],
                                            in0=mixT[:, 8 + j, tg * 512:(tg + 1) * 512], in1=spt[:, par, :],
                                            op=ALU.mult),
                  reads=[("spt", par), ("mix", 8 + j, tg)], writes=[("mix", 8 + j, tg)])

        def step_gcxa(bgc, bxa, j, tg, defer_halo=False):
            b = main_mm(bgc, tg)
            p2 = tg % 2
            jy = j % 2
            P.act(lambda e: e.activation(out=gcs[:, p2, :], in_=psf[b][:, :], func=AF.Copy),
                  reads=[("ps", b)], writes=[("gcs", p2)])

            def halo_part():
                halo_mm(bgc, 0)
                halo_mm(bxa, 2)
                P.act(lambda e: e.activation(out=gch[:, 0:4], in_=psf[3][:, 0:4], func=AF.Copy),
                      reads=[("ps", 3)], writes=["gch"])
                P.dve(lambda e: e.tensor_tensor(out=ybuf[:, jy, 0:2], in0=gch[:, 0:2], in1=gch[:, 2:4], op=ALU.mult),
                      reads=["gch"], writes=[("y", jy, "h")])

            b2 = main_mm(bxa, tg)
            P.dve(lambda e: e.tensor_tensor(out=ybuf[:, jy, 2 + tg * 512:2 + (tg + 1) * 512], in0=psf[b2][:, :],
                                            in1=gcs[:, p2, :], op=ALU.mult),
                  reads=[("ps", b2), ("gcs", p2)], writes=[("y", jy, tg)])
            if tg == 0:
                if defer_halo:
                    return halo_part
                halo_part()
            return None

        def step_za(bi, j, tg):
            b = main_mm(bi, tg)
            P.act(lambda e: e.activation(out=mixT[:, j, tg * 512:(tg + 1) * 512], in_=psf[b][:, :], func=AF.Silu),
                  reads=[("ps", b)], writes=[("mix", j, tg)])

        def step_gb(bi, j, tg):
            b = main_mm(bi, tg)
            jy = j % 2
            p2 = tg % 2
            P.act(lambda e: e.activation(out=gbs[:, p2, :], in_=psf[b][:, :], func=AF.Copy),
                  reads=[("ps", b)], writes=[("gbs", p2)])
            yk = [("y", jy, tg), ("y", jy, tg - 1 if tg > 0 else "h")]
            P.dve(lambda e: e.tensor_scalar(out=ccv[:, p2, :], in0=ybuf[:, jy, tg * 512:tg * 512 + 512],
                                            scalar1=convw[:, j * 3:j * 3 + 1], scalar2=None, op0=ALU.mult),
                  reads=yk + ["convw"], writes=[("ccv", p2)])
            for kk in (1, 2):
                P.dve(lambda e, kk=kk: e.scalar_tensor_tensor(
                    out=ccv[:, p2, :], in0=ybuf[:, jy, tg * 512 + kk:tg * 512 + kk + 512],
                    scalar=convw[:, j * 3 + kk:j * 3 + kk + 1], in1=ccv[:, p2, :], op0=ALU.mult, op1=ALU.add),
                    reads=yk + ["convw", ("ccv", p2)], writes=[("ccv", p2)])
            P.dve(lambda e: e.tensor_tensor(out=gbc[:, p2, :], in0=gbs[:, p2, :], in1=ccv[:, p2, :], op=ALU.mult),
                  reads=[("gbs", p2), ("ccv", p2)], writes=[("gbc", p2)])
            P.pool(lambda e: e.tensor_tensor(out=mixT[:, j, tg * 512:(tg + 1) * 512],
                                             in0=mixT[:, j, tg * 512:(tg + 1) * 512], in1=gbc[:, p2, :], op=ALU.mult),
                   reads=[("gbc", p2), ("mix", j, tg)], writes=[("mix", j, tg)])

        def step_zb(bi, j, tg):
            b = main_mm(bi, tg)
            P.act(lambda e: e.activation(out=mixT[:, 8 + j, tg * 512:(tg + 1) * 512], in_=psf[b][:, :], func=AF.Silu),
                  reads=[("ps", b)], writes=[("mix", 8 + j, tg)])

        def step_u(bi, j, tg):
            b = main_mm(bi, tg)
            p2 = tg % 2
            P.act(lambda e: e.activation(out=gub[:, p2, :], in_=psf[b][:, :], func=AF.Gelu_apprx_tanh),
                  reads=[("ps", b)], writes=[("gub", p2)])
            P.dve(lambda e: e.tensor_tensor(out=mixT[:, 8 + j, tg * 512:(tg + 1) * 512],
                                            in0=mixT[:, 8 + j, tg * 512:(tg + 1) * 512], in1=gub[:, p2, :], op=ALU.mult),
                  reads=[("gub", p2), ("mix", 8 + j, tg)], writes=[("mix", 8 + j, tg)])

        active = []

        def advance():
            for g_ in list(active):
                try:
                    next(g_)
                except StopIteration:
                    active.remove(g_)

        def kv_phase():
            for c in range(8):
                b = c % 3
                for k in range(8):
                    P.pe(lambda e, c=c, k=k, b=b: e.matmul(out=psf[b][:, 0:256], lhsT=wkv[:, k, c * 128:(c + 1) * 128],
                                                           rhs=mT[:, k, :], start=(k == 0), stop=(k == 7)),
                         reads=[("wkv", c // 4), ("mT", 0), ("mT", 1)], writes=[("ps", b)])
                P.act(lambda e, c=c, b=b: e.activation(out=kT[:, c, :], in_=psf[b][:, 0:256], func=AF.Copy),
                      reads=[("ps", b)], writes=[("kT", c)])
            for mt in range(2):
                for cg in range(2):
                    b = (mt * 2 + cg) % 3
                    for k in range(8):
                        P.pe(lambda e, mt=mt, cg=cg, k=k, b=b: e.matmul(
                            out=psf[b][:, :], lhsT=mT[:, k, mt * 128:(mt + 1) * 128],
                            rhs=wkv[:, k, 1024 + cg * 512:1024 + (cg + 1) * 512], start=(k == 0), stop=(k == 7)),
                            reads=[("wkv", 2 + cg), ("mT", mt)], writes=[("ps", b)])
                    P.act(lambda e, mt=mt, cg=cg, b=b: e.activation(out=Vt[:, mt, cg * 512:(cg + 1) * 512],
                                                                    in_=psf[b][:, :], func=AF.Copy),
                          reads=[("ps", b)], writes=[("V", mt)])

        def ws_setup(part):
            if part == 0:
                P.dve(lambda e: e.tensor_tensor(out=wsb, in0=wsb, in1=trilb.unsqueeze(1).broadcast_to([128, 8, 128]),
                                                op=ALU.mult),
                      reads=["wsb", "trilb"], writes=["wsb"])
            elif part == 1:
                for h in range(8):
                    P.pe(lambda e, h=h: e.transpose(out=psT[4][:, h * 128:(h + 1) * 128], in_=wsb[:, h, :],
                                                    identity=ident[:]),
                         reads=["wsb", "ident"], writes=[("ps", 4)])
                P.act(lambda e: e.activation(out=WcT.rearrange("p h t -> p (h t)"), in_=psT[4][:, :], func=AF.Copy),
                      reads=[("ps", 4)], writes=["WcT"])
            else:
                for h in range(8):
                    b = 6 + h // 4
                    P.pe(lambda e, h=h, b=b: e.matmul(out=psf[b][:, (h % 4) * 128:(h % 4 + 1) * 128], lhsT=onesb[:],
                                                     rhs=WcT[:, h, :], start=True, stop=True),
                         reads=["onesb", "WcT"], writes=[("ps", b)])
                for h in range(8):
                    b = 6 + h // 4
                    P.dve(lambda e, h=h, b=b: e.scalar_tensor_tensor(
                        out=Rb[:, h, :], in0=psf[b][:, (h % 4) * 128:(h % 4 + 1) * 128], scalar=lnb[:, h:h + 1],
                        in1=Rb[:, h, :], op0=ALU.mult, op1=ALU.add),
                        reads=[("ps", b), "lnb", "Rb"], writes=["Rb"])

        bi = 0
        gi = 0
        kind, j, cols = order[gi]

        def wkv_dmas(which=(0, 1, 2, 3)):
            for cgi in which:
                P.dma("pool", "wkv%d" % cgi,
                      lambda e, cgi=cgi: e.dma_start(out=wkv[:, :, cgi * 512:(cgi + 1) * 512],
                                                     in_=w_kv_v[:, :, cgi * 512:(cgi + 1) * 512]),
                      writes=[("wkv", cgi)])

        load_block_upto(5)
        P.act(lambda e: e.activation(out=sqj[:, 0, 0:1], in_=mhalf[:, 0:1], func=AF.Square), reads=["mhalf"], writes=[("sqj", 0)])

        def xfront(tt):
            return norm_T_tile(x_d[tt * 128:(tt + 1) * 128, :], None, "gmixb", hT, 2 + tt * 128, 0, 128,
                               [("hT", tt // 4)], gfull=gmixb)

        def xfront(tt):
            tcount["n"] = tt
            ps_, nl_ = None, False
            if tt >= 2:
                if tt % 2 == 0:
                    ps_ = x_d[tt * 128:(tt + 2) * 128, :].rearrange("(a p) d -> p a d", p=128)
                else:
                    nl_ = True
            return norm_T_tile(x_d[tt * 128:(tt + 1) * 128, :], None, "gmixb", hT, 2 + tt * 128, 0, 128,
                               [("hT", tt // 4)], gfull=gmixb, pair_src=ps_, no_load=nl_)

        def mb_group(g):
            g[0][1]()
            g[1][1]()
            g[0][2]()
            g[2][1]()
            g[1][2]()
            g[3][1]()
            g[2][2]()
            g[3][2]()

        grp = {0: [xfront(0)]}
        cdma("sp", gmixb, gmixb_d, "gmixb")
        grp[0] += [xfront(tt) for tt in range(1, 4)]
        for t_ in grp[0]:
            t_[0]()
        tcount["n"] = 16
        fh, mh, bh = norm_T_tile(xh_d, None, "gmixb", hT, 0, 126, 2, [("hT", "h")], gfull=gmixb)
        mb_group(grp[0])
        fh()
        mh()
        bh()
        grp[1] = [xfront(tt) for tt in range(4, 8)]
        late_consts()
        for t_ in grp[1]:
            t_[0]()
        for tg in range(4):
            g_ = grp.get(tg + 1) if tg + 1 < 4 else None
            early_f = (tg >= 1 and tg + 2 < 4)
            if early_f:
                grp[tg + 2] = [xfront(tt) for tt in range(4 * (tg + 2), 4 * (tg + 2) + 4)]
            hp = step_gcxa(bi, bi + 1, 0, tg, defer_halo=True)
            if g_ is not None:
                g_[0][1]()
                g_[1][1]()
                g_[0][2]()
            step_za(bi + 2, 0, tg)
            if hp is not None:
                hp()
            if g_ is not None:
                g_[2][1]()
                g_[1][2]()
                g_[3][1]()
                g_[2][2]()
                g_[3][2]()
            if early_f:
                for t_ in grp[tg + 2]:
                    t_[0]()
            step_gb(bi + 3, 0, tg)
            if tg + 2 < 4 and not early_f:
                grp[tg + 2] = [xfront(tt) for tt in range(4 * (tg + 2), 4 * (tg + 2) + 4)]
                for t_ in grp[tg + 2]:
                    t_[0]()
            if tg == 1:
                load_block_upto(7)
            if tg == 2:
                tcount["n"] = 17
                mem_t = [norm_T_tile(mem_d[mt * 128:(mt + 1) * 128, :], gmem, "gmem", mT, mt * 128, 0, 128,
                                     [("mT", mt)]) for mt in range(2)]
                for t_ in mem_t:
                    t_[0]()
        bi += 4
        gi += 1
        wout_at = {9: 0, 11: 1, 12: 2, 13: 3}
        wq_at = {24: 0, 26: 1}
        while gi < len(order):
            kind, j, cols = order[gi]
            load_block_upto(bi + len(cols) + PF)
            if 1 <= gi <= 4:
                wkv_dmas((gi - 1,))
            if gi == 1:
                mem_t[0][1]()
                mem_t[1][1]()
                mem_t[0][2]()
                mem_t[1][2]()
            if gi == 2:
                cdma("pool", wsb.rearrange("p h s -> p (h s)"), ws_d, "wsb")
                cdma("pool", trilb, tril_d, "trilb")
            if gi == 5:
                kv_phase()
            if gi in (6, 8, 10):
                ws_setup((gi - 6) // 2)
            if gi in wout_at:
                g = wout_at[gi]
                P.dma("pool", "wout%d" % g,
                      lambda e, g=g: e.dma_start(out=wout[:, 4 * g:4 * g + 4, :], in_=w_out_v[:, 4 * g:4 * g + 4, :]),
                      writes=[("wout", g)] + [("wkv", c) for c in range(4)])
            if gi == 17:
                o0 = P.dma("pool", "wq0", lambda e: e.dma_start(out=wq[:, 0:4, :], in_=w_q_v[:, 0:4, :]),
                           writes=EARLY_KEYS + [("wq", 0)])
                P.dma("pool", "wq1", lambda e: e.dma_start(out=wq[:, 4:8, :], in_=w_q_v[:, 4:8, :]),
                      writes=[("wq", 1)], like=o0)
            for tg in range(4):
                if kind == "gcxa":
                    step_gcxa(bi, bi + 1, j, tg)
                elif kind == "zagb":
                    step_za(bi, j, tg)
                    step_gb(bi + 1, j, tg)
                elif kind == "zb":
                    step_zb(bi, j, tg)
                elif kind == "vu":
                    gen = v_step(bi, j, tg)
                    next(gen)
                    step_u(bi + 1, j, tg)
                    advance()
                    active.append(gen)
            bi += len(cols)
            gi += 1
        NS2 = 3
        BIG = {0: (0, 1), 1: (2, 3), 2: (6, 7)}

        def s1_pe(tt, halves=(0, 1)):
            B = BIG[tt % NS2]
            r0 = tt * 128
            for cg in halves:
                for jc in range(16):
                    P.pe(lambda e, cg=cg, jc=jc: e.matmul(out=psf[B[cg]][:, :], lhsT=mixT[:, jc, r0:r0 + 128],
                                                          rhs=wout[:, jc, cg * 512:(cg + 1) * 512],
                                                          start=(jc == 0), stop=(jc == 15)),
                         reads=[("mix", jc, tt // 4), ("wout", jc // 4)], writes=[("ps", B[cg])])

        def x1_load(tt):
            xl = tt % NX1
            r0 = tt * 128
            extra = LATE_KEYS if tt < NX1 else []
            P.dma("sp", "x1s%d" % xl, lambda e: e.dma_start(out=x1[:, xl, :], in_=x_d[r0:r0 + 128, :]),
                  writes=[("x1", xl, 0), ("x1", xl, 1)] + extra)

        EARLY_S1 = 3
        for t_ in range(NX1):
            x1_load(t_)
        for t_ in range(2):
            s1_pe(t_)
            advance()
        while active:
            advance()

        P.barrier(exempt=("pe",))
        s1_pe(2)
        ar.off = LATE_END
        gfin = ar.alloc([1024], F32)
        tokb = ar.alloc([NS2, 1024], BF16)
        Tb = ar.alloc([NS2, 8, 128], BF16)
        qT = ar.alloc([NS2, 8, 128], BF16)
        Eb = ar.alloc([NS2, 2, 4, 128], BF16)
        sq2 = ar.alloc([1, 1024], BF16)
        ssa = ar.alloc([NS2], F32)
        ssb = ar.alloc([NS2], F32)
        tma = ar.alloc([NS2], F32)
        tmb = ar.alloc([NS2], F32)
        rsa = ar.alloc([NS2], F32)
        rsb = ar.alloc([NS2], F32)
        rden = ar.alloc([NS2, 4], F32)
        denv = psT5[:, 0:32].bitcast(F32)

        o0 = P.dma("pool", "wxo0", lambda e: e.dma_start(out=wxo[:, 0:4, :], in_=w_xo_v[:, 0:4, :]),
                   writes=HT_KEYS + [("wxo", 0)])
        P.dma("pool", "wxo1", lambda e: e.dma_start(out=wxo[:, 4:8, :], in_=w_xo_v[:, 4:8, :]),
              writes=[("wxo", 1)], like=o0)
        cdma("sp", gfin, gfin_d, "gfin")

        def pool_rstd(ssx, tmx, rsx, sl, tag):
            P.pool(lambda e: e.tensor_scalar(out=tmx[:, sl:sl + 1], in0=ssx[:, sl:sl + 1], scalar1=1.0 / D,
                                             scalar2=EPS, op0=ALU.mult, op1=ALU.add),
                   reads=[(tag + "ss", sl)], writes=[(tag + "tm", sl)])
            P.pool(lambda e: e.tensor_tensor(out=rsx[:, sl:sl + 1], in0=tmx[:, sl:sl + 1], in1=mhalf[:, 0:1],
                                             op=ALU.pow),
                   reads=[(tag + "tm", sl), "mhalf"], writes=[(tag + "rs", sl)])

        sqc = {"n": 0}
        def tile_gen(tt):
            sl = tt % NS2
            xl = tt % NX1
            B = BIG[sl]
            tbk = ("ps", B[0])
            pT = psf[B[0]][:, :].bitcast(BF16)
            dnk = ("ps", 5)
            r0 = tt * 128
            X1K = [("x1", xl, 0), ("x1", xl, 1)]
            for cg in range(2):
                P.dve(lambda e, cg=cg: e.tensor_tensor(out=x1[:, xl, cg * 512:(cg + 1) * 512], in0=psf[B[cg]][:, :],
                                                       in1=x1[:, xl, cg * 512:(cg + 1) * 512], op=ALU.add),
                      reads=[("ps", B[cg]), ("x1", xl, cg)], writes=[("x1", xl, cg)])
            q2 = 0
            sqc["n"] += 1
            P.act(lambda e: e.activation(out=sq2[:, q2, :], in_=x1[:, xl, :], func=AF.Square, accum_out=ssa[:, sl:sl + 1]),
                  reads=X1K, writes=[("sq2", q2), ("ass", sl)])
            pool_rstd(ssa, tma, rsa, sl, "a")
            P.dve(lambda e: e.tensor_scalar(out=tokb[:, sl, :], in0=x1[:, xl, :], scalar1=rsa[:, sl:sl + 1],
                                            scalar2=None, op0=ALU.mult),
                  reads=X1K + [("ars", sl)], writes=[("tokb", sl, 0), ("tokb", sl, 1)])
            yield
            for kc in range(8):
                P.pe(lambda e, kc=kc: e.transpose(out=pT[:, kc * 128:(kc + 1) * 128],
                                                  in_=tokb[:, sl, kc * 128:(kc + 1) * 128], identity=ident[:]),
                     reads=[("tokb", sl, kc // 4), "ident"], writes=[tbk])
            P.dve(lambda e: e.tensor_tensor(out=Tb[:, sl, :, :], in0=pT.rearrange("p (k t) -> p k t", k=8),
                                            in1=gx[:, :].unsqueeze(2).broadcast_to([128, 8, 128]), op=ALU.mult),
                  reads=[tbk, "gx"], writes=[("Tb", sl)])
            yield
            for c in range(8):
                bq = B[c // 4]
                for k in range(8):
                    P.pe(lambda e, c=c, k=k, bq=bq: e.matmul(out=psf[bq][:, (c % 4) * 128:(c % 4 + 1) * 128],
                                                             lhsT=wq[:, k, c * 128:(c + 1) * 128], rhs=Tb[:, sl, k, :],
                                                             start=(k == 0), stop=(k == 7)),
                         reads=[("wq", k // 4), ("Tb", sl)], writes=[("ps", bq)])
            for hb in range(2):
                P.act(lambda e, hb=hb: e.activation(out=qT[:, sl, 4 * hb:4 * hb + 4, :].rearrange("p c t -> p (c t)"),
                                                    in_=psf[B[hb]][:, :], func=AF.Copy),
                      reads=[("ps", B[hb])], writes=[("qT", sl, hb)])
            yield
            for mt in range(2):
                for h in range(4):
                    for kk in range(2):
                        P.pe(lambda e, mt=mt, h=h, kk=kk: e.matmul(
                            out=psf[B[mt]][:, h * 128:(h + 1) * 128], lhsT=kT[:, 2 * h + kk, mt * 128:(mt + 1) * 128],
                            rhs=qT[:, sl, 2 * h + kk, :], start=(kk == 0), stop=(kk == 1)),
                            reads=[("kT", 2 * h + kk), ("qT", sl, h // 2)], writes=[("ps", B[mt])])
                P.act(lambda e, mt=mt: e.activation(out=Eb[:, sl, mt, :, :].rearrange("p h t -> p (h t)"),
                                                    in_=psf[B[mt]][:, :], func=AF.Exp, scale=0.0625),
                      reads=[("ps", B[mt])], writes=[("E", sl, mt)])
            yield
            for h in range(4):
                for mt in range(2):
                    P.pe(lambda e, h=h, mt=mt: e.matmul(out=psf[B[h // 2]][:, (h % 2) * 256:(h % 2 + 1) * 256],
                                                        lhsT=Eb[:, sl, mt, h, :], rhs=Vt[:, mt, h * 256:(h + 1) * 256],
                                                        start=(mt == 0), stop=(mt == 1)),
                         reads=[("E", sl, mt), ("V", mt)], writes=[("ps", B[h // 2])])
                for mt in range(2):
                    P.pe(lambda e, h=h, mt=mt: e.matmul(out=denv[:, 4 * sl + h:4 * sl + h + 1], lhsT=Eb[:, sl, mt, h, :],
                                                        rhs=onesb[:, 0:1], start=(mt == 0), stop=(mt == 1)),
                         reads=[("E", sl, mt), "onesb"], writes=[dnk])
            P.dve(lambda e: e.reciprocal(out=rden[:, sl, :], in_=denv[:, 4 * sl:4 * sl + 4]),
                  reads=[dnk], writes=[("rden", sl)])
            for hb in range(2):
                P.dve(lambda e, hb=hb: e.tensor_tensor(
                    out=tokb[:, sl, hb * 512:(hb + 1) * 512].rearrange("p (h d) -> p h d", h=2),
                    in0=psf[B[hb]][:, :].rearrange("p (h d) -> p h d", h=2),
                    in1=rden[:, sl, 2 * hb:2 * hb + 2].unsqueeze(2).broadcast_to([128, 2, 256]), op=ALU.mult),
                    reads=[("ps", B[hb]), ("rden", sl)], writes=[("tokb", sl, hb)])
            yield
            for kc in range(8):
                P.pe(lambda e, kc=kc: e.transpose(out=pT[:, kc * 128:(kc + 1) * 128],
                                                  in_=tokb[:, sl, kc * 128:(kc + 1) * 128], identity=ident[:]),
                     reads=[("tokb", sl, kc // 4), "ident"], writes=[tbk])
            P.act(lambda e: e.activation(out=Tb[:, sl, :, :].rearrange("p k t -> p (k t)"), in_=pT,
                                         func=AF.Copy),
                  reads=[tbk], writes=[("Tb", sl)])
            yield
            for cg in range(2):
                for k in range(8):
                    P.pe(lambda e, cg=cg, k=k: e.matmul(out=psf[B[cg]][:, :], lhsT=Tb[:, sl, k, :],
                                                        rhs=wxo[:, k, cg * 512:(cg + 1) * 512],
                                                        start=(k == 0), stop=(k == 7)),
                         reads=[("Tb", sl), ("wxo", k // 4)], writes=[("ps", B[cg])])
                P.dve(lambda e, cg=cg: e.tensor_tensor(out=x1[:, xl, cg * 512:(cg + 1) * 512], in0=psf[B[cg]][:, :],
                                                       in1=x1[:, xl, cg * 512:(cg + 1) * 512], op=ALU.add),
                      reads=[("ps", B[cg]), ("x1", xl, cg)], writes=[("x1", xl, cg)])
            yield
            q2b = 0
            sqc["n"] += 1
            P.act(lambda e: e.activation(out=sq2[:, q2b, :], in_=x1[:, xl, :], func=AF.Square,
                                         accum_out=ssb[:, sl:sl + 1]),
                  reads=X1K, writes=[("sq2", q2b), ("bss", sl)])
            pool_rstd(ssb, tmb, rsb, sl, "b")
            P.dve(lambda e: e.scalar_tensor_tensor(out=x1[:, xl, :], in0=x1[:, xl, :], scalar=rsb[:, sl:sl + 1],
                                                   in1=gfin, op0=ALU.mult, op1=ALU.mult),
                  reads=X1K + [("brs", sl), "gfin"], writes=X1K)
            P.dma("sp", "st%d" % xl, lambda e: e.dma_start(out=out_d[r0:r0 + 128, :], in_=x1[:, xl, :]), reads=X1K)

        gens = {}

        def run_stage(t, k):
            if not (0 <= t < NT):
                return
            if k in (10, 11):
                if t >= EARLY_S1:
                    s1_pe(t, halves=(k - 10,))
                return
            if k == 1:
                gens[t] = tile_gen(t)
            try:
                next(gens[t])
            except StopIteration:
                assert k == 8, (t, k)

        for n in range(NT):
            if NX1 <= n + 1 < NT:
                x1_load(n + 1)
            for (t, k) in ((n - 2, 4), (n - 1, 2), (n, 10), (n - 2, 5), (n, 11), (n, 1), (n - 2, 6), (n - 1, 3),
                           (n - 2, 7), (n - 2, 8)):
                if n == NT - 1 and k == 8:
                    continue
                run_stage(t, k)
        A_, B_ = NT - 2, NT - 1
        for (t, k) in ((B_, 2), (A_ - 1, 8), (A_, 4), (B_, 3), (A_, 5), (B_, 4), (A_, 6), (B_, 5), (A_, 7), (B_, 6), (A_, 8),
                       (B_, 7), (B_, 8)):
            run_stage(t, k)

        P.emit(final_wait_streams=["st0", "st1", "st2", "st3"])
    return nc


_CACHE = {}


def _prep_inputs(inp):
    f = lambda a: np.ascontiguousarray(np.asarray(a, dtype=np.float32))
    x = f(inp["x"])
    mem = f(inp["mem"])
    vec8 = lambda v: f(np.asarray(v, dtype=np.float32).reshape(8, 128).T)
    common = {
        "w_in": f(np.asarray(inp["w_in"][0], dtype=np.float32).reshape(8, 128, 56, 128).transpose(2, 1, 0, 3)
                  .reshape(56 * 128, D)),
        "w_out": f(inp["w_out"][0]),
        "w_q": f(inp["w_q"][0]),
        "w_kv": f(inp["w_kv"][0]),
        "w_xo": f(inp["w_xo"][0]),
        "gx": vec8(inp["norm_x_g"][0]),
        "gmem": vec8(inp["norm_mem_g"][0]),
        "gfin": f(np.broadcast_to(np.asarray(inp["norm_final_g"], dtype=np.float32)[None, :], (128, D))),
        "gmixb": f(np.broadcast_to(np.asarray(inp["norm_mix_g"][0], dtype=np.float32)[None, :], (128, D))),
        "convw": f(np.asarray(inp["conv_w"][0], dtype=np.float32).reshape(3, 8, 128).transpose(2, 1, 0).reshape(128, 24)),
        "lng": vec8(inp["gm_ln_g"][0]),
        "lnb": vec8(inp["gm_ln_b"][0]),
        "ws": f(np.asarray(inp["gm_ws"][0], dtype=np.float32).transpose(1, 0, 2).reshape(128, 1024)),
        "bsb": f(np.broadcast_to(np.asarray(inp["gm_bs"][0], dtype=np.float32).reshape(1, 1024), (128, 1024))),
        "tril": f(np.tril(np.ones((128, 128), dtype=np.float32))),
        "ident": f(np.eye(128, dtype=np.float32)),
    }
    in_maps = []
    for c in range(NCORES):
        b, half = divmod(c, 2)
        xs = x[b, half * TOK:(half + 1) * TOK]
        xh = np.zeros((128, D), dtype=np.float32)
        if half == 1:
            xh[126:128] = x[b, TOK - 2:TOK]
        m = dict(common)
        m["x"] = f(xs)
        m["xh"] = xh
        m["mem"] = f(mem[b])
        in_maps.append(m)
    return in_maps


def kernel(**inputs):
    in_maps = _prep_inputs(inputs)
    if "nc" not in _CACHE:
        _CACHE["nc"] = build_program()
    nc = _CACHE["nc"]
    res = run_bass_kernel_spmd(nc, in_maps, core_ids=list(range(NCORES)))
    out = np.empty((4, 4096, D), dtype=np.float32)
    for c in range(NCORES):
        b, half = divmod(c, 2)
        out[b, half * TOK:(half + 1) * TOK] = np.asarray(res.results[c]["out"], dtype=np.float32)
    return out
```

```python
import numpy as np
from contextlib import ExitStack
import concourse.bass as bass
import concourse.mybir as mybir
from concourse.bass_utils import run_bass_kernel_spmd

F32 = mybir.dt.float32
BF16 = mybir.dt.bfloat16
AF = mybir.ActivationFunctionType
ALU = mybir.AluOpType

ENGS = ("pe", "act", "dve", "pool", "sp")
EPS = 1e-6
NCORES = 8
TOK = 2048
NT = 16
D = 1024


class Op:
    __slots__ = ("eng", "fn", "stream", "deps", "needs_inc", "sig", "idx")

    def __init__(self, eng, fn, stream):
        self.eng = eng
        self.fn = fn
        self.stream = stream
        self.deps = []
        self.needs_inc = False
        self.sig = None


class Prog:
    def __init__(self, nc):
        self.nc = nc
        self.ops = []
        self.last_writer = {}
        self.readers = {}
        self.last_on = {}
        self.pending = {}

    def op(self, eng, fn, reads=(), writes=(), stream=None, like=None):
        o = Op(eng, fn, stream)
        o.idx = len(self.ops)
        deps = {}
        if like is not None:
            for d in like.deps:
                deps[d.idx] = (d, "like")
        for r in reads:
            w = self.last_writer.get(r)
            if w is not None:
                deps[w.idx] = (w, "raw")
        for r in writes:
            w = self.last_writer.get(r)
            if w is not None and w.idx not in deps:
                deps[w.idx] = (w, "waw")
            for rd in self.readers.get(r, {}).values():
                if rd.idx not in deps:
                    deps[rd.idx] = (rd, "war")
        for d in self.pending.pop(eng, ()):
            if d.idx not in deps:
                deps[d.idx] = (d, "bar")
        for _, (d, kind) in sorted(deps.items()):
            same = (d.stream is None and o.stream is None and d.eng == o.eng)
            if same and o.eng == "pe":
                continue
            o.deps.append(d)
            d.needs_inc = True
        for r in reads:
            rk = eng if stream is None else ("dma", o.idx)
            self.readers.setdefault(r, {})[rk] = o
        for r in writes:
            self.last_writer[r] = o
            self.readers[r] = {}
        if stream is None:
            self.last_on[eng] = o
        self.ops.append(o)
        return o

    def barrier(self, exempt=()):
        lasts = [o for o in self.last_on.values()]
        for e in ENGS:
            if e not in exempt:
                self.pending[e] = list(lasts)

    def pe(self, fn, reads=(), writes=()):
        return self.op("pe", fn, reads, writes)

    def act(self, fn, reads=(), writes=()):
        return self.op("act", fn, reads, writes)

    def dve(self, fn, reads=(), writes=()):
        return self.op("dve", fn, reads, writes)

    def pool(self, fn, reads=(), writes=()):
        return self.op("pool", fn, reads, writes)

    def dma(self, q, stream, fn, reads=(), writes=(), like=None):
        return self.op(q, fn, reads, writes, stream=stream, like=like)

    def emit(self, final_wait_streams=()):
        nc = self.nc
        counters = {}
        semkeys = []
        for o in self.ops:
            key = ("dma", o.stream) if o.stream is not None else ("eng", o.eng)
            if o.needs_inc or o.stream is not None:
                step = 16 if o.stream is not None else 1
                counters[key] = counters.get(key, 0) + step
                o.sig = (key, counters[key])
                if key not in semkeys:
                    semkeys.append(key)
        with ExitStack() as es:
            sems = {}
            for key in semkeys:
                sems[key] = es.enter_context(nc.semaphore("s_%s_%s" % key))
            block = es.enter_context(nc.Block())
            per_eng = {e: [o for o in self.ops if o.eng == e] for e in ENGS}

            def run(engname, engobj):
                waited = {}
                for o in per_eng[engname]:
                    for d in o.deps:
                        key, val = d.sig
                        if waited.get(key, 0) >= val:
                            continue
                        engobj.wait_ge(sems[key], val)
                        waited[key] = val
                    ins = o.fn(engobj)
                    if o.sig is not None:
                        ins.then_inc(sems[o.sig[0]], 16 if o.stream is not None else 1)
                if engname == "sp":
                    for st in final_wait_streams:
                        key = ("dma", st)
                        if key in counters:
                            engobj.wait_ge(sems[key], counters[key])

            @block.tensor
            def _(e):
                run("pe", e)

            @block.scalar
            def _(e):
                run("act", e)

            @block.vector
            def _(e):
                run("dve", e)

            @block.gpsimd
            def _(e):
                run("pool", e)

            @block.sync
            def _(e):
                run("sp", e)
        self.nsems = len(semkeys)


class Arena:
    def __init__(self, base_ap, nelem):
        self.base = base_ap
        self.n = nelem
        self.off = 0

    def reset(self):
        self.off = 0

    def alloc(self, free_shape, dtype):
        n = int(np.prod(free_shape))
        nb = n * (2 if dtype == F32 else 1)
        nb = (nb + 15) // 16 * 16
        assert self.off + nb <= self.n, ("arena overflow", self.off, nb, self.n)
        ap = self.base[:, self.off:self.off + nb]
        self.off += nb
        if dtype == F32:
            ap = ap.bitcast(F32)
        ap = ap[:, 0:n]
        if len(free_shape) == 2:
            ap = ap.rearrange("p (a b) -> p a b", a=free_shape[0])
        elif len(free_shape) == 3:
            ap = ap.rearrange("p (a b c) -> p a b c", a=free_shape[0], b=free_shape[1])
        elif len(free_shape) == 4:
            ap = ap.rearrange("p (a b c d) -> p a b c d", a=free_shape[0], b=free_shape[1], c=free_shape[2])
        return ap


NSLOT = 7
PF = 3
NXS = 12
REGB = 35200
WQ_EL = 8192


def build_program():
    nc = bass.Bass("TRN2", target_bir_lowering=False)

    def din(name, shape):
        return nc.dram_tensor(name, list(shape), F32, kind="ExternalInput").ap()

    x_d = din("x", [TOK, D])
    xh_d = din("xh", [128, D])
    mem_d = din("mem", [256, D])
    w_in_d = din("w_in", [56 * 128, D])
    w_out_d = din("w_out", [2048, D])
    w_q_d = din("w_q", [D, D])
    w_kv_d = din("w_kv", [D, 2048])
    w_xo_d = din("w_xo", [D, D])
    gx_d = din("gx", [128, 8])
    gmem_d = din("gmem", [128, 8])
    gfin_d = din("gfin", [128, D])
    gmixb_d = din("gmixb", [128, D])
    convw_d = din("convw", [128, 24])
    lng_d = din("lng", [128, 8])
    lnb_d = din("lnb", [128, 8])
    ws_d = din("ws", [128, 1024])
    bsb_d = din("bsb", [128, 1024])
    tril_d = din("tril", [128, 128])
    ident_d = din("ident", [128, 128])
    out_d = nc.dram_tensor("out", [TOK, D], F32, kind="ExternalOutput").ap()

    w_out_v = w_out_d.rearrange("(k p) n -> p k n", p=128)
    w_q_v = w_q_d.rearrange("(k p) n -> p k n", p=128)
    w_kv_v = w_kv_d.rearrange("(k p) n -> p k n", p=128)
    w_xo_v = w_xo_d.rearrange("(k p) n -> p k n", p=128)

    with ExitStack() as es:
        def sb(name, shape, dt):
            return es.enter_context(nc.sbuf_tensor("sb_" + name, list(shape), dt))

        regA = sb("regA", [128, 16400], BF16)
        mixT = sb("mixT", [128, 16, 2048], BF16)
        wout = sb("wout", [128, 16, 1024], BF16)
        kT = sb("kT", [128, 8, 256], BF16)
        Vt = sb("Vt", [128, 2, 1024], BF16)
        ident = sb("ident", [128, 128], BF16)
        onesb = sb("onesb", [128, 128], BF16)
        gx = sb("gx", [128, 8], F32)
        gmem = sb("gmem", [128, 8], F32)
        convw = sb("convw", [128, 24], F32)
        lng = sb("lng", [128, 8], F32)
        lnb = sb("lnb", [128, 8], F32)
        mhalf = sb("mhalf", [128, 4], F32)
        regB = sb("regB", [128, REGB], BF16)

        ps = [es.enter_context(nc.psum_tensor("ps%d" % b, [128, 512], F32)) for b in (0, 1, 2, 3)]
        psT4 = es.enter_context(nc.psum_tensor("psT4", [128, 1024], BF16))
        psT5 = es.enter_context(nc.psum_tensor("psT5", [128, 1024], BF16))
        ps6 = es.enter_context(nc.psum_tensor("ps6", [128, 512], F32))
        ps7 = es.enter_context(nc.psum_tensor("ps7", [128, 512], F32))
        psf = {0: ps[0], 1: ps[1], 2: ps[2], 3: ps[3], 6: ps6, 7: ps7}
        psT = {4: psT4, 5: psT5}

        hT = regA[:, 0:16400].rearrange("p (k n) -> p k n", k=8)
        wxo = regA[:, 0:8192].rearrange("p (k n) -> p k n", k=8)
        HT_KEYS = [("hT", i) for i in range(4)] + [("hT", "h")]
        wkv = wout[:, :, :].rearrange("p j n -> p (j n)").rearrange("p (k n) -> p k n", k=8)
        wq = regB[:, 0:WQ_EL].rearrange("p (k n) -> p k n", k=8)

        def xst(s):
            return mixT[:, 4 + s, :].bitcast(F32)

        def xst_keys(s):
            return [("mix", 4 + s, tg) for tg in range(4)]

        P = Prog(nc)
        ar = Arena(regB, REGB)

        ybuf = ar.alloc([2, 2050], BF16)
        gcs = ar.alloc([2, 512], BF16)
        gch = ar.alloc([16], BF16)
        ccv = ar.alloc([2, 512], BF16)
        gbc = ar.alloc([2, 512], BF16)
        gbs = ar.alloc([2, 512], BF16)
        mT = ar.alloc([8, 256], BF16)
        assert ar.off >= WQ_EL, ar.off
        EARLY_KEYS = ([("y", jy, t) for jy in range(2) for t in (0, 1, 2, 3, "h")] + [("gcs", p) for p in range(2)]
                      + ["gch"] + [("ccv", p) for p in range(2)] + [("gbc", p) for p in range(2)] + [("gbs", p) for p in range(2)]
                      + [("mT", 0), ("mT", 1)])
        X1_OFF = ar.off
        hn = ar.alloc([4, 1024], BF16)
        sqj = ar.alloc([2, 1024], BF16)
        gmixb = ar.alloc([1024], F32)
        wsb = ar.alloc([8, 128], BF16)
        trilb = ar.alloc([128], BF16)
        LATE_END = ar.off
        LATE_KEYS = [("hn", i) for i in range(4)] + [("sqj", q) for q in range(2)] + ["gmixb", "wsb", "trilb"]
        wblk = ar.alloc([NSLOT, 8, 128], BF16)
        gvT = ar.alloc([2, 512], BF16)
        nmr = ar.alloc([2, 4], F32)
        Abf = ar.alloc([2, 4, 128], BF16)
        spt = ar.alloc([2, 512], BF16)
        gub = ar.alloc([2, 512], BF16)
        stt = ar.alloc([2, 4, 6], F32)
        mvv = ar.alloc([2, 4, 2], F32)
        rs4 = ar.alloc([2, 4], F32)
        tm4 = ar.alloc([2, 4], F32)
        ss = ar.alloc([NXS + 3], F32)
        rs = ar.alloc([NXS + 3], F32)
        tms = ar.alloc([NXS + 3], F32)
        Rb = ar.alloc([8, 128], F32)
        WcT = ar.alloc([8, 128], BF16)
        NX1 = 4
        ar.off = X1_OFF
        x1 = ar.alloc([NX1, 1024], F32)
        assert ar.off <= LATE_END, (ar.off, LATE_END)

        cnt = {"c": 0}

        def cdma(q, dst, src, key):
            cnt["c"] += 1
            P.dma(q, "c%d" % cnt["c"], lambda e: e.dma_start(out=dst, in_=src), writes=[key])

        cdma("pool", ident[:], ident_d, "ident")
        P.pool(lambda e: e.memset(mhalf[:], -0.5), writes=["mhalf"])
        P.pool(lambda e: e.memset(onesb[:], 1.0), writes=["onesb"])

        def late_consts():
            cdma("sp", gmem[:], gmem_d, "gmem")
            cdma("sp", gx[:], gx_d, "gx")
            cdma("sp", convw[:], convw_d, "convw")
            cdma("sp", lng[:], lng_d, "lng")
            cdma("sp", lnb[:], lnb_d, "lnb")
            cdma("sp", Rb.rearrange("p h s -> p (h s)"), bsb_d, "Rb")

        tcount = {"n": 0}

        def norm_T_tile(src_ap, gvec, gkey, dstT, dst_lo, src_lo, n, dst_keys, gfull=None):
            i = tcount["n"]
            tcount["n"] += 1
            if i < 16:
                s = i % NXS
                si = s
            else:
                s = -(i - 15)
                si = NXS + (i - 16)
            h3 = i % 4
            q2 = i % 2
            tb = 4 + (i % 4)
            pT_ = psT[tb][:, :] if tb in psT else psf[tb][:, :].bitcast(BF16)
            xs = xst(s)
            P.dma("sp", "xs%d" % si, lambda e: e.dma_start(out=xs, in_=src_ap), writes=xst_keys(s))
            P.act(lambda e: e.activation(out=sqj[:, q2, :], in_=xs, func=AF.Square, accum_out=ss[:, si:si + 1]),
                  reads=xst_keys(s), writes=[("sqj", q2), ("ss", si)])
            P.pool(lambda e: e.tensor_scalar(out=tms[:, si:si + 1], in0=ss[:, si:si + 1], scalar1=1.0 / D, scalar2=EPS,
                                             op0=ALU.mult, op1=ALU.add),
                   reads=[("ss", si)], writes=[("tms", si)])
            P.pool(lambda e: e.tensor_tensor(out=rs[:, si:si + 1], in0=tms[:, si:si + 1], in1=mhalf[:, 0:1], op=ALU.pow),
                   reads=[("tms", si), "mhalf"], writes=[("rs", si)])
            pv = pT_.rearrange("p (k t) -> p k t", k=8)

            def front2():
                if gfull is None:
                    P.dve(lambda e: e.tensor_scalar(out=hn[:, h3, :], in0=xs, scalar1=rs[:, si:si + 1], scalar2=None,
                                                    op0=ALU.mult),
                          reads=xst_keys(s) + [("rs", si)], writes=[("hn", h3)])
                else:
                    P.dve(lambda e: e.scalar_tensor_tensor(out=hn[:, h3, :], in0=xs, scalar=rs[:, si:si + 1], in1=gfull,
                                                           op0=ALU.mult, op1=ALU.mult),
                          reads=xst_keys(s) + [("rs", si), gkey], writes=[("hn", h3)])

            def mid():
                for kc in range(8):
                    P.pe(lambda e, kc=kc: e.transpose(out=pT_[:, kc * 128:(kc + 1) * 128],
                                                      in_=hn[:, h3, kc * 128:(kc + 1) * 128], identity=ident[:]),
                         reads=[("hn", h3), "ident"], writes=[("ps", tb)])

            def back():
                if gfull is None:
                    P.dve(lambda e: e.tensor_tensor(out=dstT[:, :, dst_lo:dst_lo + n], in0=pv[:, :, src_lo:src_lo + n],
                                                    in1=gvec[:, :].unsqueeze(2).broadcast_to([128, 8, n]), op=ALU.mult),
                          reads=[("ps", tb), gkey], writes=dst_keys)
                elif i % 4 in (0, 3):
                    P.act(lambda e: e.activation(out=dstT[:, :, dst_lo:dst_lo + n], in_=pv[:, :, src_lo:src_lo + n],
                                                 func=AF.Copy),
                          reads=[("ps", tb)], writes=dst_keys)
                else:
                    P.dve(lambda e: e.tensor_copy(out=dstT[:, :, dst_lo:dst_lo + n], in_=pv[:, :, src_lo:src_lo + n]),
                          reads=[("ps", tb)], writes=dst_keys)
            return front2, mid, back

        order = []
        order.append(("conv0", 0, [1024, 2048, 3072, 0]))
        for j in range(1, 8):
            order.append(("gcxa", j, [1024 + j * 128, 2048 + j * 128]))
            order.append(("zagb", j, [3072 + j * 128, j * 128]))
        for j in range(8):
            order.append(("zb", j, [6144 + j * 128]))
        for j in range(8):
            order.append(("vu", j, [5120 + j * 128, 4096 + j * 128]))
        blocks = [c for (_, _, cols) in order for c in cols]
        loaded = {"n": 0}

        def load_block_upto(n):
            while loaded["n"] < min(n, len(blocks)):
                bi = loaded["n"]
                loaded["n"] += 1
                slot = bi % NSLOT
                col = blocks[bi]
                P.dma("pool", "wb%d" % slot,
                      lambda e, slot=slot, col=col: e.dma_start(out=wblk[:, slot, :, :],
                                                                in_=w_in_d[col:col + 128, :].rearrange("p (k n) -> p k n", k=8)),
                      writes=[("wblk", slot)])

        mainb = {"n": 0}

        def main_mm(bi, tg):
            slot = bi % NSLOT
            b = mainb["n"] % 3
            mainb["n"] += 1
            for k in range(8):
                P.pe(lambda e, k=k, b=b: e.matmul(out=psf[b][:, :], lhsT=wblk[:, slot, k, :],
                                                  rhs=hT[:, k, 2 + tg * 512:2 + (tg + 1) * 512],
                                                  start=(k == 0), stop=(k == 7)),
                     reads=[("wblk", slot), ("hT", tg)], writes=[("ps", b)])
            return b

        def halo_mm(bi, off):
            slot = bi % NSLOT
            for k in range(8):
                P.pe(lambda e, k=k: e.matmul(out=psf[3][:, off:off + 2], lhsT=wblk[:, slot, k, :],
                                             rhs=hT[:, k, 0:2], start=(k == 0), stop=(k == 7)),
                     reads=[("wblk", slot), ("hT", "h")], writes=[("ps", 3)])

        vcount = {"n": 0}

        def v_step(bi, j, tg):
            par = vcount["n"] % 2
            vcount["n"] += 1
            tb = 4 + par
            mb = 6 + par
            b = main_mm(bi, tg)
            P.act(lambda e: e.activation(out=gvT[:, par, :], in_=psf[b][:, :], func=AF.Gelu_apprx_tanh),
                  reads=[("ps", b)], writes=[("gvT", par)])
            yield
            for ck in range(4):
                P.pe(lambda e, ck=ck: e.transpose(out=psT[tb][:, ck * 128:(ck + 1) * 128],
                                                  in_=gvT[:, par, ck * 128:(ck + 1) * 128], identity=ident[:]),
                     reads=[("gvT", par), "ident"], writes=[("ps", tb)])
            for ck in range(4):
                P.dve(lambda e, ck=ck: e.bn_stats(out=stt[:, par, ck, :], in_=psT[tb][:, ck * 128:(ck + 1) * 128]),
                      reads=[("ps", tb)], writes=[("stt", par, ck)])
            for ck in range(4):
                P.dve(lambda e, ck=ck: e.bn_aggr(out=mvv[:, par, ck, :], in_=stt[:, par, ck, :]),
                      reads=[("stt", par, ck)], writes=[("mvv", par, ck)])
            mvk = [("mvv", par, ck) for ck in range(4)]
            P.pool(lambda e: e.tensor_scalar(out=tm4[:, par, :], in0=mvv[:, par, :, 1], scalar1=EPS, scalar2=None,
                                             op0=ALU.add),
                   reads=mvk, writes=[("tm4", par)])
            P.pool(lambda e: e.tensor_tensor(out=rs4[:, par, :], in0=tm4[:, par, :], in1=mhalf[:, :], op=ALU.pow),
                   reads=[("tm4", par), "mhalf"], writes=[("rs4", par)])
            P.pool(lambda e: e.tensor_tensor(out=tm4[:, par, :], in0=mvv[:, par, :, 0], in1=rs4[:, par, :], op=ALU.mult),
                   reads=mvk + [("rs4", par), ("tm4", par)], writes=[("tm4", par)])
            P.pool(lambda e: e.tensor_scalar(out=nmr[:, par, :], in0=tm4[:, par, :], scalar1=-1.0, scalar2=None,
                                             op0=ALU.mult),
                   reads=[("tm4", par)], writes=[("nmr", par)])
            yield
            for ck in range(4):
                P.act(lambda e, ck=ck: e.activation(out=Abf[:, par, ck, :], in_=psT[tb][:, ck * 128:(ck + 1) * 128],
                                                    func=AF.Identity, scale=rs4[:, par, ck:ck + 1],
                                                    bias=nmr[:, par, ck:ck + 1]),
                      reads=[("ps", tb), ("rs4", par), ("nmr", par)], writes=[("Abf", par, ck)])
            yield
            for ck in range(4):
                P.pe(lambda e, ck=ck: e.matmul(out=psf[mb][:, ck * 128:(ck + 1) * 128], lhsT=Abf[:, par, ck, :],
                                               rhs=WcT[:, j, :], start=True, stop=True),
                     reads=[("Abf", par, ck), "WcT"], writes=[("ps", mb)])
            P.dve(lambda e: e.scalar_tensor_tensor(
                out=spt[:, par, :].rearrange("p (c t) -> p c t", c=4),
                in0=psf[mb][:, :].rearrange("p (c t) -> p c t", c=4), scalar=lng[:, j:j + 1],
                in1=Rb[:, j, :].unsqueeze(1).broadcast_to([128, 4, 128]), op0=ALU.mult, op1=ALU.add),
                reads=[("ps", mb), "lng", "Rb"], writes=[("spt", par)])
            P.dve(lambda e: e.tensor_tensor(out=mixT[:, 8 + j, tg * 512:(tg + 1) * 512],
                                            in0=mixT[:, 8 + j, tg * 512:(tg + 1) * 512], in1=spt[:, par, :],
                                            op=ALU.mult),
                  reads=[("spt", par), ("mix", 8 + j, tg)], writes=[("mix", 8 + j, tg)])

        def step_gcxa(bgc, bxa, j, tg, defer_halo=False):
            b = main_mm(bgc, tg)
            p2 = tg % 2
            jy = j % 2
            P.act(lambda e: e.activation(out=gcs[:, p2, :], in_=psf[b][:, :], func=AF.Copy),
                  reads=[("ps", b)], writes=[("gcs", p2)])

            def halo_part():
                halo_mm(bgc, 0)
                halo_mm(bxa, 2)
                P.act(lambda e: e.activation(out=gch[:, 0:4], in_=psf[3][:, 0:4], func=AF.Copy),
                      reads=[("ps", 3)], writes=["gch"])
                P.dve(lambda e: e.tensor_tensor(out=ybuf[:, jy, 0:2], in0=gch[:, 0:2], in1=gch[:, 2:4], op=ALU.mult),
                      reads=["gch"], writes=[("y", jy, "h")])

            b2 = main_mm(bxa, tg)
            P.dve(lambda e: e.tensor_tensor(out=ybuf[:, jy, 2 + tg * 512:2 + (tg + 1) * 512], in0=psf[b2][:, :],
                                            in1=gcs[:, p2, :], op=ALU.mult),
                  reads=[("ps", b2), ("gcs", p2)], writes=[("y", jy, tg)])
            if tg == 0:
                if defer_halo:
                    return halo_part
                halo_part()
            return None

        def step_za(bi, j, tg):
            b = main_mm(bi, tg)
            P.act(lambda e: e.activation(out=mixT[:, j, tg * 512:(tg + 1) * 512], in_=psf[b][:, :], func=AF.Silu),
                  reads=[("ps", b)], writes=[("mix", j, tg)])

        def step_gb(bi, j, tg):
            b = main_mm(bi, tg)
            jy = j % 2
            p2 = tg % 2
            P.act(lambda e: e.activation(out=gbs[:, p2, :], in_=psf[b][:, :], func=AF.Copy),
                  reads=[("ps", b)], writes=[("gbs", p2)])
            yk = [("y", jy, tg), ("y", jy, tg - 1 if tg > 0 else "h")]
            P.dve(lambda e: e.tensor_scalar(out=ccv[:, p2, :], in0=ybuf[:, jy, tg * 512:tg * 512 + 512],
                                            scalar1=convw[:, j * 3:j * 3 + 1], scalar2=None, op0=ALU.mult),
                  reads=yk + ["convw"], writes=[("ccv", p2)])
            for kk in (1, 2):
                P.dve(lambda e, kk=kk: e.scalar_tensor_tensor(
                    out=ccv[:, p2, :], in0=ybuf[:, jy, tg * 512 + kk:tg * 512 + kk + 512],
                    scalar=convw[:, j * 3 + kk:j * 3 + kk + 1], in1=ccv[:, p2, :], op0=ALU.mult, op1=ALU.add),
                    reads=yk + ["convw", ("ccv", p2)], writes=[("ccv", p2)])
            P.dve(lambda e: e.tensor_tensor(out=gbc[:, p2, :], in0=gbs[:, p2, :], in1=ccv[:, p2, :], op=ALU.mult),
                  reads=[("gbs", p2), ("ccv", p2)], writes=[("gbc", p2)])
            P.pool(lambda e: e.tensor_tensor(out=mixT[:, j, tg * 512:(tg + 1) * 512],
                                             in0=mixT[:, j, tg * 512:(tg + 1) * 512], in1=gbc[:, p2, :], op=ALU.mult),
                   reads=[("gbc", p2), ("mix", j, tg)], writes=[("mix", j, tg)])

        def step_zb(bi, j, tg):
            b = main_mm(bi, tg)
            P.act(lambda e: e.activation(out=mixT[:, 8 + j, tg * 512:(tg + 1) * 512], in_=psf[b][:, :], func=AF.Silu),
                  reads=[("ps", b)], writes=[("mix", 8 + j, tg)])

        def step_u(bi, j, tg):
            b = main_mm(bi, tg)
            p2 = tg % 2
            P.act(lambda e: e.activation(out=gub[:, p2, :], in_=psf[b][:, :], func=AF.Gelu_apprx_tanh),
                  reads=[("ps", b)], writes=[("gub", p2)])
            P.dve(lambda e: e.tensor_tensor(out=mixT[:, 8 + j, tg * 512:(tg + 1) * 512],
                                            in0=mixT[:, 8 + j, tg * 512:(tg + 1) * 512], in1=gub[:, p2, :], op=ALU.mult),
                  reads=[("gub", p2), ("mix", 8 + j, tg)], writes=[("mix", 8 + j, tg)])

        active = []

        def advance():
            for g_ in list(active):
                try:
                    next(g_)
                except StopIteration:
                    active.remove(g_)

        def kv_phase():
            for c in range(8):
                b = c % 3
                for k in range(8):
                    P.pe(lambda e, c=c, k=k, b=b: e.matmul(out=psf[b][:, 0:256], lhsT=wkv[:, k, c * 128:(c + 1) * 128],
                                                           rhs=mT[:, k, :], start=(k == 0), stop=(k == 7)),
                         reads=[("wkv", c // 4), ("mT", 0), ("mT", 1)], writes=[("ps", b)])
                P.act(lambda e, c=c, b=b: e.activation(out=kT[:, c, :], in_=psf[b][:, 0:256], func=AF.Copy),
                      reads=[("ps", b)], writes=[("kT", c)])
            for mt in range(2):
                for cg in range(2):
                    b = (mt * 2 + cg) % 3
                    for k in range(8):
                        P.pe(lambda e, mt=mt, cg=cg, k=k, b=b: e.matmul(
                            out=psf[b][:, :], lhsT=mT[:, k, mt * 128:(mt + 1) * 128],
                            rhs=wkv[:, k, 1024 + cg * 512:1024 + (cg + 1) * 512], start=(k == 0), stop=(k == 7)),
                            reads=[("wkv", 2 + cg), ("mT", mt)], writes=[("ps", b)])
                    P.act(lambda e, mt=mt, cg=cg, b=b: e.activation(out=Vt[:, mt, cg * 512:(cg + 1) * 512],
                                                                    in_=psf[b][:, :], func=AF.Copy),
                          reads=[("ps", b)], writes=[("V", mt)])

        def ws_setup(part):
            if part == 0:
                P.dve(lambda e: e.tensor_tensor(out=wsb, in0=wsb, in1=trilb.unsqueeze(1).broadcast_to([128, 8, 128]),
                                                op=ALU.mult),
                      reads=["wsb", "trilb"], writes=["wsb"])
            elif part == 1:
                for h in range(8):
                    P.pe(lambda e, h=h: e.transpose(out=psT[4][:, h * 128:(h + 1) * 128], in_=wsb[:, h, :],
                                                    identity=ident[:]),
                         reads=["wsb", "ident"], writes=[("ps", 4)])
                P.act(lambda e: e.activation(out=WcT.rearrange("p h t -> p (h t)"), in_=psT[4][:, :], func=AF.Copy),
                      reads=[("ps", 4)], writes=["WcT"])
            else:
                for h in range(8):
                    b = 6 + h // 4
                    P.pe(lambda e, h=h, b=b: e.matmul(out=psf[b][:, (h % 4) * 128:(h % 4 + 1) * 128], lhsT=onesb[:],
                                                     rhs=WcT[:, h, :], start=True, stop=True),
                         reads=["onesb", "WcT"], writes=[("ps", b)])
                for h in range(8):
                    b = 6 + h // 4
                    P.dve(lambda e, h=h, b=b: e.scalar_tensor_tensor(
                        out=Rb[:, h, :], in0=psf[b][:, (h % 4) * 128:(h % 4 + 1) * 128], scalar=lnb[:, h:h + 1],
                        in1=Rb[:, h, :], op0=ALU.mult, op1=ALU.add),
                        reads=[("ps", b), "lnb", "Rb"], writes=["Rb"])

        bi = 0
        gi = 0
        kind, j, cols = order[gi]

        def wkv_dmas(which=(0, 1, 2, 3)):
            for cgi in which:
                P.dma("pool", "wkv%d" % cgi,
                      lambda e, cgi=cgi: e.dma_start(out=wkv[:, :, cgi * 512:(cgi + 1) * 512],
                                                     in_=w_kv_v[:, :, cgi * 512:(cgi + 1) * 512]),
                      writes=[("wkv", cgi)])

        load_block_upto(5)
        P.act(lambda e: e.activation(out=sqj[:, 0, 0:1], in_=mhalf[:, 0:1], func=AF.Square), reads=["mhalf"], writes=[("sqj", 0)])

        def xfront(tt):
            return norm_T_tile(x_d[tt * 128:(tt + 1) * 128, :], None, "gmixb", hT, 2 + tt * 128, 0, 128,
                               [("hT", tt // 4)], gfull=gmixb)

        def xfront(tt):
            tcount["n"] = tt
            return norm_T_tile(x_d[tt * 128:(tt + 1) * 128, :], None, "gmixb", hT, 2 + tt * 128, 0, 128,
                               [("hT", tt // 4)], gfull=gmixb)

        def mb_group(g):
            g[0][1]()
            g[1][1]()
            g[0][2]()
            g[2][1]()
            g[1][2]()
            g[3][1]()
            g[2][2]()
            g[3][2]()

        grp = {0: [xfront(0)]}
        cdma("sp", gmixb, gmixb_d, "gmixb")
        grp[0] += [xfront(tt) for tt in range(1, 4)]
        for t_ in grp[0]:
            t_[0]()
        tcount["n"] = 16
        fh, mh, bh = norm_T_tile(xh_d, None, "gmixb", hT, 0, 126, 2, [("hT", "h")], gfull=gmixb)
        mb_group(grp[0])
        fh()
        mh()
        bh()
        grp[1] = [xfront(tt) for tt in range(4, 8)]
        late_consts()
        for t_ in grp[1]:
            t_[0]()
        for tg in range(4):
            g_ = grp.get(tg + 1) if tg + 1 < 4 else None
            early_f = (tg >= 1 and tg + 2 < 4)
            if early_f:
                grp[tg + 2] = [xfront(tt) for tt in range(4 * (tg + 2), 4 * (tg + 2) + 4)]
            hp = step_gcxa(bi, bi + 1, 0, tg, defer_halo=True)
            if g_ is not None:
                g_[0][1]()
                g_[1][1]()
                g_[0][2]()
            step_za(bi + 2, 0, tg)
            if hp is not None:
                hp()
            if g_ is not None:
                g_[2][1]()
                g_[1][2]()
                g_[3][1]()
                g_[2][2]()
                g_[3][2]()
            if early_f:
                for t_ in grp[tg + 2]:
                    t_[0]()
            step_gb(bi + 3, 0, tg)
            if tg + 2 < 4 and not early_f:
                grp[tg + 2] = [xfront(tt) for tt in range(4 * (tg + 2), 4 * (tg + 2) + 4)]
                for t_ in grp[tg + 2]:
                    t_[0]()
            if tg == 1:
                load_block_upto(7)
            if tg == 2:
                tcount["n"] = 17
                mem_t = [norm_T_tile(mem_d[mt * 128:(mt + 1) * 128, :], gmem, "gmem", mT, mt * 128, 0, 128,
                                     [("mT", mt)]) for mt in range(2)]
                for t_ in mem_t:
                    t_[0]()
        bi += 4
        gi += 1
        wout_at = {9: 0, 11: 1, 12: 2, 13: 3}
        wq_at = {24: 0, 26: 1}
        while gi < len(order):
            kind, j, cols = order[gi]
            load_block_upto(bi + len(cols) + PF)
            if 1 <= gi <= 4:
                wkv_dmas((gi - 1,))
            if gi == 1:
                mem_t[0][1]()
                mem_t[1][1]()
                mem_t[0][2]()
                mem_t[1][2]()
            if gi == 2:
                cdma("pool", wsb.rearrange("p h s -> p (h s)"), ws_d, "wsb")
                cdma("pool", trilb, tril_d, "trilb")
            if gi == 5:
                kv_phase()
            if gi in (6, 8, 10):
                ws_setup((gi - 6) // 2)
            if gi in wout_at:
                g = wout_at[gi]
                P.dma("pool", "wout%d" % g,
                      lambda e, g=g: e.dma_start(out=wout[:, 4 * g:4 * g + 4, :], in_=w_out_v[:, 4 * g:4 * g + 4, :]),
                      writes=[("wout", g)] + [("wkv", c) for c in range(4)])
            if gi == 17:
                o0 = P.dma("pool", "wq0", lambda e: e.dma_start(out=wq[:, 0:4, :], in_=w_q_v[:, 0:4, :]),
                           writes=EARLY_KEYS + [("wq", 0)])
                P.dma("pool", "wq1", lambda e: e.dma_start(out=wq[:, 4:8, :], in_=w_q_v[:, 4:8, :]),
                      writes=[("wq", 1)], like=o0)
            for tg in range(4):
                if kind == "gcxa":
                    step_gcxa(bi, bi + 1, j, tg)
                elif kind == "zagb":
                    step_za(bi, j, tg)
                    step_gb(bi + 1, j, tg)
                elif kind == "zb":
                    step_zb(bi, j, tg)
                elif kind == "vu":
                    gen = v_step(bi, j, tg)
                    next(gen)
                    step_u(bi + 1, j, tg)
                    advance()
                    active.append(gen)
            bi += len(cols)
            gi += 1
        NS2 = 3
        BIG = {0: (0, 1), 1: (2, 3), 2: (6, 7)}

        def s1_pe(tt, halves=(0, 1)):
            B = BIG[tt % NS2]
            r0 = tt * 128
            for cg in halves:
                for jc in range(16):
                    P.pe(lambda e, cg=cg, jc=jc: e.matmul(out=psf[B[cg]][:, :], lhsT=mixT[:, jc, r0:r0 + 128],
                                                          rhs=wout[:, jc, cg * 512:(cg + 1) * 512],
                                                          start=(jc == 0), stop=(jc == 15)),
                         reads=[("mix", jc, tt // 4), ("wout", jc // 4)], writes=[("ps", B[cg])])

        def x1_load(tt):
            xl = tt % NX1
            r0 = tt * 128
            extra = LATE_KEYS if tt < NX1 else []
            P.dma("sp", "x1s%d" % xl, lambda e: e.dma_start(out=x1[:, xl, :], in_=x_d[r0:r0 + 128, :]),
                  writes=[("x1", xl, 0), ("x1", xl, 1)] + extra)

        EARLY_S1 = 3
        for t_ in range(NX1):
            x1_load(t_)
        for t_ in range(2):
            s1_pe(t_)
            advance()
        while active:
            advance()

        P.barrier(exempt=("pe",))
        s1_pe(2)
        ar.off = LATE_END
        gfin = ar.alloc([1024], F32)
        tokb = ar.alloc([NS2, 1024], BF16)
        Tb = ar.alloc([NS2, 8, 128], BF16)
        qT = ar.alloc([NS2, 8, 128], BF16)
        Eb = ar.alloc([NS2, 2, 4, 128], BF16)
        sq2 = ar.alloc([1, 1024], BF16)
        ssa = ar.alloc([NS2], F32)
        ssb = ar.alloc([NS2], F32)
        tma = ar.alloc([NS2], F32)
        tmb = ar.alloc([NS2], F32)
        rsa = ar.alloc([NS2], F32)
        rsb = ar.alloc([NS2], F32)
        rden = ar.alloc([NS2, 4], F32)
        denv = psT5[:, 0:32].bitcast(F32)

        o0 = P.dma("pool", "wxo0", lambda e: e.dma_start(out=wxo[:, 0:4, :], in_=w_xo_v[:, 0:4, :]),
                   writes=HT_KEYS + [("wxo", 0)])
        P.dma("pool", "wxo1", lambda e: e.dma_start(out=wxo[:, 4:8, :], in_=w_xo_v[:, 4:8, :]),
              writes=[("wxo", 1)], like=o0)
        cdma("sp", gfin, gfin_d, "gfin")

        def pool_rstd(ssx, tmx, rsx, sl, tag):
            P.pool(lambda e: e.tensor_scalar(out=tmx[:, sl:sl + 1], in0=ssx[:, sl:sl + 1], scalar1=1.0 / D,
                                             scalar2=EPS, op0=ALU.mult, op1=ALU.add),
                   reads=[(tag + "ss", sl)], writes=[(tag + "tm", sl)])
            P.pool(lambda e: e.tensor_tensor(out=rsx[:, sl:sl + 1], in0=tmx[:, sl:sl + 1], in1=mhalf[:, 0:1],
                                             op=ALU.pow),
                   reads=[(tag + "tm", sl), "mhalf"], writes=[(tag + "rs", sl)])

        sqc = {"n": 0}
        def tile_gen(tt):
            sl = tt % NS2
            xl = tt % NX1
            B = BIG[sl]
            tbk = ("ps", B[0])
            pT = psf[B[0]][:, :].bitcast(BF16)
            dnk = ("ps", 5)
            r0 = tt * 128
            X1K = [("x1", xl, 0), ("x1", xl, 1)]
            for cg in range(2):
                P.dve(lambda e, cg=cg: e.tensor_tensor(out=x1[:, xl, cg * 512:(cg + 1) * 512], in0=psf[B[cg]][:, :],
                                                       in1=x1[:, xl, cg * 512:(cg + 1) * 512], op=ALU.add),
                      reads=[("ps", B[cg]), ("x1", xl, cg)], writes=[("x1", xl, cg)])
            q2 = 0
            sqc["n"] += 1
            P.act(lambda e: e.activation(out=sq2[:, q2, :], in_=x1[:, xl, :], func=AF.Square, accum_out=ssa[:, sl:sl + 1]),
                  reads=X1K, writes=[("sq2", q2), ("ass", sl)])
            pool_rstd(ssa, tma, rsa, sl, "a")
            P.dve(lambda e: e.tensor_scalar(out=tokb[:, sl, :], in0=x1[:, xl, :], scalar1=rsa[:, sl:sl + 1],
                                            scalar2=None, op0=ALU.mult),
                  reads=X1K + [("ars", sl)], writes=[("tokb", sl, 0), ("tokb", sl, 1)])
            yield
            for kc in range(8):
                P.pe(lambda e, kc=kc: e.transpose(out=pT[:, kc * 128:(kc + 1) * 128],
                                                  in_=tokb[:, sl, kc * 128:(kc + 1) * 128], identity=ident[:]),
                     reads=[("tokb", sl, kc // 4), "ident"], writes=[tbk])
            P.dve(lambda e: e.tensor_tensor(out=Tb[:, sl, :, :], in0=pT.rearrange("p (k t) -> p k t", k=8),
                                            in1=gx[:, :].unsqueeze(2).broadcast_to([128, 8, 128]), op=ALU.mult),
                  reads=[tbk, "gx"], writes=[("Tb", sl)])
            yield
            for c in range(8):
                bq = B[c // 4]
                for k in range(8):
                    P.pe(lambda e, c=c, k=k, bq=bq: e.matmul(out=psf[bq][:, (c % 4) * 128:(c % 4 + 1) * 128],
                                                             lhsT=wq[:, k, c * 128:(c + 1) * 128], rhs=Tb[:, sl, k, :],
                                                             start=(k == 0), stop=(k == 7)),
                         reads=[("wq", k // 4), ("Tb", sl)], writes=[("ps", bq)])
            for hb in range(2):
                P.act(lambda e, hb=hb: e.activation(out=qT[:, sl, 4 * hb:4 * hb + 4, :].rearrange("p c t -> p (c t)"),
                                                    in_=psf[B[hb]][:, :], func=AF.Copy),
                      reads=[("ps", B[hb])], writes=[("qT", sl, hb)])
            yield
            for mt in range(2):
                for h in range(4):
                    for kk in range(2):
                        P.pe(lambda e, mt=mt, h=h, kk=kk: e.matmul(
                            out=psf[B[mt]][:, h * 128:(h + 1) * 128], lhsT=kT[:, 2 * h + kk, mt * 128:(mt + 1) * 128],
                            rhs=qT[:, sl, 2 * h + kk, :], start=(kk == 0), stop=(kk == 1)),
                            reads=[("kT", 2 * h + kk), ("qT", sl, h // 2)], writes=[("ps", B[mt])])
                P.act(lambda e, mt=mt: e.activation(out=Eb[:, sl, mt, :, :].rearrange("p h t -> p (h t)"),
                                                    in_=psf[B[mt]][:, :], func=AF.Exp, scale=0.0625),
                      reads=[("ps", B[mt])], writes=[("E", sl, mt)])
            yield
            for h in range(4):
                for mt in range(2):
                    P.pe(lambda e, h=h, mt=mt: e.matmul(out=psf[B[h // 2]][:, (h % 2) * 256:(h % 2 + 1) * 256],
                                                        lhsT=Eb[:, sl, mt, h, :], rhs=Vt[:, mt, h * 256:(h + 1) * 256],
                                                        start=(mt == 0), stop=(mt == 1)),
                         reads=[("E", sl, mt), ("V", mt)], writes=[("ps", B[h // 2])])
                for mt in range(2):
                    P.pe(lambda e, h=h, mt=mt: e.matmul(out=denv[:, 4 * sl + h:4 * sl + h + 1], lhsT=Eb[:, sl, mt, h, :],
                                                        rhs=onesb[:, 0:1], start=(mt == 0), stop=(mt == 1)),
                         reads=[("E", sl, mt), "onesb"], writes=[dnk])
            P.dve(lambda e: e.reciprocal(out=rden[:, sl, :], in_=denv[:, 4 * sl:4 * sl + 4]),
                  reads=[dnk], writes=[("rden", sl)])
            for hb in range(2):
                P.dve(lambda e, hb=hb: e.tensor_tensor(
                    out=tokb[:, sl, hb * 512:(hb + 1) * 512].rearrange("p (h d) -> p h d", h=2),
                    in0=psf[B[hb]][:, :].rearrange("p (h d) -> p h d", h=2),
                    in1=rden[:, sl, 2 * hb:2 * hb + 2].unsqueeze(2).broadcast_to([128, 2, 256]), op=ALU.mult),
                    reads=[("ps", B[hb]), ("rden", sl)], writes=[("tokb", sl, hb)])
            yield
            for kc in range(8):
                P.pe(lambda e, kc=kc: e.transpose(out=pT[:, kc * 128:(kc + 1) * 128],
                                                  in_=tokb[:, sl, kc * 128:(kc + 1) * 128], identity=ident[:]),
                     reads=[("tokb", sl, kc // 4), "ident"], writes=[tbk])
            P.act(lambda e: e.activation(out=Tb[:, sl, :, :].rearrange("p k t -> p (k t)"), in_=pT,
                                         func=AF.Copy),
                  reads=[tbk], writes=[("Tb", sl)])
            yield
            for cg in range(2):
                for k in range(8):
                    P.pe(lambda e, cg=cg, k=k: e.matmul(out=psf[B[cg]][:, :], lhsT=Tb[:, sl, k, :],
                                                        rhs=wxo[:, k, cg * 512:(cg + 1) * 512],
                                                        start=(k == 0), stop=(k == 7)),
                         reads=[("Tb", sl), ("wxo", k // 4)], writes=[("ps", B[cg])])
                P.dve(lambda e, cg=cg: e.tensor_tensor(out=x1[:, xl, cg * 512:(cg + 1) * 512], in0=psf[B[cg]][:, :],
                                                       in1=x1[:, xl, cg * 512:(cg + 1) * 512], op=ALU.add),
                      reads=[("ps", B[cg]), ("x1", xl, cg)], writes=[("x1", xl, cg)])
            yield
            q2b = 0
            sqc["n"] += 1
            P.act(lambda e: e.activation(out=sq2[:, q2b, :], in_=x1[:, xl, :], func=AF.Square,
                                         accum_out=ssb[:, sl:sl + 1]),
                  reads=X1K, writes=[("sq2", q2b), ("bss", sl)])
            pool_rstd(ssb, tmb, rsb, sl, "b")
            P.dve(lambda e: e.scalar_tensor_tensor(out=x1[:, xl, :], in0=x1[:, xl, :], scalar=rsb[:, sl:sl + 1],
                                                   in1=gfin, op0=ALU.mult, op1=ALU.mult),
                  reads=X1K + [("brs", sl), "gfin"], writes=X1K)
            P.dma("sp", "st%d" % xl, lambda e: e.dma_start(out=out_d[r0:r0 + 128, :], in_=x1[:, xl, :]), reads=X1K)

        gens = {}

        def run_stage(t, k):
            if not (0 <= t < NT):
                return
            if k in (10, 11):
                if t >= EARLY_S1:
                    s1_pe(t, halves=(k - 10,))
                return
            if k == 1:
                gens[t] = tile_gen(t)
            try:
                next(gens[t])
            except StopIteration:
                assert k == 8, (t, k)

        for n in range(NT):
            if NX1 <= n + 1 < NT:
                x1_load(n + 1)
            for (t, k) in ((n - 2, 4), (n - 1, 2), (n, 10), (n - 2, 5), (n, 11), (n, 1), (n - 2, 6), (n - 1, 3),
                           (n - 2, 7), (n - 2, 8)):
                if n == NT - 1 and k == 8:
                    continue
                run_stage(t, k)
        A_, B_ = NT - 2, NT - 1
        for (t, k) in ((B_, 2), (A_ - 1, 8), (A_, 4), (B_, 3), (A_, 5), (B_, 4), (A_, 6), (B_, 5), (A_, 7), (B_, 6), (A_, 8),
                       (B_, 7), (B_, 8)):
            run_stage(t, k)

        P.emit(final_wait_streams=["st0", "st1", "st2", "st3"])
    return nc


_CACHE = {}


def _prep_inputs(inp):
    f = lambda a: np.ascontiguousarray(np.asarray(a, dtype=np.float32))
    x = f(inp["x"])
    mem = f(inp["mem"])
    vec8 = lambda v: f(np.asarray(v, dtype=np.float32).reshape(8, 128).T)
    common = {
        "w_in": f(np.asarray(inp["w_in"][0], dtype=np.float32).reshape(8, 128, 56, 128).transpose(2, 1, 0, 3)
                  .reshape(56 * 128, D)),
        "w_out": f(inp["w_out"][0]),
        "w_q": f(inp["w_q"][0]),
        "w_kv": f(inp["w_kv"][0]),
        "w_xo": f(inp["w_xo"][0]),
        "gx": vec8(inp["norm_x_g"][0]),
        "gmem": vec8(inp["norm_mem_g"][0]),
        "gfin": f(np.broadcast_to(np.asarray(inp["norm_final_g"], dtype=np.float32)[None, :], (128, D))),
        "gmixb": f(np.broadcast_to(np.asarray(inp["norm_mix_g"][0], dtype=np.float32)[None, :], (128, D))),
        "convw": f(np.asarray(inp["conv_w"][0], dtype=np.float32).reshape(3, 8, 128).transpose(2, 1, 0).reshape(128, 24)),
        "lng": vec8(inp["gm_ln_g"][0]),
        "lnb": vec8(inp["gm_ln_b"][0]),
        "ws": f(np.asarray(inp["gm_ws"][0], dtype=np.float32).transpose(1, 0, 2).reshape(128, 1024)),
        "bsb": f(np.broadcast_to(np.asarray(inp["gm_bs"][0], dtype=np.float32).reshape(1, 1024), (128, 1024))),
        "tril": f(np.tril(np.ones((128, 128), dtype=np.float32))),
        "ident": f(np.eye(128, dtype=np.float32)),
    }
    in_maps = []
    for c in range(NCORES):
        b, half = divmod(c, 2)
        xs = x[b, half * TOK:(half + 1) * TOK]
        xh = np.zeros((128, D), dtype=np.float32)
        if half == 1:
            xh[126:128] = x[b, TOK - 2:TOK]
        m = dict(common)
        m["x"] = f(xs)
        m["xh"] = xh
        m["mem"] = f(mem[b])
        in_maps.append(m)
    return in_maps


def kernel(**inputs):
    in_maps = _prep_inputs(inputs)
    if "nc" not in _CACHE:
        _CACHE["nc"] = build_program()
    nc = _CACHE["nc"]
    res = run_bass_kernel_spmd(nc, in_maps, core_ids=list(range(NCORES)))
    out = np.empty((4, 4096, D), dtype=np.float32)
    for c in range(NCORES):
        b, half = divmod(c, 2)
        out[b, half * TOK:(half + 1) * TOK] = np.asarray(res.results[c]["out"], dtype=np.float32)
    return out
```

```python
import numpy as np
from contextlib import ExitStack
import concourse.bass as bass
import concourse.mybir as mybir
from concourse.bass_utils import run_bass_kernel_spmd

F32 = mybir.dt.float32
BF16 = mybir.dt.bfloat16
AF = mybir.ActivationFunctionType
ALU = mybir.AluOpType

ENGS = ("pe", "act", "dve", "pool", "sp")
EPS = 1e-6
NCORES = 8
TOK = 2048
NT = 16
D = 1024


class Op:
    __slots__ = ("eng", "fn", "stream", "deps", "needs_inc", "sig", "idx")

    def __init__(self, eng, fn, stream):
        self.eng = eng
        self.fn = fn
        self.stream = stream
        self.deps = []
        self.needs_inc = False
        self.sig = None


class Prog:
    def __init__(self, nc):
        self.nc = nc
        self.ops = []
        self.last_writer = {}
        self.readers = {}
        self.last_on = {}
        self.pending = {}

    def op(self, eng, fn, reads=(), writes=(), stream=None, like=None):
        o = Op(eng, fn, stream)
        o.idx = len(self.ops)
        deps = {}
        if like is not None:
            for d in like.deps:
                deps[d.idx] = (d, "like")
        for r in reads:
            w = self.last_writer.get(r)
            if w is not None:
                deps[w.idx] = (w, "raw")
        for r in writes:
            w = self.last_writer.get(r)
            if w is not None and w.idx not in deps:
                deps[w.idx] = (w, "waw")
            for rd in self.readers.get(r, {}).values():
                if rd.idx not in deps:
                    deps[rd.idx] = (rd, "war")
        for d in self.pending.pop(eng, ()):
            if d.idx not in deps:
                deps[d.idx] = (d, "bar")
        for _, (d, kind) in sorted(deps.items()):
            same = (d.stream is None and o.stream is None and d.eng == o.eng)
            if same and o.eng == "pe":
                continue
            o.deps.append(d)
            d.needs_inc = True
        for r in reads:
            rk = eng if stream is None else ("dma", o.idx)
            self.readers.setdefault(r, {})[rk] = o
        for r in writes:
            self.last_writer[r] = o
            self.readers[r] = {}
        if stream is None:
            self.last_on[eng] = o
        self.ops.append(o)
        return o

    def barrier(self, exempt=()):
        lasts = [o for o in self.last_on.values()]
        for e in ENGS:
            if e not in exempt:
                self.pending[e] = list(lasts)

    def pe(self, fn, reads=(), writes=()):
        return self.op("pe", fn, reads, writes)

    def act(self, fn, reads=(), writes=()):
        return self.op("act", fn, reads, writes)

    def dve(self, fn, reads=(), writes=()):
        return self.op("dve", fn, reads, writes)

    def pool(self, fn, reads=(), writes=()):
        return self.op("pool", fn, reads, writes)

    def dma(self, q, stream, fn, reads=(), writes=(), like=None):
        return self.op(q, fn, reads, writes, stream=stream, like=like)

    def emit(self, final_wait_streams=()):
        nc = self.nc
        counters = {}
        semkeys = []
        for o in self.ops:
            key = ("dma", o.stream) if o.stream is not None else ("eng", o.eng)
            if o.needs_inc or o.stream is not None:
                step = 16 if o.stream is not None else 1
                counters[key] = counters.get(key, 0) + step
                o.sig = (key, counters[key])
                if key not in semkeys:
                    semkeys.append(key)
        with ExitStack() as es:
            sems = {}
            for key in semkeys:
                sems[key] = es.enter_context(nc.semaphore("s_%s_%s" % key))
            block = es.enter_context(nc.Block())
            per_eng = {e: [o for o in self.ops if o.eng == e] for e in ENGS}

            def run(engname, engobj):
                waited = {}
                for o in per_eng[engname]:
                    for d in o.deps:
                        key, val = d.sig
                        if waited.get(key, 0) >= val:
                            continue
                        engobj.wait_ge(sems[key], val)
                        waited[key] = val
                    ins = o.fn(engobj)
                    if o.sig is not None:
                        ins.then_inc(sems[o.sig[0]], 16 if o.stream is not None else 1)
                if engname == "sp":
                    for st in final_wait_streams:
                        key = ("dma", st)
                        if key in counters:
                            engobj.wait_ge(sems[key], counters[key])

            @block.tensor
            def _(e):
                run("pe", e)

            @block.scalar
            def _(e):
                run("act", e)

            @block.vector
            def _(e):
                run("dve", e)

            @block.gpsimd
            def _(e):
                run("pool", e)

            @block.sync
            def _(e):
                run("sp", e)
        self.nsems = len(semkeys)


class Arena:
    def __init__(self, base_ap, nelem):
        self.base = base_ap
        self.n = nelem
        self.off = 0

    def reset(self):
        self.off = 0

    def alloc(self, free_shape, dtype):
        n = int(np.prod(free_shape))
        nb = n * (2 if dtype == F32 else 1)
        nb = (nb + 15) // 16 * 16
        assert self.off + nb <= self.n, ("arena overflow", self.off, nb, self.n)
        ap = self.base[:, self.off:self.off + nb]
        self.off += nb
        if dtype == F32:
            ap = ap.bitcast(F32)
        ap = ap[:, 0:n]
        if len(free_shape) == 2:
            ap = ap.rearrange("p (a b) -> p a b", a=free_shape[0])
        elif len(free_shape) == 3:
            ap = ap.rearrange("p (a b c) -> p a b c", a=free_shape[0], b=free_shape[1])
        elif len(free_shape) == 4:
            ap = ap.rearrange("p (a b c d) -> p a b c d", a=free_shape[0], b=free_shape[1], c=free_shape[2])
        return ap


NSLOT = 7
PF = 3
NXS = 12
REGB = 35200
WQ_EL = 8192


def build_program():
    nc = bass.Bass("TRN2", target_bir_lowering=False)

    def din(name, shape):
        return nc.dram_tensor(name, list(shape), F32, kind="ExternalInput").ap()

    x_d = din("x", [TOK, D])
    xh_d = din("xh", [128, D])
    mem_d = din("mem", [256, D])
    w_in_d = din("w_in", [56 * 128, D])
    w_out_d = din("w_out", [2048, D])
    w_q_d = din("w_q", [D, D])
    w_kv_d = din("w_kv", [D, 2048])
    w_xo_d = din("w_xo", [D, D])
    gx_d = din("gx", [128, 8])
    gmem_d = din("gmem", [128, 8])
    gfin_d = din("gfin", [128, D])
    gmixb_d = din("gmixb", [128, D])
    convw_d = din("convw", [128, 24])
    lng_d = din("lng", [128, 8])
    lnb_d = din("lnb", [128, 8])
    ws_d = din("ws", [128, 1024])
    bsb_d = din("bsb", [128, 1024])
    tril_d = din("tril", [128, 128])
    ident_d = din("ident", [128, 128])
    out_d = nc.dram_tensor("out", [TOK, D], F32, kind="ExternalOutput").ap()

    w_out_v = w_out_d.rearrange("(k p) n -> p k n", p=128)
    w_q_v = w_q_d.rearrange("(k p) n -> p k n", p=128)
    w_kv_v = w_kv_d.rearrange("(k p) n -> p k n", p=128)
    w_xo_v = w_xo_d.rearrange("(k p) n -> p k n", p=128)

    with ExitStack() as es:
        def sb(name, shape, dt):
            return es.enter_context(nc.sbuf_tensor("sb_" + name, list(shape), dt))

        regA = sb("regA", [128, 16400], BF16)
        mixT = sb("mixT", [128, 16, 2048], BF16)
        wout = sb("wout", [128, 16, 1024], BF16)
        kT = sb("kT", [128, 8, 256], BF16)
        Vt = sb("Vt", [128, 2, 1024], BF16)
        ident = sb("ident", [128, 128], BF16)
        onesb = sb("onesb", [128, 128], BF16)
        gx = sb("gx", [128, 8], F32)
        gmem = sb("gmem", [128, 8], F32)
        convw = sb("convw", [128, 24], F32)
        lng = sb("lng", [128, 8], F32)
        lnb = sb("lnb", [128, 8], F32)
        mhalf = sb("mhalf", [128, 4], F32)
        regB = sb("regB", [128, REGB], BF16)

        ps = [es.enter_context(nc.psum_tensor("ps%d" % b, [128, 512], F32)) for b in (0, 1, 2, 3)]
        psT4 = es.enter_context(nc.psum_tensor("psT4", [128, 1024], BF16))
        psT5 = es.enter_context(nc.psum_tensor("psT5", [128, 1024], BF16))
        ps6 = es.enter_context(nc.psum_tensor("ps6", [128, 512], F32))
        ps7 = es.enter_context(nc.psum_tensor("ps7", [128, 512], F32))
        psf = {0: ps[0], 1: ps[1], 2: ps[2], 3: ps[3], 6: ps6, 7: ps7}
        psT = {4: psT4, 5: psT5}

        hT = regA[:, 0:16400].rearrange("p (k n) -> p k n", k=8)
        wxo = regA[:, 0:8192].rearrange("p (k n) -> p k n", k=8)
        HT_KEYS = [("hT", i) for i in range(4)] + [("hT", "h")]
        wkv = wout[:, :, :].rearrange("p j n -> p (j n)").rearrange("p (k n) -> p k n", k=8)
        wq = regB[:, 0:WQ_EL].rearrange("p (k n) -> p k n", k=8)

        def xst(s):
            return mixT[:, 4 + s, :].bitcast(F32)

        def xst_keys(s):
            return [("mix", 4 + s, tg) for tg in range(4)]

        P = Prog(nc)
        ar = Arena(regB, REGB)

        ybuf = ar.alloc([2, 2050], BF16)
        gcs = ar.alloc([2, 512], BF16)
        gch = ar.alloc([16], BF16)
        ccv = ar.alloc([2, 512], BF16)
        gbc = ar.alloc([2, 512], BF16)
        gbs = ar.alloc([2, 512], BF16)
        mT = ar.alloc([8, 256], BF16)
        assert ar.off >= WQ_EL, ar.off
        EARLY_KEYS = ([("y", jy, t) for jy in range(2) for t in (0, 1, 2, 3, "h")] + [("gcs", p) for p in range(2)]
                      + ["gch"] + [("ccv", p) for p in range(2)] + [("gbc", p) for p in range(2)] + [("gbs", p) for p in range(2)]
                      + [("mT", 0), ("mT", 1)])
        X1_OFF = ar.off
        hn = ar.alloc([4, 1024], BF16)
        sqj = ar.alloc([2, 1024], BF16)
        gmixb = ar.alloc([1024], F32)
        wsb = ar.alloc([8, 128], BF16)
        trilb = ar.alloc([128], BF16)
        LATE_END = ar.off
        LATE_KEYS = [("hn", i) for i in range(4)] + [("sqj", q) for q in range(2)] + ["gmixb", "wsb", "trilb"]
        wblk = ar.alloc([NSLOT, 8, 128], BF16)
        gvT = ar.alloc([2, 512], BF16)
        nmr = ar.alloc([2, 4], F32)
        Abf = ar.alloc([2, 4, 128], BF16)
        spt = ar.alloc([2, 512], BF16)
        gub = ar.alloc([2, 512], BF16)
        stt = ar.alloc([2, 4, 6], F32)
        mvv = ar.alloc([2, 4, 2], F32)
        rs4 = ar.alloc([2, 4], F32)
        tm4 = ar.alloc([2, 4], F32)
        ss = ar.alloc([NXS + 3], F32)
        rs = ar.alloc([NXS + 3], F32)
        tms = ar.alloc([NXS + 3], F32)
        Rb = ar.alloc([8, 128], F32)
        WcT = ar.alloc([8, 128], BF16)
        NX1 = 4
        ar.off = X1_OFF
        x1 = ar.alloc([NX1, 1024], F32)
        assert ar.off <= LATE_END, (ar.off, LATE_END)

        cnt = {"c": 0}

        def cdma(q, dst, src, key):
            cnt["c"] += 1
            P.dma(q, "c%d" % cnt["c"], lambda e: e.dma_start(out=dst, in_=src), writes=[key])

        cdma("pool", ident[:], ident_d, "ident")
        P.pool(lambda e: e.memset(mhalf[:], -0.5), writes=["mhalf"])
        P.pool(lambda e: e.memset(onesb[:], 1.0), writes=["onesb"])

        def late_consts():
            cdma("sp", gmem[:], gmem_d, "gmem")
            cdma("sp", gx[:], gx_d, "gx")
            cdma("sp", convw[:], convw_d, "convw")
            cdma("sp", lng[:], lng_d, "lng")
            cdma("sp", lnb[:], lnb_d, "lnb")
            cdma("sp", Rb.rearrange("p h s -> p (h s)"), bsb_d, "Rb")

        tcount = {"n": 0}
        preloaded = set()
        for tt_ in (1, 3, 5, 7, 9, 11):
            preloaded.add(tt_)
            P.dma("act", "xs%d" % (tt_ % NXS),
                  lambda e, tt_=tt_: e.dma_start(out=xst(tt_ % NXS), in_=x_d[tt_ * 128:(tt_ + 1) * 128, :]),
                  writes=xst_keys(tt_ % NXS))

        def norm_T_tile(src_ap, gvec, gkey, dstT, dst_lo, src_lo, n, dst_keys, gfull=None):
            i = tcount["n"]
            tcount["n"] += 1
            if i < 16:
                s = i % NXS
                si = s
            else:
                s = -(i - 15)
                si = NXS + (i - 16)
            h3 = i % 4
            q2 = i % 2
            tb = 4 + (i % 4)
            pT_ = psT[tb][:, :] if tb in psT else psf[tb][:, :].bitcast(BF16)
            xs = xst(s)
            if not (i < 16 and i in preloaded):
                P.dma("sp", "xs%d" % si, lambda e: e.dma_start(out=xs, in_=src_ap), writes=xst_keys(s))
            P.act(lambda e: e.activation(out=sqj[:, q2, :], in_=xs, func=AF.Square, accum_out=ss[:, si:si + 1]),
                  reads=xst_keys(s), writes=[("sqj", q2), ("ss", si)])
            P.pool(lambda e: e.tensor_scalar(out=tms[:, si:si + 1], in0=ss[:, si:si + 1], scalar1=1.0 / D, scalar2=EPS,
                                             op0=ALU.mult, op1=ALU.add),
                   reads=[("ss", si)], writes=[("tms", si)])
            P.pool(lambda e: e.tensor_tensor(out=rs[:, si:si + 1], in0=tms[:, si:si + 1], in1=mhalf[:, 0:1], op=ALU.pow),
                   reads=[("tms", si), "mhalf"], writes=[("rs", si)])
            pv = pT_.rearrange("p (k t) -> p k t", k=8)

            def front2():
                if gfull is None:
                    P.dve(lambda e: e.tensor_scalar(out=hn[:, h3, :], in0=xs, scalar1=rs[:, si:si + 1], scalar2=None,
                                                    op0=ALU.mult),
                          reads=xst_keys(s) + [("rs", si)], writes=[("hn", h3)])
                else:
                    P.dve(lambda e: e.scalar_tensor_tensor(out=hn[:, h3, :], in0=xs, scalar=rs[:, si:si + 1], in1=gfull,
                                                           op0=ALU.mult, op1=ALU.mult),
                          reads=xst_keys(s) + [("rs", si), gkey], writes=[("hn", h3)])

            def mid():
                for kc in range(8):
                    P.pe(lambda e, kc=kc: e.transpose(out=pT_[:, kc * 128:(kc + 1) * 128],
                                                      in_=hn[:, h3, kc * 128:(kc + 1) * 128], identity=ident[:]),
                         reads=[("hn", h3), "ident"], writes=[("ps", tb)])

            def back():
                if gfull is None:
                    P.dve(lambda e: e.tensor_tensor(out=dstT[:, :, dst_lo:dst_lo + n], in0=pv[:, :, src_lo:src_lo + n],
                                                    in1=gvec[:, :].unsqueeze(2).broadcast_to([128, 8, n]), op=ALU.mult),
                          reads=[("ps", tb), gkey], writes=dst_keys)
                elif i % 4 in (0, 3):
                    P.act(lambda e: e.activation(out=dstT[:, :, dst_lo:dst_lo + n], in_=pv[:, :, src_lo:src_lo + n],
                                                 func=AF.Copy),
                          reads=[("ps", tb)], writes=dst_keys)
                else:
                    P.dve(lambda e: e.tensor_copy(out=dstT[:, :, dst_lo:dst_lo + n], in_=pv[:, :, src_lo:src_lo + n]),
                          reads=[("ps", tb)], writes=dst_keys)
            return front2, mid, back

        order = []
        order.append(("conv0", 0, [1024, 2048, 3072, 0]))
        for j in range(1, 8):
            order.append(("gcxa", j, [1024 + j * 128, 2048 + j * 128]))
            order.append(("zagb", j, [3072 + j * 128, j * 128]))
        for j in range(8):
            order.append(("zb", j, [6144 + j * 128]))
        for j in range(8):
            order.append(("vu", j, [5120 + j * 128, 4096 + j * 128]))
        blocks = [c for (_, _, cols) in order for c in cols]
        loaded = {"n": 0}

        def load_block_upto(n):
            while loaded["n"] < min(n, len(blocks)):
                bi = loaded["n"]
                loaded["n"] += 1
                slot = bi % NSLOT
                col = blocks[bi]
                P.dma("pool", "wb%d" % slot,
                      lambda e, slot=slot, col=col: e.dma_start(out=wblk[:, slot, :, :],
                                                                in_=w_in_d[col:col + 128, :].rearrange("p (k n) -> p k n", k=8)),
                      writes=[("wblk", slot)])

        mainb = {"n": 0}

        def main_mm(bi, tg):
            slot = bi % NSLOT
            b = mainb["n"] % 3
            mainb["n"] += 1
            for k in range(8):
                P.pe(lambda e, k=k, b=b: e.matmul(out=psf[b][:, :], lhsT=wblk[:, slot, k, :],
                                                  rhs=hT[:, k, 2 + tg * 512:2 + (tg + 1) * 512],
                                                  start=(k == 0), stop=(k == 7)),
                     reads=[("wblk", slot), ("hT", tg)], writes=[("ps", b)])
            return b

        def halo_mm(bi, off):
            slot = bi % NSLOT
            for k in range(8):
                P.pe(lambda e, k=k: e.matmul(out=psf[3][:, off:off + 2], lhsT=wblk[:, slot, k, :],
                                             rhs=hT[:, k, 0:2], start=(k == 0), stop=(k == 7)),
                     reads=[("wblk", slot), ("hT", "h")], writes=[("ps", 3)])

        vcount = {"n": 0}

        def v_step(bi, j, tg):
            par = vcount["n"] % 2
            vcount["n"] += 1
            tb = 4 + par
            mb = 6 + par
            b = main_mm(bi, tg)
            P.act(lambda e: e.activation(out=gvT[:, par, :], in_=psf[b][:, :], func=AF.Gelu_apprx_tanh),
                  reads=[("ps", b)], writes=[("gvT", par)])
            yield
            for ck in range(4):
                P.pe(lambda e, ck=ck: e.transpose(out=psT[tb][:, ck * 128:(ck + 1) * 128],
                                                  in_=gvT[:, par, ck * 128:(ck + 1) * 128], identity=ident[:]),
                     reads=[("gvT", par), "ident"], writes=[("ps", tb)])
            for ck in range(4):
                P.dve(lambda e, ck=ck: e.bn_stats(out=stt[:, par, ck, :], in_=psT[tb][:, ck * 128:(ck + 1) * 128]),
                      reads=[("ps", tb)], writes=[("stt", par, ck)])
            for ck in range(4):
                P.dve(lambda e, ck=ck: e.bn_aggr(out=mvv[:, par, ck, :], in_=stt[:, par, ck, :]),
                      reads=[("stt", par, ck)], writes=[("mvv", par, ck)])
            mvk = [("mvv", par, ck) for ck in range(4)]
            P.pool(lambda e: e.tensor_scalar(out=tm4[:, par, :], in0=mvv[:, par, :, 1], scalar1=EPS, scalar2=None,
                                             op0=ALU.add),
                   reads=mvk, writes=[("tm4", par)])
            P.pool(lambda e: e.tensor_tensor(out=rs4[:, par, :], in0=tm4[:, par, :], in1=mhalf[:, :], op=ALU.pow),
                   reads=[("tm4", par), "mhalf"], writes=[("rs4", par)])
            P.pool(lambda e: e.tensor_tensor(out=tm4[:, par, :], in0=mvv[:, par, :, 0], in1=rs4[:, par, :], op=ALU.mult),
                   reads=mvk + [("rs4", par), ("tm4", par)], writes=[("tm4", par)])
            P.pool(lambda e: e.tensor_scalar(out=nmr[:, par, :], in0=tm4[:, par, :], scalar1=-1.0, scalar2=None,
                                             op0=ALU.mult),
                   reads=[("tm4", par)], writes=[("nmr", par)])
            yield
            for ck in range(4):
                P.act(lambda e, ck=ck: e.activation(out=Abf[:, par, ck, :], in_=psT[tb][:, ck * 128:(ck + 1) * 128],
                                                    func=AF.Identity, scale=rs4[:, par, ck:ck + 1],
                                                    bias=nmr[:, par, ck:ck + 1]),
                      reads=[("ps", tb), ("rs4", par), ("nmr", par)], writes=[("Abf", par, ck)])
            yield
            for ck in range(4):
                P.pe(lambda e, ck=ck: e.matmul(out=psf[mb][:, ck * 128:(ck + 1) * 128], lhsT=Abf[:, par, ck, :],
                                               rhs=WcT[:, j, :], start=True, stop=True),
                     reads=[("Abf", par, ck), "WcT"], writes=[("ps", mb)])
            P.dve(lambda e: e.scalar_tensor_tensor(
                out=spt[:, par, :].rearrange("p (c t) -> p c t", c=4),
                in0=psf[mb][:, :].rearrange("p (c t) -> p c t", c=4), scalar=lng[:, j:j + 1],
                in1=Rb[:, j, :].unsqueeze(1).broadcast_to([128, 4, 128]), op0=ALU.mult, op1=ALU.add),
                reads=[("ps", mb), "lng", "Rb"], writes=[("spt", par)])
            P.dve(lambda e: e.tensor_tensor(out=mixT[:, 8 + j, tg * 512:(tg + 1) * 512],
                                            in0=mixT[:, 8 + j, tg * 512:(tg + 1) * 512], in1=spt[:, par, :],
                                            op=ALU.mult),
                  reads=[("spt", par), ("mix", 8 + j, tg)], writes=[("mix", 8 + j, tg)])

        def step_gcxa(bgc, bxa, j, tg, defer_halo=False):
            b = main_mm(bgc, tg)
            p2 = tg % 2
            jy = j % 2
            P.act(lambda e: e.activation(out=gcs[:, p2, :], in_=psf[b][:, :], func=AF.Copy),
                  reads=[("ps", b)], writes=[("gcs", p2)])

            def halo_part():
                halo_mm(bgc, 0)
                halo_mm(bxa, 2)
                P.act(lambda e: e.activation(out=gch[:, 0:4], in_=psf[3][:, 0:4], func=AF.Copy),
                      reads=[("ps", 3)], writes=["gch"])
                P.dve(lambda e: e.tensor_tensor(out=ybuf[:, jy, 0:2], in0=gch[:, 0:2], in1=gch[:, 2:4], op=ALU.mult),
                      reads=["gch"], writes=[("y", jy, "h")])

            b2 = main_mm(bxa, tg)
            P.dve(lambda e: e.tensor_tensor(out=ybuf[:, jy, 2 + tg * 512:2 + (tg + 1) * 512], in0=psf[b2][:, :],
                                            in1=gcs[:, p2, :], op=ALU.mult),
                  reads=[("ps", b2), ("gcs", p2)], writes=[("y", jy, tg)])
            if tg == 0:
                if defer_halo:
                    return halo_part
                halo_part()
            return None

        def step_za(bi, j, tg):
            b = main_mm(bi, tg)
            P.act(lambda e: e.activation(out=mixT[:, j, tg * 512:(tg + 1) * 512], in_=psf[b][:, :], func=AF.Silu),
                  reads=[("ps", b)], writes=[("mix", j, tg)])

        def step_gb(bi, j, tg):
            b = main_mm(bi, tg)
            jy = j % 2
            p2 = tg % 2
            P.act(lambda e: e.activation(out=gbs[:, p2, :], in_=psf[b][:, :], func=AF.Copy),
                  reads=[("ps", b)], writes=[("gbs", p2)])
            yk = [("y", jy, tg), ("y", jy, tg - 1 if tg > 0 else "h")]
            P.dve(lambda e: e.tensor_scalar(out=ccv[:, p2, :], in0=ybuf[:, jy, tg * 512:tg * 512 + 512],
                                            scalar1=convw[:, j * 3:j * 3 + 1], scalar2=None, op0=ALU.mult),
                  reads=yk + ["convw"], writes=[("ccv", p2)])
            for kk in (1, 2):
                P.dve(lambda e, kk=kk: e.scalar_tensor_tensor(
                    out=ccv[:, p2, :], in0=ybuf[:, jy, tg * 512 + kk:tg * 512 + kk + 512],
                    scalar=convw[:, j * 3 + kk:j * 3 + kk + 1], in1=ccv[:, p2, :], op0=ALU.mult, op1=ALU.add),
                    reads=yk + ["convw", ("ccv", p2)], writes=[("ccv", p2)])
            P.dve(lambda e: e.tensor_tensor(out=gbc[:, p2, :], in0=gbs[:, p2, :], in1=ccv[:, p2, :], op=ALU.mult),
                  reads=[("gbs", p2), ("ccv", p2)], writes=[("gbc", p2)])
            P.pool(lambda e: e.tensor_tensor(out=mixT[:, j, tg * 512:(tg + 1) * 512],
                                             in0=mixT[:, j, tg * 512:(tg + 1) * 512], in1=gbc[:, p2, :], op=ALU.mult),
                   reads=[("gbc", p2), ("mix", j, tg)], writes=[("mix", j, tg)])

        def step_zb(bi, j, tg):
            b = main_mm(bi, tg)
            P.act(lambda e: e.activation(out=mixT[:, 8 + j, tg * 512:(tg + 1) * 512], in_=psf[b][:, :], func=AF.Silu),
                  reads=[("ps", b)], writes=[("mix", 8 + j, tg)])

        def step_u(bi, j, tg):
            b = main_mm(bi, tg)
            p2 = tg % 2
            P.act(lambda e: e.activation(out=gub[:, p2, :], in_=psf[b][:, :], func=AF.Gelu_apprx_tanh),
                  reads=[("ps", b)], writes=[("gub", p2)])
            P.dve(lambda e: e.tensor_tensor(out=mixT[:, 8 + j, tg * 512:(tg + 1) * 512],
                                            in0=mixT[:, 8 + j, tg * 512:(tg + 1) * 512], in1=gub[:, p2, :], op=ALU.mult),
                  reads=[("gub", p2), ("mix", 8 + j, tg)], writes=[("mix", 8 + j, tg)])

        active = []

        def advance():
            for g_ in list(active):
                try:
                    next(g_)
                except StopIteration:
                    active.remove(g_)

        def kv_phase():
            for c in range(8):
                b = c % 3
                for k in range(8):
                    P.pe(lambda e, c=c, k=k, b=b: e.matmul(out=psf[b][:, 0:256], lhsT=wkv[:, k, c * 128:(c + 1) * 128],
                                                           rhs=mT[:, k, :], start=(k == 0), stop=(k == 7)),
                         reads=[("wkv", c // 4), ("mT", 0), ("mT", 1)], writes=[("ps", b)])
                P.act(lambda e, c=c, b=b: e.activation(out=kT[:, c, :], in_=psf[b][:, 0:256], func=AF.Copy),
                      reads=[("ps", b)], writes=[("kT", c)])
            for mt in range(2):
                for cg in range(2):
                    b = (mt * 2 + cg) % 3
                    for k in range(8):
                        P.pe(lambda e, mt=mt, cg=cg, k=k, b=b: e.matmul(
                            out=psf[b][:, :], lhsT=mT[:, k, mt * 128:(mt + 1) * 128],
                            rhs=wkv[:, k, 1024 + cg * 512:1024 + (cg + 1) * 512], start=(k == 0), stop=(k == 7)),
                            reads=[("wkv", 2 + cg), ("mT", mt)], writes=[("ps", b)])
                    P.act(lambda e, mt=mt, cg=cg, b=b: e.activation(out=Vt[:, mt, cg * 512:(cg + 1) * 512],
                                                                    in_=psf[b][:, :], func=AF.Copy),
                          reads=[("ps", b)], writes=[("V", mt)])

        def ws_setup(part):
            if part == 0:
                P.dve(lambda e: e.tensor_tensor(out=wsb, in0=wsb, in1=trilb.unsqueeze(1).broadcast_to([128, 8, 128]),
                                                op=ALU.mult),
                      reads=["wsb", "trilb"], writes=["wsb"])
            elif part == 1:
                for h in range(8):
                    P.pe(lambda e, h=h: e.transpose(out=psT[4][:, h * 128:(h + 1) * 128], in_=wsb[:, h, :],
                                                    identity=ident[:]),
                         reads=["wsb", "ident"], writes=[("ps", 4)])
                P.act(lambda e: e.activation(out=WcT.rearrange("p h t -> p (h t)"), in_=psT[4][:, :], func=AF.Copy),
                      reads=[("ps", 4)], writes=["WcT"])
            else:
                for h in range(8):
                    b = 6 + h // 4
                    P.pe(lambda e, h=h, b=b: e.matmul(out=psf[b][:, (h % 4) * 128:(h % 4 + 1) * 128], lhsT=onesb[:],
                                                     rhs=WcT[:, h, :], start=True, stop=True),
                         reads=["onesb", "WcT"], writes=[("ps", b)])
                for h in range(8):
                    b = 6 + h // 4
                    P.dve(lambda e, h=h, b=b: e.scalar_tensor_tensor(
                        out=Rb[:, h, :], in0=psf[b][:, (h % 4) * 128:(h % 4 + 1) * 128], scalar=lnb[:, h:h + 1],
                        in1=Rb[:, h, :], op0=ALU.mult, op1=ALU.add),
                        reads=[("ps", b), "lnb", "Rb"], writes=["Rb"])

        bi = 0
        gi = 0
        kind, j, cols = order[gi]

        def wkv_dmas(which=(0, 1, 2, 3)):
            for cgi in which:
                P.dma("pool", "wkv%d" % cgi,
                      lambda e, cgi=cgi: e.dma_start(out=wkv[:, :, cgi * 512:(cgi + 1) * 512],
                                                     in_=w_kv_v[:, :, cgi * 512:(cgi + 1) * 512]),
                      writes=[("wkv", cgi)])

        load_block_upto(5)
        P.act(lambda e: e.activation(out=sqj[:, 0, 0:1], in_=mhalf[:, 0:1], func=AF.Square), reads=["mhalf"], writes=[("sqj", 0)])

        def xfront(tt):
            return norm_T_tile(x_d[tt * 128:(tt + 1) * 128, :], None, "gmixb", hT, 2 + tt * 128, 0, 128,
                               [("hT", tt // 4)], gfull=gmixb)

        def xfront(tt):
            tcount["n"] = tt
            return norm_T_tile(x_d[tt * 128:(tt + 1) * 128, :], None, "gmixb", hT, 2 + tt * 128, 0, 128,
                               [("hT", tt // 4)], gfull=gmixb)

        def mb_group(g):
            g[0][1]()
            g[1][1]()
            g[0][2]()
            g[2][1]()
            g[1][2]()
            g[3][1]()
            g[2][2]()
            g[3][2]()

        grp = {0: [xfront(0)]}
        cdma("sp", gmixb, gmixb_d, "gmixb")
        grp[0] += [xfront(tt) for tt in range(1, 4)]
        for t_ in grp[0]:
            t_[0]()
        tcount["n"] = 16
        fh, mh, bh = norm_T_tile(xh_d, None, "gmixb", hT, 0, 126, 2, [("hT", "h")], gfull=gmixb)
        mb_group(grp[0])
        fh()
        mh()
        bh()
        grp[1] = [xfront(tt) for tt in range(4, 8)]
        late_consts()
        for t_ in grp[1]:
            t_[0]()
        for tg in range(4):
            g_ = grp.get(tg + 1) if tg + 1 < 4 else None
            early_f = (tg >= 1 and tg + 2 < 4)
            if early_f:
                grp[tg + 2] = [xfront(tt) for tt in range(4 * (tg + 2), 4 * (tg + 2) + 4)]
            hp = step_gcxa(bi, bi + 1, 0, tg, defer_halo=True)
            if g_ is not None:
                g_[0][1]()
                g_[1][1]()
                g_[0][2]()
            step_za(bi + 2, 0, tg)
            if hp is not None:
                hp()
            if g_ is not None:
                g_[2][1]()
                g_[1][2]()
                g_[3][1]()
                g_[2][2]()
                g_[3][2]()
            if early_f:
                for t_ in grp[tg + 2]:
                    t_[0]()
            step_gb(bi + 3, 0, tg)
            if tg + 2 < 4 and not early_f:
                grp[tg + 2] = [xfront(tt) for tt in range(4 * (tg + 2), 4 * (tg + 2) + 4)]
                for t_ in grp[tg + 2]:
                    t_[0]()
            if tg == 1:
                load_block_upto(7)
            if tg == 2:
                tcount["n"] = 17
                mem_t = [norm_T_tile(mem_d[mt * 128:(mt + 1) * 128, :], gmem, "gmem", mT, mt * 128, 0, 128,
                                     [("mT", mt)]) for mt in range(2)]
                for t_ in mem_t:
                    t_[0]()
        bi += 4
        gi += 1
        wout_at = {9: 0, 11: 1, 12: 2, 13: 3}
        wq_at = {24: 0, 26: 1}
        while gi < len(order):
            kind, j, cols = order[gi]
            load_block_upto(bi + len(cols) + PF)
            if 1 <= gi <= 4:
                wkv_dmas((gi - 1,))
            if gi == 1:
                mem_t[0][1]()
                mem_t[1][1]()
                mem_t[0][2]()
                mem_t[1][2]()
            if gi == 2:
                cdma("pool", wsb.rearrange("p h s -> p (h s)"), ws_d, "wsb")
                cdma("pool", trilb, tril_d, "trilb")
            if gi == 5:
                kv_phase()
            if gi in (6, 8, 10):
                ws_setup((gi - 6) // 2)
            if gi in wout_at:
                g = wout_at[gi]
                P.dma("pool", "wout%d" % g,
                      lambda e, g=g: e.dma_start(out=wout[:, 4 * g:4 * g + 4, :], in_=w_out_v[:, 4 * g:4 * g + 4, :]),
                      writes=[("wout", g)] + [("wkv", c) for c in range(4)])
            if gi == 17:
                o0 = P.dma("pool", "wq0", lambda e: e.dma_start(out=wq[:, 0:4, :], in_=w_q_v[:, 0:4, :]),
                           writes=EARLY_KEYS + [("wq", 0)])
                P.dma("pool", "wq1", lambda e: e.dma_start(out=wq[:, 4:8, :], in_=w_q_v[:, 4:8, :]),
                      writes=[("wq", 1)], like=o0)
            for tg in range(4):
                if kind == "gcxa":
                    step_gcxa(bi, bi + 1, j, tg)
                elif kind == "zagb":
                    step_za(bi, j, tg)
                    step_gb(bi + 1, j, tg)
                elif kind == "zb":
                    step_zb(bi, j, tg)
                elif kind == "vu":
                    gen = v_step(bi, j, tg)
                    next(gen)
                    step_u(bi + 1, j, tg)
                    advance()
                    active.append(gen)
            bi += len(cols)
            gi += 1
        NS2 = 3
        BIG = {0: (0, 1), 1: (2, 3), 2: (6, 7)}

        def s1_pe(tt, halves=(0, 1)):
            B = BIG[tt % NS2]
            r0 = tt * 128
            for cg in halves:
                for jc in range(16):
                    P.pe(lambda e, cg=cg, jc=jc: e.matmul(out=psf[B[cg]][:, :], lhsT=mixT[:, jc, r0:r0 + 128],
                                                          rhs=wout[:, jc, cg * 512:(cg + 1) * 512],
                                                          start=(jc == 0), stop=(jc == 15)),
                         reads=[("mix", jc, tt // 4), ("wout", jc // 4)], writes=[("ps", B[cg])])

        def x1_load(tt):
            xl = tt % NX1
            r0 = tt * 128
            extra = LATE_KEYS if tt < NX1 else []
            P.dma("sp", "x1s%d" % xl, lambda e: e.dma_start(out=x1[:, xl, :], in_=x_d[r0:r0 + 128, :]),
                  writes=[("x1", xl, 0), ("x1", xl, 1)] + extra)

        EARLY_S1 = 3
        for t_ in range(NX1):
            x1_load(t_)
        for t_ in range(2):
            s1_pe(t_)
            advance()
        while active:
            advance()

        P.barrier(exempt=("pe",))
        s1_pe(2)
        ar.off = LATE_END
        gfin = ar.alloc([1024], F32)
        tokb = ar.alloc([NS2, 1024], BF16)
        Tb = ar.alloc([NS2, 8, 128], BF16)
        qT = ar.alloc([NS2, 8, 128], BF16)
        Eb = ar.alloc([NS2, 2, 4, 128], BF16)
        sq2 = ar.alloc([1, 1024], BF16)
        ssa = ar.alloc([NS2], F32)
        ssb = ar.alloc([NS2], F32)
        tma = ar.alloc([NS2], F32)
        tmb = ar.alloc([NS2], F32)
        rsa = ar.alloc([NS2], F32)
        rsb = ar.alloc([NS2], F32)
        rden = ar.alloc([NS2, 4], F32)
        denv = psT5[:, 0:32].bitcast(F32)

        o0 = P.dma("pool", "wxo0", lambda e: e.dma_start(out=wxo[:, 0:4, :], in_=w_xo_v[:, 0:4, :]),
                   writes=HT_KEYS + [("wxo", 0)])
        P.dma("pool", "wxo1", lambda e: e.dma_start(out=wxo[:, 4:8, :], in_=w_xo_v[:, 4:8, :]),
              writes=[("wxo", 1)], like=o0)
        cdma("sp", gfin, gfin_d, "gfin")

        def pool_rstd(ssx, tmx, rsx, sl, tag):
            P.pool(lambda e: e.tensor_scalar(out=tmx[:, sl:sl + 1], in0=ssx[:, sl:sl + 1], scalar1=1.0 / D,
                                             scalar2=EPS, op0=ALU.mult, op1=ALU.add),
                   reads=[(tag + "ss", sl)], writes=[(tag + "tm", sl)])
            P.pool(lambda e: e.tensor_tensor(out=rsx[:, sl:sl + 1], in0=tmx[:, sl:sl + 1], in1=mhalf[:, 0:1],
                                             op=ALU.pow),
                   reads=[(tag + "tm", sl), "mhalf"], writes=[(tag + "rs", sl)])

        sqc = {"n": 0}
        def tile_gen(tt):
            sl = tt % NS2
            xl = tt % NX1
            B = BIG[sl]
            tbk = ("ps", B[0])
            pT = psf[B[0]][:, :].bitcast(BF16)
            dnk = ("ps", 5)
            r0 = tt * 128
            X1K = [("x1", xl, 0), ("x1", xl, 1)]
            for cg in range(2):
                P.dve(lambda e, cg=cg: e.tensor_tensor(out=x1[:, xl, cg * 512:(cg + 1) * 512], in0=psf[B[cg]][:, :],
                                                       in1=x1[:, xl, cg * 512:(cg + 1) * 512], op=ALU.add),
                      reads=[("ps", B[cg]), ("x1", xl, cg)], writes=[("x1", xl, cg)])
            q2 = 0
            sqc["n"] += 1
            P.act(lambda e: e.activation(out=sq2[:, q2, :], in_=x1[:, xl, :], func=AF.Square, accum_out=ssa[:, sl:sl + 1]),
                  reads=X1K, writes=[("sq2", q2), ("ass", sl)])
            pool_rstd(ssa, tma, rsa, sl, "a")
            P.dve(lambda e: e.tensor_scalar(out=tokb[:, sl, :], in0=x1[:, xl, :], scalar1=rsa[:, sl:sl + 1],
                                            scalar2=None, op0=ALU.mult),
                  reads=X1K + [("ars", sl)], writes=[("tokb", sl, 0), ("tokb", sl, 1)])
            yield
            for kc in range(8):
                P.pe(lambda e, kc=kc: e.transpose(out=pT[:, kc * 128:(kc + 1) * 128],
                                                  in_=tokb[:, sl, kc * 128:(kc + 1) * 128], identity=ident[:]),
                     reads=[("tokb", sl, kc // 4), "ident"], writes=[tbk])
            P.dve(lambda e: e.tensor_tensor(out=Tb[:, sl, :, :], in0=pT.rearrange("p (k t) -> p k t", k=8),
                                            in1=gx[:, :].unsqueeze(2).broadcast_to([128, 8, 128]), op=ALU.mult),
                  reads=[tbk, "gx"], writes=[("Tb", sl)])
            yield
            for c in range(8):
                bq = B[c // 4]
                for k in range(8):
                    P.pe(lambda e, c=c, k=k, bq=bq: e.matmul(out=psf[bq][:, (c % 4) * 128:(c % 4 + 1) * 128],
                                                             lhsT=wq[:, k, c * 128:(c + 1) * 128], rhs=Tb[:, sl, k, :],
                                                             start=(k == 0), stop=(k == 7)),
                         reads=[("wq", k // 4), ("Tb", sl)], writes=[("ps", bq)])
            for hb in range(2):
                P.act(lambda e, hb=hb: e.activation(out=qT[:, sl, 4 * hb:4 * hb + 4, :].rearrange("p c t -> p (c t)"),
                                                    in_=psf[B[hb]][:, :], func=AF.Copy),
                      reads=[("ps", B[hb])], writes=[("qT", sl, hb)])
            yield
            for mt in range(2):
                for h in range(4):
                    for kk in range(2):
                        P.pe(lambda e, mt=mt, h=h, kk=kk: e.matmul(
                            out=psf[B[mt]][:, h * 128:(h + 1) * 128], lhsT=kT[:, 2 * h + kk, mt * 128:(mt + 1) * 128],
                            rhs=qT[:, sl, 2 * h + kk, :], start=(kk == 0), stop=(kk == 1)),
                            reads=[("kT", 2 * h + kk), ("qT", sl, h // 2)], writes=[("ps", B[mt])])
                P.act(lambda e, mt=mt: e.activation(out=Eb[:, sl, mt, :, :].rearrange("p h t -> p (h t)"),
                                                    in_=psf[B[mt]][:, :], func=AF.Exp, scale=0.0625),
                      reads=[("ps", B[mt])], writes=[("E", sl, mt)])
            yield
            for h in range(4):
                for mt in range(2):
                    P.pe(lambda e, h=h, mt=mt: e.matmul(out=psf[B[h // 2]][:, (h % 2) * 256:(h % 2 + 1) * 256],
                                                        lhsT=Eb[:, sl, mt, h, :], rhs=Vt[:, mt, h * 256:(h + 1) * 256],
                                                        start=(mt == 0), stop=(mt == 1)),
                         reads=[("E", sl, mt), ("V", mt)], writes=[("ps", B[h // 2])])
                for mt in range(2):
                    P.pe(lambda e, h=h, mt=mt: e.matmul(out=denv[:, 4 * sl + h:4 * sl + h + 1], lhsT=Eb[:, sl, mt, h, :],
                                                        rhs=onesb[:, 0:1], start=(mt == 0), stop=(mt == 1)),
                         reads=[("E", sl, mt), "onesb"], writes=[dnk])
            P.dve(lambda e: e.reciprocal(out=rden[:, sl, :], in_=denv[:, 4 * sl:4 * sl + 4]),
                  reads=[dnk], writes=[("rden", sl)])
            for hb in range(2):
                P.dve(lambda e, hb=hb: e.tensor_tensor(
                    out=tokb[:, sl, hb * 512:(hb + 1) * 512].rearrange("p (h d) -> p h d", h=2),
                    in0=psf[B[hb]][:, :].rearrange("p (h d) -> p h d", h=2),
                    in1=rden[:, sl, 2 * hb:2 * hb + 2].unsqueeze(2).broadcast_to([128, 2, 256]), op=ALU.mult),
                    reads=[("ps", B[hb]), ("rden", sl)], writes=[("tokb", sl, hb)])
            yield
            for kc in range(8):
                P.pe(lambda e, kc=kc: e.transpose(out=pT[:, kc * 128:(kc + 1) * 128],
                                                  in_=tokb[:, sl, kc * 128:(kc + 1) * 128], identity=ident[:]),
                     reads=[("tokb", sl, kc // 4), "ident"], writes=[tbk])
            P.act(lambda e: e.activation(out=Tb[:, sl, :, :].rearrange("p k t -> p (k t)"), in_=pT,
                                         func=AF.Copy),
                  reads=[tbk], writes=[("Tb", sl)])
            yield
            for cg in range(2):
                for k in range(8):
                    P.pe(lambda e, cg=cg, k=k: e.matmul(out=psf[B[cg]][:, :], lhsT=Tb[:, sl, k, :],
                                                        rhs=wxo[:, k, cg * 512:(cg + 1) * 512],
                                                        start=(k == 0), stop=(k == 7)),
                         reads=[("Tb", sl), ("wxo", k // 4)], writes=[("ps", B[cg])])
                P.dve(lambda e, cg=cg: e.tensor_tensor(out=x1[:, xl, cg * 512:(cg + 1) * 512], in0=psf[B[cg]][:, :],
                                                       in1=x1[:, xl, cg * 512:(cg + 1) * 512], op=ALU.add),
                      reads=[("ps", B[cg]), ("x1", xl, cg)], writes=[("x1", xl, cg)])
            yield
            q2b = 0
            sqc["n"] += 1
            P.act(lambda e: e.activation(out=sq2[:, q2b, :], in_=x1[:, xl, :], func=AF.Square,
                                         accum_out=ssb[:, sl:sl + 1]),
                  reads=X1K, writes=[("sq2", q2b), ("bss", sl)])
            pool_rstd(ssb, tmb, rsb, sl, "b")
            P.dve(lambda e: e.scalar_tensor_tensor(out=x1[:, xl, :], in0=x1[:, xl, :], scalar=rsb[:, sl:sl + 1],
                                                   in1=gfin, op0=ALU.mult, op1=ALU.mult),
                  reads=X1K + [("brs", sl), "gfin"], writes=X1K)
            P.dma("sp", "st%d" % xl, lambda e: e.dma_start(out=out_d[r0:r0 + 128, :], in_=x1[:, xl, :]), reads=X1K)

        gens = {}

        def run_stage(t, k):
            if not (0 <= t < NT):
                return
            if k in (10, 11):
                if t >= EARLY_S1:
                    s1_pe(t, halves=(k - 10,))
                return
            if k == 1:
                gens[t] = tile_gen(t)
            try:
                next(gens[t])
            except StopIteration:
                assert k == 8, (t, k)

        for n in range(NT):
            if NX1 <= n + 1 < NT:
                x1_load(n + 1)
            for (t, k) in ((n - 2, 4), (n - 1, 2), (n, 10), (n - 2, 5), (n, 11), (n, 1), (n - 2, 6), (n - 1, 3),
                           (n - 2, 7), (n - 2, 8)):
                if n == NT - 1 and k == 8:
                    continue
                run_stage(t, k)
        A_, B_ = NT - 2, NT - 1
        for (t, k) in ((B_, 2), (A_ - 1, 8), (A_, 4), (B_, 3), (A_, 5), (B_, 4), (A_, 6), (B_, 5), (A_, 7), (B_, 6), (A_, 8),
                       (B_, 7), (B_, 8)):
            run_stage(t, k)

        P.emit(final_wait_streams=["st0", "st1", "st2", "st3"])
    return nc


_CACHE = {}


def _prep_inputs(inp):
    f = lambda a: np.ascontiguousarray(np.asarray(a, dtype=np.float32))
    x = f(inp["x"])
    mem = f(inp["mem"])
    vec8 = lambda v: f(np.asarray(v, dtype=np.float32).reshape(8, 128).T)
    common = {
        "w_in": f(np.asarray(inp["w_in"][0], dtype=np.float32).reshape(8, 128, 56, 128).transpose(2, 1, 0, 3)
                  .reshape(56 * 128, D)),
        "w_out": f(inp["w_out"][0]),
        "w_q": f(inp["w_q"][0]),
        "w_kv": f(inp["w_kv"][0]),
        "w_xo": f(inp["w_xo"][0]),
        "gx": vec8(inp["norm_x_g"][0]),
        "gmem": vec8(inp["norm_mem_g"][0]),
        "gfin": f(np.broadcast_to(np.asarray(inp["norm_final_g"], dtype=np.float32)[None, :], (128, D))),
        "gmixb": f(np.broadcast_to(np.asarray(inp["norm_mix_g"][0], dtype=np.float32)[None, :], (128, D))),
        "convw": f(np.asarray(inp["conv_w"][0], dtype=np.float32).reshape(3, 8, 128).transpose(2, 1, 0).reshape(128, 24)),
        "lng": vec8(inp["gm_ln_g"][0]),
        "lnb": vec8(inp["gm_ln_b"][0]),
        "ws": f(np.asarray(inp["gm_ws"][0], dtype=np.float32).transpose(1, 0, 2).reshape(128, 1024)),
        "bsb": f(np.broadcast_to(np.asarray(inp["gm_bs"][0], dtype=np.float32).reshape(1, 1024), (128, 1024))),
        "tril": f(np.tril(np.ones((128, 128), dtype=np.float32))),
        "ident": f(np.eye(128, dtype=np.float32)),
    }
    in_maps = []
    for c in range(NCORES):
        b, half = divmod(c, 2)
        xs = x[b, half * TOK:(half + 1) * TOK]
        xh = np.zeros((128, D), dtype=np.float32)
        if half == 1:
            xh[126:128] = x[b, TOK - 2:TOK]
        m = dict(common)
        m["x"] = f(xs)
        m["xh"] = xh
        m["mem"] = f(mem[b])
        in_maps.append(m)
    return in_maps


def kernel(**inputs):
    in_maps = _prep_inputs(inputs)
    if "nc" not in _CACHE:
        _CACHE["nc"] = build_program()
    nc = _CACHE["nc"]
    res = run_bass_kernel_spmd(nc, in_maps, core_ids=list(range(NCORES)))
    out = np.empty((4, 4096, D), dtype=np.float32)
    for c in range(NCORES):
        b, half = divmod(c, 2)
        out[b, half * TOK:(half + 1) * TOK] = np.asarray(res.results[c]["out"], dtype=np.float32)
    return out
```
